# Optimizing a Trainium2 kernel written in Bass

```python
import math
import jax, jax.numpy as jnp
from jax import lax
import numpy as np

D_MODEL = 2048
BATCH = 4
SEQ = 4096
DEPTH = 4

N_AB_LAYERS = (DEPTH + 1) // 2
N_C_LAYERS = DEPTH // 2
MIX_WIDTH = D_MODEL
ATT_WIDTH = MIX_WIDTH // 2
SSM_WIDTH = MIX_WIDTH - ATT_WIDTH
IN_WIDTH = 3 * ATT_WIDTH + SSM_WIDTH
N_DIFF_HEADS = 8
DIFF_HEAD_DIM = ATT_WIDTH // (2 * N_DIFF_HEADS)
Q_BLOCK = 128
N_BUCKETS = 32
MAX_DISTANCE = 128
SSM_GROUP = 16
N_SSM_GROUPS = SSM_WIDTH // SSM_GROUP
SSM_STATE = 64
CONV_KERNEL = 31
FFN_HIDDEN = 5632
FFN_CONV = 3
EPS = 1e-6

kernel_name = 'hybrid_diffattn_s5_conformer_convffn'


def rmsnorm(x, g):
    xf = x.astype(jnp.float32)
    r = xf * lax.rsqrt(jnp.mean(xf * xf, axis=-1, keepdims=True) + EPS)
    return (r * g.astype(jnp.float32)).astype(x.dtype)


def layernorm(x, g, b):
    xf = x.astype(jnp.float32)
    mu = jnp.mean(xf, axis=-1, keepdims=True)
    var = jnp.mean(jnp.square(xf - mu), axis=-1, keepdims=True)
    r = (xf - mu) * lax.rsqrt(var + EPS)
    return (r * g.astype(jnp.float32) + b.astype(jnp.float32)).astype(x.dtype)


def causal_dwconv(x, w, b):
    k_width, ch = w.shape
    y = lax.conv_general_dilated(
        x, w[:, None, :].astype(x.dtype), window_strides=(1,),
        padding=[(k_width - 1, 0)], dimension_numbers=('NWC', 'WIO', 'NWC'),
        feature_group_count=ch)
    return y + b.astype(x.dtype)


def t5_bucket(rel):
    n = jnp.maximum(rel, 0)
    max_exact = N_BUCKETS // 2
    nf = jnp.maximum(n, 1).astype(jnp.float32)
    large = max_exact + (jnp.log(nf / max_exact) / math.log(MAX_DISTANCE / max_exact)
                         * (N_BUCKETS - max_exact)).astype(jnp.int32)
    large = jnp.minimum(large, N_BUCKETS - 1)
    return jnp.where(n < max_exact, n, large)


def diff_attention(q, k, v, rel_bias, lam):
    bsz, slen, nh, _, hd = q.shape
    nb = slen // Q_BLOCK
    scale = hd ** -0.5
    k1 = k[:, :, :, 0]
    k2 = k[:, :, :, 1]
    vf = v.astype(jnp.float32)
    qb = q.reshape(bsz, nb, Q_BLOCK, nh, 2, hd).transpose(1, 0, 2, 3, 4, 5)
    k_pos = jnp.arange(slen)

    def block(args):
        q_blk, blk = args
        q_pos = blk * Q_BLOCK + jnp.arange(Q_BLOCK)
        rel = q_pos[:, None] - k_pos[None, :]
        bias = rel_bias[t5_bucket(rel)].astype(jnp.float32).transpose(2, 0, 1)
        causal = rel >= 0

        def probs(qh, kh):
            logits = jnp.einsum('bqhd,bkhd->bhqk', qh, kh).astype(jnp.float32) * scale + bias
            logits = jnp.where(causal, logits, -jnp.inf)
            return jax.nn.softmax(logits, axis=-1)

        p = probs(q_blk[:, :, :, 0], k1) - lam * probs(q_blk[:, :, :, 1], k2)
        return jnp.einsum('bhqk,bkhv->bqhv', p, vf)

    out = lax.map(block, (qb, jnp.arange(nb)))
    return out.transpose(1, 0, 2, 3, 4).reshape(bsz, slen, nh, 2 * hd)


def s5_mixer(u, lre, lim, log_step, bre, bim, cre, cim, dd, w_glu, b_glu):
    bsz, slen, _ = u.shape
    uf = u.astype(jnp.float32).reshape(bsz, slen, N_SSM_GROUPS, SSM_GROUP)
    lam = lax.complex(lre.astype(jnp.float32), lim.astype(jnp.float32))
    dt = jnp.exp(log_step.astype(jnp.float32))[:, None]
    lam_bar = jnp.exp(lam * dt)
    b_mat = lax.complex(bre.astype(jnp.float32), bim.astype(jnp.float32))
    b_bar = ((lam_bar - 1.0) / lam)[..., None] * b_mat
    bu = jnp.einsum('gph,bsgh->bsgp', b_bar, uf.astype(jnp.complex64))
    a = jnp.broadcast_to(lam_bar, (1, slen) + lam_bar.shape)

    def combine(e1, e2):
        a1, b1 = e1
        a2, b2 = e2
        return a1 * a2, a2 * b1 + b2

    _, states = lax.associative_scan(combine, (a, bu), axis=1)
    c_mat = lax.complex(cre.astype(jnp.float32), cim.astype(jnp.float32))
    y = jnp.einsum('ghp,bsgp->bsgh', c_mat, states).real + dd.astype(jnp.float32) * uf
    y = y.reshape(bsz, slen, SSM_WIDTH)
    g = jax.nn.gelu(y)
    out = g * jax.nn.sigmoid(g @ w_glu.astype(jnp.float32) + b_glu.astype(jnp.float32))
    return out.astype(u.dtype)


def conformer_conv(h, w_pw1, w_dw, b_dw, ln_g, ln_b, w_pw2):
    a, gate = jnp.split(h @ w_pw1, 2, axis=-1)
    z = a * jax.nn.sigmoid(gate)
    z = causal_dwconv(z, w_dw, b_dw)
    z = layernorm(z, ln_g, ln_b)
    return jax.nn.silu(z) @ w_pw2


def conv_ffn(h, w_up, w_dw, b_dw, w_down):
    up = causal_dwconv(h @ w_up, w_dw, b_dw)
    gate, val = jnp.split(up, 2, axis=-1)
    return (jax.nn.silu(gate) * val) @ w_down


def setup_inputs(seed: int = 0) -> dict:
    key = jax.random.key(seed)
    ks = iter(jax.random.split(key, 48))
    f32 = jnp.float32

    def nrm(shape, scale):
        return jax.random.normal(next(ks), shape, f32) * scale

    def gain(shape):
        return 1.0 + nrm(shape, 0.02)

    na, nc = N_AB_LAYERS, N_C_LAYERS
    G, P, H16 = N_SSM_GROUPS, SSM_STATE, SSM_GROUP
    lam_im0 = jnp.pi * jnp.arange(P, dtype=f32)
    return {
        'x': nrm((BATCH, SEQ, D_MODEL), 1.0),
        'rel_bias': nrm((N_BUCKETS, N_DIFF_HEADS), 0.5),
        'norm_mix': gain((DEPTH, D_MODEL)),
        'norm_ffn': gain((DEPTH, D_MODEL)),
        'norm_final': gain((D_MODEL,)),
        'ab_w_in': nrm((na, D_MODEL, IN_WIDTH), D_MODEL ** -0.5),
        'ab_w_out': nrm((na, MIX_WIDTH, D_MODEL), MIX_WIDTH ** -0.5),
        'diff_lq1': nrm((na, DIFF_HEAD_DIM), 0.1),
        'diff_lk1': nrm((na, DIFF_HEAD_DIM), 0.1),
        'diff_lq2': nrm((na, DIFF_HEAD_DIM), 0.1),
        'diff_lk2': nrm((na, DIFF_HEAD_DIM), 0.1),
        'diff_head_norm': gain((na, 2 * DIFF_HEAD_DIM)),
        's5_lambda_re': -0.5 + nrm((na, G, P), 0.01),
        's5_lambda_im': lam_im0 + nrm((na, G, P), 0.01),
        's5_log_step': jax.random.uniform(next(ks), (na, G), f32, math.log(0.001), math.log(0.1)),
        's5_b_re': nrm((na, G, P, H16), (2 * H16) ** -0.5),
        's5_b_im': nrm((na, G, P, H16), (2 * H16) ** -0.5),
        's5_c_re': nrm((na, G, H16, P), (2 * P) ** -0.5),
        's5_c_im': nrm((na, G, H16, P), (2 * P) ** -0.5),
        's5_d': nrm((na, G, H16), 0.5),
        's5_w_glu': nrm((na, SSM_WIDTH, SSM_WIDTH), SSM_WIDTH ** -0.5),
        's5_b_glu': nrm((na, SSM_WIDTH), 0.02),
        'conv_w_pw1': nrm((nc, D_MODEL, 2 * D_MODEL), D_MODEL ** -0.5),
        'conv_w_dw': nrm((nc, CONV_KERNEL, D_MODEL), CONV_KERNEL ** -0.5),
        'conv_b_dw': nrm((nc, D_MODEL), 0.02),
        'conv_ln_g': gain((nc, D_MODEL)),
        'conv_ln_b': nrm((nc, D_MODEL), 0.02),
        'conv_w_pw2': nrm((nc, D_MODEL, D_MODEL), D_MODEL ** -0.5),
        'ffn_w_up': nrm((DEPTH, D_MODEL, 2 * FFN_HIDDEN), D_MODEL ** -0.5),
        'ffn_w_dw': nrm((DEPTH, FFN_CONV, 2 * FFN_HIDDEN), FFN_CONV ** -0.5),
        'ffn_b_dw': nrm((DEPTH, 2 * FFN_HIDDEN), 0.02),
        'ffn_w_down': nrm((DEPTH, FFN_HIDDEN, D_MODEL), FFN_HIDDEN ** -0.5),
    }


def reference(x, rel_bias, norm_mix, norm_ffn, norm_final, ab_w_in, ab_w_out,
              diff_lq1, diff_lk1, diff_lq2, diff_lk2, diff_head_norm,
              s5_lambda_re, s5_lambda_im, s5_log_step, s5_b_re, s5_b_im, s5_c_re, s5_c_im,
              s5_d, s5_w_glu, s5_b_glu,
              conv_w_pw1, conv_w_dw, conv_b_dw, conv_ln_g, conv_ln_b, conv_w_pw2,
              ffn_w_up, ffn_w_dw, ffn_b_dw, ffn_w_down):
    bsz, slen, _ = x.shape
    for layer in range(DEPTH):
        i = layer // 2
        h = rmsnorm(x, norm_mix[layer])
        if layer % 2 == 0:
            proj = h @ ab_w_in[i]
            q, k, v, u = jnp.split(proj, [ATT_WIDTH, 2 * ATT_WIDTH, 3 * ATT_WIDTH], axis=-1)
            q = q.reshape(bsz, slen, N_DIFF_HEADS, 2, DIFF_HEAD_DIM)
            k = k.reshape(bsz, slen, N_DIFF_HEADS, 2, DIFF_HEAD_DIM)
            v = v.reshape(bsz, slen, N_DIFF_HEADS, 2 * DIFF_HEAD_DIM)
            lam_init = 0.8 - 0.6 * math.exp(-0.3 * layer)
            lam = (jnp.exp(jnp.sum(diff_lq1[i].astype(jnp.float32) * diff_lk1[i].astype(jnp.float32)))
                   - jnp.exp(jnp.sum(diff_lq2[i].astype(jnp.float32) * diff_lk2[i].astype(jnp.float32)))
                   + lam_init)
            att = diff_attention(q, k, v, rel_bias, lam)
            att = rmsnorm(att, diff_head_norm[i]) * (1.0 - lam_init)
            att = att.reshape(bsz, slen, ATT_WIDTH).astype(x.dtype)
            ssm = s5_mixer(u, s5_lambda_re[i], s5_lambda_im[i], s5_log_step[i],
                           s5_b_re[i], s5_b_im[i], s5_c_re[i], s5_c_im[i], s5_d[i],
                           s5_w_glu[i], s5_b_glu[i])
            mix = jnp.concatenate([att, ssm], axis=-1) @ ab_w_out[i]
        else:
            mix = conformer_conv(h, conv_w_pw1[i], conv_w_dw[i], conv_b_dw[i],
                                 conv_ln_g[i], conv_ln_b[i], conv_w_pw2[i])
        x = x + mix
        h = rmsnorm(x, norm_ffn[layer])
        x = x + conv_ffn(h, ffn_w_up[layer], ffn_w_dw[layer], ffn_b_dw[layer], ffn_w_down[layer])
    return rmsnorm(x, norm_final)
```

```python
import math
import numpy as np
import concourse.bass as bass
import concourse.mybir as mybir
from concourse.bass_utils import run_bass_kernel_spmd

F32 = mybir.dt.float32
BF16 = mybir.dt.bfloat16
ALU = mybir.AluOpType
AF = mybir.ActivationFunctionType

PE, ACT, DVE, POOL, SP = "tensor", "scalar", "vector", "gpsimd", "sync"
ENGS = (PE, ACT, DVE, POOL, SP)

D = 2048
SEQ = 4096
DEPTH = 4
NCH = D // 128
TT = 512
FH = 5632
NFH = FH // 128
EPS = 1e-6
N_BUCKETS = 32
MAX_DISTANCE = 128
NH = 8
HD = 64
SBUF_LO = 24576
SBUF_HI = 229344


class DmaSem:
    def __init__(self, sem):
        self.sem = sem
        self.count = 0
        self.open_ops = []


class Op:
    __slots__ = ("eng", "fn", "deps", "need_inc", "dsem", "incval", "wait_val", "idx")

    def __init__(self, eng, fn, dsem):
        self.eng = eng
        self.fn = fn
        self.deps = []
        self.need_inc = False
        self.dsem = dsem
        self.incval = None
        self.wait_val = None


class Sched:
    def __init__(self, nc):
        self.nc = nc
        self.ops = {e: [] for e in ENGS}
        self.last_w = {}
        self.readers = {}
        self.barrier_deps = {e: [] for e in ENGS}
        self.dma_sems = []
        self.free_sems = []
        self.phase_sems = []
        self.n_ops = 0

    def new_dma_sem(self, name, persistent=False, sw=False):
        pool = [d for d in self.free_sems if d.sw == sw]
        if not persistent and pool:
            ds = pool[-1]
            self.free_sems.remove(ds)
        else:
            ds = DmaSem(self.nc.alloc_semaphore(name))
            self.dma_sems.append(ds)
        ds.sw = sw
        ds.persistent = persistent
        if not persistent:
            self.phase_sems.append(ds)
        return ds

    def add(self, eng, fn, reads=(), writes=(), dsem=None):
        op = Op(eng, fn, dsem)
        deps = []
        for k in reads:
            w = self.last_w.get(k)
            if w is not None:
                deps.append(w)
        for k in writes:
            w = self.last_w.get(k)
            if w is not None:
                deps.append(w)
            deps.extend(self.readers.get(k, ()))
        for k in reads:
            self.readers.setdefault(k, []).append(op)
        for k in writes:
            self.last_w[k] = op
            self.readers[k] = []
        if self.barrier_deps[eng]:
            deps.extend(self.barrier_deps[eng])
            self.barrier_deps[eng] = []
        seen = set()
        for d in deps:
            if d is op or id(d) in seen:
                continue
            seen.add(id(d))
            if d.eng == PE and eng == PE and d.dsem is None and dsem is None:
                continue
            d.need_inc = True
            op.deps.append(d)
        if dsem is not None:
            dsem.count += 16
            op.incval = dsem.count
            dsem.open_ops.append(op)
            op.need_inc = True
        self.ops[eng].append(op)
        self.n_ops += 1
        return op

    def close_group(self, dsem):
        for o in dsem.open_ops:
            o.wait_val = dsem.count
        dsem.open_ops = []

    def barrier(self):
        lasts = []
        for e in ENGS:
            for o in reversed(self.ops[e]):
                if o.dsem is None:
                    lasts.append(o)
                    break
        for ds in self.dma_sems:
            ds.open_ops = []
            if ds.count:
                po = Op(None, None, ds)
                po.wait_val = ds.count
                lasts.append(po)
        for e in ENGS:
            self.barrier_deps[e] = list(lasts)
        self.last_w = {}
        self.readers = {}
        self.free_sems.extend(self.phase_sems)
        self.phase_sems = []

    def emit(self, final_waits=()):
        nc = self.nc
        esem = {e: nc.alloc_semaphore("prog_" + e) for e in ENGS}
        for e in ENGS:
            c = 0
            for o in self.ops[e]:
                if o.dsem is None and o.need_inc:
                    c += 1
                    o.incval = c
        ops = self.ops

        def stream(e):
            def body(eng):
                waited = {}
                for o in ops[e]:
                    for d in o.deps:
                        if d.dsem is not None:
                            sem = d.dsem.sem
                            val = d.wait_val if d.wait_val is not None else d.incval
                        else:
                            sem = esem[d.eng]
                            val = d.incval
                        key = id(sem)
                        if waited.get(key, 0) >= val:
                            continue
                        waited[key] = val
                        eng.wait_ge(sem, val)
                    ins = o.fn(eng)
                    if o.dsem is not None:
                        ins.then_inc(o.dsem.sem, 16)
                    elif o.need_inc:
                        ins.then_inc(esem[e], 1)
                if e == SP:
                    for ds in final_waits:
                        eng.wait_ge(ds.sem, ds.count)
            return body

        with nc.Block() as block:
            block.tensor(stream(PE))
            block.scalar(stream(ACT))
            block.vector(stream(DVE))
            block.gpsimd(stream(POOL))
            block.sync(stream(SP))


class Arena:
    def __init__(self, nc):
        self.nc = nc
        self.off = SBUF_LO
        self.n = 0

    def reset(self, to=SBUF_LO):
        self.off = to

    def alloc(self, shape, dtype, name="t"):
        size = int(np.prod(shape[1:])) * (2 if dtype == BF16 else 4)
        size = (size + 63) // 64 * 64
        assert self.off + size <= SBUF_HI, f"SBUF overflow {name} {self.off} {size}"
        self.n += 1
        t = self.nc.alloc_sbuf_tensor_at(f"{name}{self.n}", list(shape), dtype, offset=self.off)
        self.off += size
        return t


ATT_W = 1024
INTERLEAVE_AB = True
SSM_W = 1024
CONV_K = 31
NPAIR = 32
NEG = -30000.0
LAM_INIT = [0.8 - 0.6 * math.exp(-0.3 * l) for l in range(DEPTH)]

NORM_MIX_COL0 = 0
NORM_FFN_COL0 = NORM_MIX_COL0 + DEPTH * NCH
NORM_FINAL_COL0 = NORM_FFN_COL0 + DEPTH * NCH
FFN_COL0 = NORM_FINAL_COL0 + NCH
FFN_NCOL = 2 * NFH * 4
CF_COL0 = FFN_COL0 + DEPTH * FFN_NCOL
CF_NCOL = NCH * 32 + 2 * NCH
AB_COL0 = CF_COL0 + 2 * CF_NCOL
AB_BGLU, AB_D, AB_HN, AB_LAM, AB_S5 = 0, 8, 16, 17, 17 + 256
AB_NCOL = 17 + 256 + 3 * NPAIR
CH_COL0 = AB_COL0 + 2 * AB_NCOL
NCOLS = CH_COL0 + NH


def t5_bucket_np(rel):
    n = np.maximum(rel, 0)
    max_exact = N_BUCKETS // 2
    nf = np.maximum(n, 1).astype(np.float32)
    large = max_exact + (np.log(nf / np.float32(max_exact)) / np.float32(math.log(MAX_DISTANCE / max_exact))
                         * np.float32(N_BUCKETS - max_exact)).astype(np.int32)
    large = np.minimum(large, N_BUCKETS - 1)
    return np.where(n < max_exact, n, large)


def pack_cols(inp):
    cols = np.zeros((128, NCOLS), np.float32)

    def put(c0, vec):
        v = np.asarray(vec, np.float32).reshape(-1, 128).T
        cols[:, c0:c0 + v.shape[1]] = v

    for l in range(DEPTH):
        put(NORM_MIX_COL0 + l * NCH, inp["norm_mix"][l])
        put(NORM_FFN_COL0 + l * NCH, inp["norm_ffn"][l])
        w = np.asarray(inp["ffn_w_dw"][l], np.float32)
        b = np.asarray(inp["ffn_b_dw"][l], np.float32)
        blk = np.stack([w[0], w[1], w[2], b], axis=-1)
        blk = blk.reshape(2 * NFH, 128, 4).transpose(1, 0, 2).reshape(128, 2 * NFH * 4)
        cols[:, FFN_COL0 + l * FFN_NCOL: FFN_COL0 + (l + 1) * FFN_NCOL] = blk
    put(NORM_FINAL_COL0, inp["norm_final"])
    for i in range(2):
        c0 = CF_COL0 + i * CF_NCOL
        w = np.asarray(inp["conv_w_dw"][i], np.float32)
        b = np.asarray(inp["conv_b_dw"][i], np.float32)
        blk = np.concatenate([w.T, b[:, None]], axis=1)
        blk = blk.reshape(NCH, 128, 32).transpose(1, 0, 2).reshape(128, NCH * 32)
        cols[:, c0:c0 + NCH * 32] = blk
        put(c0 + NCH * 32, inp["conv_ln_g"][i])
        put(c0 + NCH * 32 + NCH, inp["conv_ln_b"][i])
    for i in range(2):
        c0 = AB_COL0 + i * AB_NCOL
        put(c0 + AB_BGLU, inp["s5_b_glu"][i])
        put(c0 + AB_D, np.asarray(inp["s5_d"][i]).reshape(-1))
        cols[:, c0 + AB_HN] = np.asarray(inp["diff_head_norm"][i], np.float32)
        for q, nm in enumerate(["diff_lq1", "diff_lk1", "diff_lq2", "diff_lk2"]):
            cols[:, c0 + AB_LAM + q * 64: c0 + AB_LAM + (q + 1) * 64] = np.asarray(inp[nm][i], np.float32)[None, :]
        lre = np.asarray(inp["s5_lambda_re"][i], np.float32).reshape(NPAIR, 128)
        lim = np.asarray(inp["s5_lambda_im"][i], np.float32).reshape(NPAIR, 128)
        lst = np.repeat(np.asarray(inp["s5_log_step"][i], np.float32), 64).reshape(NPAIR, 128)
        for gp in range(NPAIR):
            cols[:, c0 + AB_S5 + 3 * gp + 0] = lre[gp]
            cols[:, c0 + AB_S5 + 3 * gp + 1] = lim[gp]
            cols[:, c0 + AB_S5 + 3 * gp + 2] = lst[gp]
    cols[:, CH_COL0:CH_COL0 + NH] = np.asarray(inp["rel_bias"], np.float32)[N_BUCKETS - 1][None, :]
    return cols


def pack_bias(inp):
    rb = np.asarray(inp["rel_bias"], np.float32)
    k = np.arange(128)[:, None]
    q = np.arange(128)[None, :]
    b0 = t5_bucket_np(q - k)
    b1 = t5_bucket_np(128 + q - k)
    out = np.zeros((128, NH, 2, 128), np.float32)
    for h in range(NH):
        out[:, h, 0, :] = rb[b0, h]
        out[:, h, 1, :] = rb[b1, h]
    return out


def pack_consts():
    c = np.zeros((128, 128 + 512), np.float32)
    k = np.arange(128)[:, None]
    q = np.arange(128)[None, :]
    c[:, 0:128] = np.where(q >= k, 0.0, NEG)
    c[:, 128:] = np.arange(512, dtype=np.float32)[None, :]
    return c


def pack_s5mats(inp):
    out = np.zeros((2, 8, 128, 16, 128), np.float32)
    for i in range(2):
        B = [np.asarray(inp["s5_b_re"][i], np.float32), np.asarray(inp["s5_b_im"][i], np.float32)]
        C = [np.asarray(inp["s5_c_re"][i], np.float32), np.asarray(inp["s5_c_im"][i], np.float32)]
        for c in range(8):
            for pr in range(4):
                for g2 in range(2):
                    g = (c * 4 + pr) * 2 + g2
                    ch0 = pr * 32 + g2 * 16
                    for m in range(2):
                        out[i, c, ch0:ch0 + 16, pr * 4 + m, g2 * 64:(g2 + 1) * 64] = B[m][g].T
                        out[i, c, g2 * 64:(g2 + 1) * 64, pr * 4 + 2 + m, ch0:ch0 + 16] = C[m][g].T
    return out


def build_program(T=SEQ, n_layers=DEPTH):
    nc = bass.Bass("TRN2", target_bir_lowering=False)
    S = Sched(nc)
    A = Arena(nc)
    NT = T // TT
    NKB = T // 128
    n_ab = (n_layers + 1) // 2
    n_c = n_layers // 2

    def din(name, shape):
        return nc.dram_tensor(name, list(shape), F32, kind="ExternalInput").ap()

    def dscr(name, shape, dt):
        return nc.dram_tensor(name, list(shape), dt).ap()

    xT_in = din("xT", (D, T))
    outT = nc.dram_tensor("outT", [D, T], F32, kind="ExternalOutput").ap()
    cols_in = din("cols", (128, NCOLS))
    bias_in = din("biasT", (128, NH, 2, 128))
    consts_in = din("consts", (128, 640))
    s5mats_in = din("s5mats", (2, 8, 128, 16, 128))
    ffn_w_up = din("ffn_w_up", (DEPTH, D, 2 * FH))
    ffn_w_down = din("ffn_w_down", (DEPTH, FH, D))
    ab_w_in = din("ab_w_in", (2, D, 4096))
    ab_w_out = din("ab_w_out", (2, D, D))
    s5_w_glu = din("s5_w_glu", (2, SSM_W, SSM_W))
    conv_w_pw1 = din("conv_w_pw1", (2, D, 2 * D))
    conv_w_pw2 = din("conv_w_pw2", (2, D, D))

    XT = dscr("XT", [D, T], F32)
    WUP = dscr("WUP", [DEPTH, NFH, 128, 2, NCH, 128], BF16)
    WDN = dscr("WDN", [DEPTH, NCH, 128, NFH, 128], BF16)
    WIN = dscr("WIN", [2, 32, 128, NCH, 128], BF16)
    WV = dscr("WV", [2, 2, 128, NCH, 512], BF16)
    WOUT = dscr("WOUT", [2, NCH, 128, NCH, 128], BF16)
    WGLU = dscr("WGLU", [2, 8, 128, 8, 128], BF16)
    WP1 = dscr("WP1", [2, 32, 128, NCH, 128], BF16)
    WP2 = dscr("WP2", [2, NCH, 128, NCH, 128], BF16)
    QT = dscr("QT", [ATT_W, T], BF16)
    KT = dscr("KT", [ATT_W, T], BF16)
    VTM = dscr("VTM", [T, ATT_W], BF16)
    U32 = dscr("U32", [SSM_W, T], F32)
    ATT = dscr("ATT", [ATT_W, T], BF16)
    G32 = dscr("G32", [SSM_W, T], F32)

    ps = [nc.alloc_psum_tensor(f"ps{i}", [128, 512], F32) for i in range(8)]

    cols = A.alloc([128, NCOLS], F32, "cols")
    ones_bf = A.alloc([128, 128], BF16, "onesD")
    ones1 = A.alloc([128, 128], BF16, "ones1")
    ones128 = A.alloc([128, 128], BF16, "ones128")
    onesf = A.alloc([128, 128], F32, "onesf")
    cst = A.alloc([128, 8], F32, "cst")
    consts = A.alloc([128, 640], F32, "consts")
    persist_end = A.off

    ld_const = S.new_dma_sem("ld_const", persistent=True)
    S.add(SP, lambda e: e.dma_start(out=cols[:, :], in_=cols_in), writes=["cols"], dsem=ld_const)
    ld_const2 = S.new_dma_sem("ld_const2", persistent=True)
    S.add(SP, lambda e: e.dma_start(out=consts[:, :], in_=consts_in), writes=["consts"], dsem=ld_const2)
    S.add(POOL, lambda e: e.memset(ones_bf[:, :], 1.0 / D), writes=["ones"])
    S.add(POOL, lambda e: e.memset(ones1[:, :], 1.0), writes=["ones1"])
    S.add(POOL, lambda e: e.memset(ones128[:, :], 1.0 / 128), writes=["ones128"])
    S.add(POOL, lambda e: e.memset(onesf[:, :], 1.0 / D), writes=["onesf"])
    S.add(POOL, lambda e: e.memset(cst[:, 0:1], EPS), writes=["cst0"])
    S.add(POOL, lambda e: e.memset(cst[:, 1:2], -math.pi), writes=["cst1"])
    epsc = cst
    maskT = consts[:, 0:128]
    iota = consts[:, 128:640]

    st_out = S.new_dma_sem("st_out", persistent=True, sw=True)

    def cview(ap2d):
        return ap2d.rearrange("(c p) t -> p c t", p=128)

    def prepass():
        A.reset(persist_end)
        NS = 3
        STG = 8192
        stage = [A.alloc([128, STG], F32, "stg") for _ in range(NS)]
        stb = [A.alloc([128, STG], BF16, "stb") for _ in range(NS)]
        lsem = [S.new_dma_sem(f"pp_ld{i}") for i in range(NS)]
        ssem = [S.new_dma_sem(f"pp_st{i}", sw=True) for i in range(NS)]
        cnt = [0]
        cast_engs = [DVE, ACT]

        def convert(src_ap, dst_ap, kc, nb, permute=True):
            i = cnt[0] % NS
            ce = cast_engs[cnt[0] % len(cast_engs)]
            cnt[0] += 1
            n = kc * nb * 128
            sv = stage[i][:, 0:n].rearrange("p (k m) -> p k m", k=kc)
            S.add(SP, lambda e: e.dma_start(out=sv, in_=src_ap), writes=[("stg", i)], dsem=lsem[i])
            if permute:
                cin = stage[i][:, 0:n].rearrange("p (k n b) -> p n k b", k=kc, n=nb)
                cout = stb[i][:, 0:n].rearrange("p (n k b) -> p n k b", n=nb, k=kc)
                bv = stb[i][:, 0:n].rearrange("p (n m) -> p n m", n=nb)
            else:
                cin = stage[i][:, 0:n]
                cout = stb[i][:, 0:n]
                bv = stb[i][:, 0:n]
            if ce == ACT:
                S.add(ACT, lambda e: e.copy(out=cout, in_=cin), reads=[("stg", i)], writes=[("stb", i)])
            else:
                S.add(ce, lambda e: e.tensor_copy(out=cout, in_=cin), reads=[("stg", i)], writes=[("stb", i)])
            S.add(POOL, lambda e: e.dma_start(out=dst_ap, in_=bv), reads=[("stb", i)], dsem=ssem[i])

        def conv_w(src2d, dst, K, N, blocks=None):
            kc = K // 128
            nb = 8192 // (kc * 128)
            sv = src2d.rearrange("(k p) n -> p k n", p=128)
            for oc0 in range(0, N // 128, nb):
                if blocks is not None and oc0 not in blocks:
                    continue
                convert(sv[:, :, oc0 * 128:(oc0 + nb) * 128],
                        dst[oc0:oc0 + nb].rearrange("n p k b -> p n (k b)"), kc, nb)

        for l in range(n_layers):
            i = l // 2
            if l % 2 == 0:
                conv_w(ab_w_in[i], WIN[i], D, 4096, blocks=[0, 4, 8, 12, 24, 28])
                sv = ab_w_in[i].rearrange("(k p) n -> p k n", p=128)
                for hf in range(2):
                    convert(sv[:, :, 2048 + hf * 512: 2048 + (hf + 1) * 512],
                            WV[i, hf].rearrange("p k n -> p (k n)"), NCH, 4, permute=False)
                conv_w(ab_w_out[i], WOUT[i], D, D)
                conv_w(s5_w_glu[i], WGLU[i], SSM_W, SSM_W)
            else:
                conv_w(conv_w_pw1[i], WP1[i], D, 2 * D)
                conv_w(conv_w_pw2[i], WP2[i], D, D)
            wu_src = ffn_w_up[l].rearrange("(k p) n -> p k n", p=128)
            for path in range(2):
                for j0 in range(0, NFH, 4):
                    c0 = path * FH + j0 * 128
                    convert(wu_src[:, :, c0:c0 + 512],
                            WUP[l, j0:j0 + 4, :, path].rearrange("n p k b -> p n (k b)"), NCH, 4)
            wd_src = ffn_w_down[l].rearrange("(k p) n -> p k n", p=128)
            for oc0 in range(0, NCH, 2):
                for k0 in range(0, NFH, 22):
                    convert(wd_src[:, k0:k0 + 22, oc0 * 128:oc0 * 128 + 256],
                            WDN[l, oc0:oc0 + 2, :, k0:k0 + 22, :].rearrange("n p k b -> p n (k b)"), 22, 2)
        S.barrier()

    def rsqrt_ps(rstd, psb, key="rstd"):
        S.add(ACT, lambda e: e.activation(out=rstd[:, :], in_=ps[psb][:, :], func=AF.Sqrt, bias=epsc[:, 0:1]),
              reads=[("ps", psb), "cst0"], writes=[key])
        S.add(DVE, lambda e: e.reciprocal(out=rstd[:, :], in_=rstd[:, :]), reads=[key], writes=[key])

    def rmsnorm_tile(xt, hT, gcol0, psb, rstd, sq, out_key="hT"):
        for c in range(NCH):
            S.add(ACT, lambda e, c=c: e.activation(out=sq[c % 2][:, :], in_=xt[:, c, :], func=AF.Square),
                  reads=["xt"], writes=[("sq", c % 2)])
            S.add(PE, lambda e, c=c: e.matmul(ps[psb][:, :], lhsT=ones_bf[:, :], rhs=sq[c % 2][:, :],
                                              start=(c == 0), stop=(c == NCH - 1)),
                  reads=[("sq", c % 2), "ones"], writes=[("ps", psb)])
        rsqrt_ps(rstd, psb)
        for c in range(NCH):
            S.add(DVE, lambda e, c=c: e.scalar_tensor_tensor(
                out=hT[:, c, :], in0=xt[:, c, :], scalar=cols[:, gcol0 + c:gcol0 + c + 1], in1=rstd[:, :],
                op0=ALU.mult, op1=ALU.mult),
                reads=["xt", "rstd", "cols"], writes=[out_key])

    class WStream:
        def __init__(self, tag, kc_max, n=3):
            self.slots = [A.alloc([128, kc_max, 128], BF16, "wl") for _ in range(n)]
            self.sems = [S.new_dma_sem(f"wl_{tag}_{i}") for i in range(n)]
            self.n = n
            self.i = 0
            self.tag = tag

        def load(self, src_blk, kc):
            s = self.i % self.n
            self.i += 1
            sl = self.slots[s]
            S.add(SP, lambda e: e.dma_start(out=sl[:, 0:kc, :], in_=src_blk),
                  writes=[("wl", self.tag, s)], dsem=self.sems[s])
            return sl, ("wl", self.tag, s)

    def linear_block(ws, wsrc_blk, kc, rhs_fn, rhs_keys, pb):
        sl, key = ws.load(wsrc_blk, kc)
        for k in range(kc):
            S.add(PE, lambda e, k=k: e.matmul(ps[pb][:, :], lhsT=sl[:, k, :], rhs=rhs_fn(k),
                                              start=(k == 0), stop=(k == kc - 1)),
                  reads=[key] + list(rhs_keys(k)), writes=[("ps", pb)])

    def load_x(xt, src, t, sem):
        tsl = slice(t * TT, (t + 1) * TT)
        S.add(SP, lambda e: e.dma_start(out=xt[:, :, :], in_=cview(src)[:, :, tsl]),
              writes=["xt"], dsem=sem)

    def store_x(xt, dst, t, sem):
        tsl = slice(t * TT, (t + 1) * TT)
        S.add(POOL, lambda e: e.dma_start(out=cview(dst)[:, :, tsl], in_=xt[:, :, :]),
              reads=["xt"], dsem=sem)

    def ffn_phase(l, src, dst, final_norm=False):
        A.reset(persist_end)
        xt = A.alloc([128, NCH, TT], F32, "xt")
        hT = A.alloc([128, NCH, TT], BF16, "hT")
        hid = A.alloc([128, NFH, TT], BF16, "hid")
        NWU, NWD = 3, 2
        wu = [A.alloc([128, 2, NCH, 128], BF16, "wu") for _ in range(NWU)]
        wd = [A.alloc([128, NFH, 128], BF16, "wd") for _ in range(NWD)]
        wu_sem = [S.new_dma_sem(f"wu{l}_{i}") for i in range(NWU)]
        wd_sem = [S.new_dma_sem(f"wd{l}_{i}") for i in range(NWD)]
        x_sem = S.new_dma_sem(f"fx{l}")
        xs_sem = st_out if dst is outT else S.new_dma_sem(f"fxs{l}", sw=True)
        gb = [[A.alloc([128, TT + 2], F32, "gb") for _ in range(2)] for _ in range(2)]
        acc = [[A.alloc([128, TT], F32, "acc") for _ in range(2)] for _ in range(2)]
        sg = [A.alloc([128, TT], F32, "sg") for _ in range(2)]
        halo = A.alloc([128, 2 * NFH, 2], F32, "halo")
        rstd = A.alloc([128, TT], F32, "rstd")
        sq = [A.alloc([128, TT], BF16, "sq") for _ in range(2)]
        cw = FFN_COL0 + l * FFN_NCOL

        S.add(POOL, lambda e: e.memset(halo[:, :, :], 0.0), writes=["halo"])
        nwu = nwd = 0
        it = 0
        for t in range(NT):
            load_x(xt, src, t, x_sem)
            rmsnorm_tile(xt, hT, NORM_FFN_COL0 + l * NCH, 7, rstd, sq)
            for j in range(NFH):
                ws = nwu % NWU
                nwu += 1
                S.add(SP, lambda e, ws=ws, j=j: e.dma_start(out=wu[ws][:, :, :, :], in_=WUP[l, j]),
                      writes=[("wu", ws)], dsem=wu_sem[ws])
                for path in range(2):
                    sl = it % 2
                    pb = (it % 2) * 2 + path
                    ch = path * NFH + j
                    for k in range(NCH):
                        S.add(PE, lambda e, ws=ws, k=k, path=path, pb=pb: e.matmul(
                            ps[pb][:, :], lhsT=wu[ws][:, path, k, :], rhs=hT[:, k, :],
                            start=(k == 0), stop=(k == NCH - 1)),
                            reads=[("wu", ws), "hT"], writes=[("ps", pb)])
                    g = gb[path][sl]
                    a = acc[path][sl]
                    S.add(ACT, lambda e, g=g, pb=pb: e.copy(out=g[:, 2:TT + 2], in_=ps[pb][:, :]),
                          reads=[("ps", pb)], writes=[("gb", path, sl)])
                    S.add(POOL, lambda e, g=g, ch=ch: e.tensor_copy(out=g[:, 0:2], in_=halo[:, ch, :]),
                          reads=["halo"], writes=[("gbh", path, sl)])
                    wc = cw + ch * 4
                    S.add(DVE, lambda e, g=g, a=a, wc=wc: e.tensor_scalar(
                        out=a[:, :], in0=g[:, 2:TT + 2], scalar1=cols[:, wc + 2:wc + 3],
                        scalar2=cols[:, wc + 3:wc + 4], op0=ALU.mult, op1=ALU.add),
                        reads=[("gb", path, sl), "cols"], writes=[("acc", path, sl)])
                    S.add(DVE, lambda e, g=g, a=a, wc=wc: e.scalar_tensor_tensor(
                        out=a[:, :], in0=g[:, 1:TT + 1], scalar=cols[:, wc + 1:wc + 2], in1=a[:, :],
                        op0=ALU.mult, op1=ALU.add),
                        reads=[("gb", path, sl), ("gbh", path, sl), ("acc", path, sl)], writes=[("acc", path, sl)])
                    S.add(DVE, lambda e, g=g, a=a, wc=wc: e.scalar_tensor_tensor(
                        out=a[:, :], in0=g[:, 0:TT], scalar=cols[:, wc:wc + 1], in1=a[:, :],
                        op0=ALU.mult, op1=ALU.add),
                        reads=[("gb", path, sl), ("gbh", path, sl), ("acc", path, sl)], writes=[("acc", path, sl)])
                    S.add(POOL, lambda e, g=g, ch=ch: e.tensor_copy(out=halo[:, ch, :], in_=g[:, TT:TT + 2]),
                          reads=[("gb", path, sl)], writes=["halo"])
                sl = it % 2
                S.add(ACT, lambda e, sl=sl: e.activation(out=sg[sl][:, :], in_=acc[0][sl][:, :], func=AF.Silu),
                      reads=[("acc", 0, sl)], writes=[("sg", sl)])
                S.add(DVE, lambda e, sl=sl, j=j: e.tensor_tensor(
                    out=hid[:, j, :], in0=sg[sl][:, :], in1=acc[1][sl][:, :], op=ALU.mult),
                    reads=[("sg", sl), ("acc", 1, sl)], writes=[("hid", j)])
                it += 1
            for oc in range(NCH):
                ws = nwd % NWD
                nwd += 1
                pb = 4 + (oc % 2)
                S.add(SP, lambda e, ws=ws, oc=oc: e.dma_start(out=wd[ws][:, :, :], in_=WDN[l, oc]),
                      writes=[("wd", ws)], dsem=wd_sem[ws])
                for k in range(NFH):
                    S.add(PE, lambda e, ws=ws, k=k, pb=pb: e.matmul(
                        ps[pb][:, :], lhsT=wd[ws][:, k, :], rhs=hid[:, k, :],
                        start=(k == 0), stop=(k == NFH - 1)),
                        reads=[("wd", ws), ("hid", k)], writes=[("ps", pb)])
                S.add(DVE, lambda e, oc=oc, pb=pb: e.tensor_tensor(
                    out=xt[:, oc, :], in0=xt[:, oc, :], in1=ps[pb][:, :], op=ALU.add),
                    reads=[("ps", pb), "xt"], writes=["xt"])
            if final_norm:
                for c in range(NCH):
                    S.add(ACT, lambda e, c=c: e.activation(out=sq[c % 2][:, :], in_=xt[:, c, :], func=AF.Square),
                          reads=["xt"], writes=[("sq", c % 2)])
                    S.add(PE, lambda e, c=c: e.matmul(ps[6][:, :], lhsT=ones_bf[:, :], rhs=sq[c % 2][:, :],
                                                      start=(c == 0), stop=(c == NCH - 1)),
                          reads=[("sq", c % 2), "ones"], writes=[("ps", 6)])
                rsqrt_ps(rstd, 6)
                for c in range(NCH):
                    S.add(DVE, lambda e, c=c: e.scalar_tensor_tensor(
                        out=xt[:, c, :], in0=xt[:, c, :],
                        scalar=cols[:, NORM_FINAL_COL0 + c:NORM_FINAL_COL0 + c + 1],
                        in1=rstd[:, :], op0=ALU.mult, op1=ALU.mult),
                        reads=["xt", "rstd", "cols"], writes=["xt"])
            store_x(xt, dst, t, xs_sem)
        S.barrier()

    def conformer_phase(l, src, dst):
        i = l // 2
        A.reset(persist_end)
        xt = A.alloc([128, NCH, TT], F32, "xt")
        hT = A.alloc([128, NCH, TT], BF16, "hT")
        zb = A.alloc([128, NCH, TT + 30], F32, "zb")
        zc = A.alloc([128, NCH, TT], F32, "zc")
        sgm = [A.alloc([128, TT], F32, "sgm") for _ in range(2)]
        sqf = [A.alloc([128, TT], F32, "sqf") for _ in range(2)]
        sq = [A.alloc([128, TT], BF16, "sq") for _ in range(2)]
        rstd = A.alloc([128, TT], F32, "rstd")
        mean = A.alloc([128, TT], F32, "mean")
        lnr = A.alloc([128, TT], F32, "lnr")
        ut = [A.alloc([128, TT], F32, "ut") for _ in range(2)]
        ws = WStream(f"cf{l}", NCH, 3)
        x_sem = S.new_dma_sem(f"cx{l}")
        xs_sem = S.new_dma_sem(f"cxs{l}", sw=True)
        c0 = CF_COL0 + i * CF_NCOL

        S.add(POOL, lambda e: e.memset(zb[:, :, 0:30], 0.0), writes=["zbh"])
        for t in range(NT):
            load_x(xt, src, t, x_sem)
            rmsnorm_tile(xt, hT, NORM_MIX_COL0 + l * NCH, 7, rstd, sq)
            G = 4

            def pw1_group(g0):
                for c in range(g0, g0 + G):
                    pa, pg = (c % 2) * 2, (c % 2) * 2 + 1
                    linear_block(ws, WP1[i, c], NCH, lambda k: hT[:, k, :], lambda k: ["hT"], pa)
                    linear_block(ws, WP1[i, NCH + c], NCH, lambda k: hT[:, k, :], lambda k: ["hT"], pg)
                    sl = c % 2
                    S.add(ACT, lambda e, sl=sl, pg=pg: e.activation(out=sgm[sl][:, :], in_=ps[pg][:, :],
                                                                    func=AF.Sigmoid),
                          reads=[("ps", pg)], writes=[("sgm", sl)])
                    S.add(DVE, lambda e, sl=sl, pa=pa, c=c: e.tensor_tensor(
                        out=zb[:, c, 30:30 + TT], in0=ps[pa][:, :], in1=sgm[sl][:, :], op=ALU.mult),
                        reads=[("ps", pa), ("sgm", sl)], writes=[("zb", c)])

            def conv_group(g0):
                for k in range(CONV_K):
                    for c in range(g0, g0 + G):
                        wc = c0 + c * 32
                        if k == 0:
                            S.add(DVE, lambda e, c=c, wc=wc: e.tensor_scalar(
                                out=zc[:, c, :], in0=zb[:, c, 0:TT], scalar1=cols[:, wc:wc + 1],
                                scalar2=cols[:, wc + 31:wc + 32], op0=ALU.mult, op1=ALU.add),
                                reads=[("zb", c), "zbh", "cols"], writes=[("zc", c)])
                        else:
                            S.add(DVE, lambda e, c=c, wc=wc, k=k: e.scalar_tensor_tensor(
                                out=zc[:, c, :], in0=zb[:, c, k:k + TT], scalar=cols[:, wc + k:wc + k + 1],
                                in1=zc[:, c, :], op0=ALU.mult, op1=ALU.add),
                                reads=[("zb", c), "zbh", ("zc", c)], writes=[("zc", c)])

            def stats_group(g0):
                for c in range(g0, g0 + G):
                    sl = c % 2
                    S.add(ACT, lambda e, c=c, sl=sl: e.activation(out=sqf[sl][:, :], in_=zc[:, c, :], func=AF.Square),
                          reads=[("zc", c)], writes=[("sqf", sl)])
                    S.add(PE, lambda e, c=c: e.matmul(ps[4][:, :], lhsT=onesf[:, :], rhs=zc[:, c, :],
                                                      start=(c == 0), stop=(c == NCH - 1)),
                          reads=[("zc", c), "onesf"], writes=[("ps", 4)])
                    S.add(PE, lambda e, c=c, sl=sl: e.matmul(ps[5][:, :], lhsT=onesf[:, :], rhs=sqf[sl][:, :],
                                                             start=(c == 0), stop=(c == NCH - 1)),
                          reads=[("sqf", sl), "onesf"], writes=[("ps", 5)])

            pw1_group(0)
            for g0 in range(0, NCH, G):
                conv_group(g0)
                if g0 + G < NCH:
                    pw1_group(g0 + G)
                stats_group(g0)
            S.add(POOL, lambda e: e.tensor_copy(out=zb[:, :, 0:30], in_=zb[:, :, TT:TT + 30]),
                  reads=[("zb", c) for c in range(NCH)], writes=["zbh"])
            S.add(ACT, lambda e: e.copy(out=mean[:, :], in_=ps[4][:, :]), reads=[("ps", 4)], writes=["mean"])
            S.add(DVE, lambda e: e.tensor_tensor(out=lnr[:, :], in0=mean[:, :], in1=mean[:, :], op=ALU.mult),
                  reads=["mean"], writes=["lnr"])
            S.add(DVE, lambda e: e.tensor_tensor(out=lnr[:, :], in0=ps[5][:, :], in1=lnr[:, :], op=ALU.subtract),
                  reads=[("ps", 5), "lnr"], writes=["lnr"])
            S.add(DVE, lambda e: e.tensor_scalar_max(out=lnr[:, :], in0=lnr[:, :], scalar1=0.0),
                  reads=["lnr"], writes=["lnr"])
            S.add(ACT, lambda e: e.activation(out=lnr[:, :], in_=lnr[:, :], func=AF.Sqrt, bias=epsc[:, 0:1]),
                  reads=["lnr", "cst0"], writes=["lnr"])
            S.add(DVE, lambda e: e.reciprocal(out=lnr[:, :], in_=lnr[:, :]), reads=["lnr"], writes=["lnr"])
            for c in range(NCH):
                sl = c % 2
                S.add(DVE, lambda e, c=c, sl=sl: e.tensor_tensor(out=ut[sl][:, :], in0=zc[:, c, :], in1=mean[:, :],
                                                                op=ALU.subtract),
                      reads=[("zc", c), "mean"], writes=[("ut", sl)])
                S.add(DVE, lambda e, sl=sl: e.tensor_tensor(out=ut[sl][:, :], in0=ut[sl][:, :], in1=lnr[:, :],
                                                           op=ALU.mult),
                      reads=[("ut", sl), "lnr"], writes=[("ut", sl)])
                gcol = c0 + NCH * 32 + c
                S.add(ACT, lambda e, c=c, sl=sl, gcol=gcol: e.activation(
                    out=hT[:, c, :], in_=ut[sl][:, :], func=AF.Silu,
                    scale=cols[:, gcol:gcol + 1], bias=cols[:, gcol + NCH:gcol + NCH + 1]),
                    reads=[("ut", sl), "cols"], writes=["hT"])
            for oc in range(NCH):
                pb = oc % 2
                linear_block(ws, WP2[i, oc], NCH, lambda k: hT[:, k, :], lambda k: ["hT"], pb)
                S.add(DVE, lambda e, oc=oc, pb=pb: e.tensor_tensor(
                    out=xt[:, oc, :], in0=xt[:, oc, :], in1=ps[pb][:, :], op=ALU.add),
                    reads=[("ps", pb), "xt"], writes=["xt"])
            store_x(xt, dst, t, xs_sem)
        S.barrier()

    def inproj_phase(l, src):
        i = l // 2
        A.reset(persist_end)
        xt = A.alloc([128, NCH, TT], F32, "xt")
        hT = A.alloc([128, NCH, TT], BF16, "hT")
        rstd = A.alloc([128, TT], F32, "rstd")
        sq = [A.alloc([128, TT], BF16, "sq") for _ in range(2)]
        ob = [A.alloc([128, TT], BF16, "ob") for _ in range(2)]
        of = [A.alloc([128, TT], F32, "of") for _ in range(2)]
        wv = A.alloc([128, NCH, 512], BF16, "wv")
        ws = WStream(f"ip{l}", NCH, 3)
        x_sem = S.new_dma_sem(f"ix{l}")
        ob_sem = [S.new_dma_sem(f"iob{l}_{j}", sw=True) for j in range(2)]
        of_sem = [S.new_dma_sem(f"iof{l}_{j}", sw=True) for j in range(2)]
        wv_sem = S.new_dma_sem(f"iwv{l}")
        n = 0
        for t in range(NT):
            tsl = slice(t * TT, (t + 1) * TT)
            load_x(xt, src, t, x_sem)
            rmsnorm_tile(xt, hT, NORM_MIX_COL0 + l * NCH, 7, rstd, sq)
            for blk in list(range(16)) + list(range(24, 32)):
                pb = n % 2
                sl = n % 2
                n += 1
                linear_block(ws, WIN[i, blk], NCH, lambda k: hT[:, k, :], lambda k: ["hT"], pb)
                if blk < 16:
                    dst = (QT if blk < 8 else KT)[(blk % 8) * 128:(blk % 8 + 1) * 128, tsl]
                    S.add(ACT, lambda e, sl=sl, pb=pb: e.copy(out=ob[sl][:, :], in_=ps[pb][:, :]),
                          reads=[("ps", pb)], writes=[("ob", sl)])
                    S.add(POOL, lambda e, sl=sl, dst=dst: e.dma_start(out=dst, in_=ob[sl][:, :]),
                          reads=[("ob", sl)], dsem=ob_sem[sl])
                else:
                    dst = U32[(blk - 24) * 128:(blk - 23) * 128, tsl]
                    S.add(ACT, lambda e, sl=sl, pb=pb: e.copy(out=of[sl][:, :], in_=ps[pb][:, :]),
                          reads=[("ps", pb)], writes=[("of", sl)])
                    S.add(POOL, lambda e, sl=sl, dst=dst: e.dma_start(out=dst, in_=of[sl][:, :]),
                          reads=[("of", sl)], dsem=of_sem[sl])
            for hf in range(2):
                S.add(SP, lambda e, hf=hf: e.dma_start(out=wv[:, :, :], in_=WV[i, hf]),
                      writes=["wv"], dsem=wv_sem)
                for tb in range(4):
                    pb = 2 + (n % 2)
                    sl = n % 2
                    n += 1
                    for k in range(NCH):
                        S.add(PE, lambda e, k=k, tb=tb, pb=pb: e.matmul(
                            ps[pb][:, :], lhsT=hT[:, k, tb * 128:(tb + 1) * 128], rhs=wv[:, k, :],
                            start=(k == 0), stop=(k == NCH - 1)),
                            reads=["hT", "wv"], writes=[("ps", pb)])
                    tok0 = t * TT + tb * 128
                    dst = VTM[tok0:tok0 + 128, hf * 512:(hf + 1) * 512]
                    S.add(ACT, lambda e, sl=sl, pb=pb: e.copy(out=ob[sl][:, :], in_=ps[pb][:, :]),
                          reads=[("ps", pb)], writes=[("ob", sl)])
                    S.add(POOL, lambda e, sl=sl, dst=dst: e.dma_start(out=dst, in_=ob[sl][:, :]),
                          reads=[("ob", sl)], dsem=ob_sem[sl])
        S.barrier()

    def attention_gen(l):
        i = l // 2
        abc = AB_COL0 + i * AB_NCOL
        QW = 256
        NQ = T // QW
        qz = [A.alloc([128, T], BF16, "qz") for _ in range(2)]
        kh = A.alloc([128, T], BF16, "kh")
        vh2 = [A.alloc([128, NKB, 128], BF16, "vh") for _ in range(2)]
        ld_sem = [S.new_dma_sem(f"ah{l}_{q}") for q in range(5)]
        S.add(POOL, lambda e: e.memset(qz[0][64:128, :], 0.0), writes=[("qz", 0)])
        S.add(POOL, lambda e: e.memset(qz[1][0:64, :], 0.0), writes=[("qz", 1)])
        biasT = A.alloc([128, NH, 2, 128], F32, "biasT")
        pT = [A.alloc([128, 2 * QW], BF16, "pT") for _ in range(2)]
        tmp = [A.alloc([128, 2, 128], F32, "tmp") for _ in range(2)]
        lamt = A.alloc([128, 64], F32, "lamt")
        lamc = A.alloc([128, 8], F32, "lamc")
        r = A.alloc([128, 2 * QW], F32, "r")
        o = A.alloc([128, 2 * QW], F32, "o")
        of = A.alloc([128, QW], F32, "of")
        sqb = A.alloc([128, QW], BF16, "sqb")
        rstd = A.alloc([128, QW], F32, "arstd")
        ao = [A.alloc([128, QW], BF16, "ao") for _ in range(2)]
        ao_sem = [S.new_dma_sem(f"ao{l}_{j}", sw=True) for j in range(2)]
        b_sem = S.new_dma_sem(f"ab{l}")

        S.add(SP, lambda e: e.dma_start(out=biasT[:, :, :, :], in_=bias_in), writes=["biasT"], dsem=b_sem)
        for h in range(NH):
            for m in range(2):
                S.add(DVE, lambda e, h=h, m=m: e.tensor_scalar(
                    out=biasT[:, h, m, :], in0=biasT[:, h, m, :], scalar1=cols[:, CH_COL0 + h:CH_COL0 + h + 1],
                    scalar2=None, op0=ALU.subtract),
                    reads=["biasT", "cols"], writes=["biasT"])
            S.add(DVE, lambda e, h=h: e.tensor_tensor(out=biasT[:, h, 0, :], in0=biasT[:, h, 0, :], in1=maskT,
                                                     op=ALU.add),
                  reads=["biasT", "consts"], writes=["biasT"])
        for m in range(2):
            a0 = abc + AB_LAM + m * 128
            S.add(DVE, lambda e, a0=a0: e.tensor_tensor(out=lamt[:, :], in0=cols[:, a0:a0 + 64],
                                                       in1=cols[:, a0 + 64:a0 + 128], op=ALU.mult),
                  reads=["cols"], writes=["lamt"])
            S.add(DVE, lambda e, m=m: e.reduce_sum(out=lamc[:, 4 + m:5 + m], in_=lamt[:, :],
                                                  axis=mybir.AxisListType.X),
                  reads=["lamt"], writes=[("lamc", 4 + m)])
            S.add(ACT, lambda e, m=m: e.activation(out=lamc[:, m:m + 1], in_=lamc[:, 4 + m:5 + m], func=AF.Exp),
                  reads=[("lamc", 4 + m)], writes=[("lamc", m)])
        S.add(DVE, lambda e: e.tensor_tensor(out=lamc[:, 2:3], in0=lamc[:, 1:2], in1=lamc[:, 0:1], op=ALU.subtract),
              reads=[("lamc", 0), ("lamc", 1)], writes=[("lamc", 2)])
        S.add(DVE, lambda e: e.tensor_scalar_add(out=lamc[:, 2:3], in0=lamc[:, 2:3], scalar1=-LAM_INIT[l]),
              reads=[("lamc", 2)], writes=[("lamc", 2)])
        S.add(DVE, lambda e: e.tensor_scalar_mul(out=lamc[:, 3:4], in0=cols[:, abc + AB_HN:abc + AB_HN + 1],
                                                 scalar1=1.0 - LAM_INIT[l]),
              reads=["cols"], writes=[("lamc", 3)])
        yield

        scale = HD ** -0.5
        loaded = set()

        def v3(ap2d):
            return ap2d.rearrange("p (m q) -> p m q", m=2)

        def emit_loads(h):
            if h in loaded:
                return
            loaded.add(h)
            S.add(SP, lambda e: e.dma_start(out=qz[0][0:64, :], in_=QT[h * 128:h * 128 + 64, :]),
                  writes=[("qz", 0)], dsem=ld_sem[0])
            S.add(SP, lambda e: e.dma_start(out=qz[1][64:128, :], in_=QT[h * 128 + 64:(h + 1) * 128, :]),
                  writes=[("qz", 1)], dsem=ld_sem[4])
            S.add(SP, lambda e: e.dma_start(out=kh[:, :], in_=KT[h * 128:(h + 1) * 128, :]),
                  writes=["kh"], dsem=ld_sem[1])
            S.add(SP, lambda e: e.dma_start(
                out=vh2[h % 2][:, :, :], in_=VTM[:, h * 128:(h + 1) * 128].rearrange("(j p) d -> p j d", p=128)),
                writes=[("vh", h % 2)], dsem=ld_sem[2 + h % 2])

        def emit_qk(n, h, t, j):
            emit_loads(h)
            sl = n % 2
            c_lo = max(j - 2 * t, 0) * 128
            for m in range(2):
                S.add(PE, lambda e, m=m: e.matmul(
                    ps[sl][:, m * QW + c_lo:(m + 1) * QW], lhsT=kh[:, j * 128:(j + 1) * 128],
                    rhs=qz[m][:, t * QW + c_lo:(t + 1) * QW], start=True, stop=True),
                    reads=["kh", ("qz", m)], writes=[("ps", sl)])

        def emit_exp(n, h, t, j):
            sl = n % 2
            jj = j - 2 * t
            c_lo = max(jj, 0) * 128
            near = []
            if jj >= 0:
                near.append((jj * 128, 0))
                if jj < 1:
                    near.append(((jj + 1) * 128, 1))
            elif jj == -1:
                near.append((0, 1))
            far_lo = c_lo + 128 * len(near) if jj >= 0 else (128 if jj == -1 else 0)
            S3 = v3(ps[sl][:, :])
            P3 = v3(pT[sl][:, :])
            for ni, (cq, which) in enumerate(near):
                for m in range(2):
                    S.add(DVE, lambda e, m=m, ni=ni, cq=cq, which=which: e.scalar_tensor_tensor(
                        out=tmp[ni][:, m, :], in0=S3[:, m, cq:cq + 128], scalar=scale,
                        in1=biasT[:, h, which, :], op0=ALU.mult, op1=ALU.add),
                        reads=[("ps", sl), "biasT"], writes=[("tmp", ni)])
                S.add(ACT, lambda e, ni=ni, cq=cq: e.activation(
                    out=P3[:, :, cq:cq + 128], in_=tmp[ni][:, :, :], func=AF.Exp),
                    reads=[("tmp", ni)], writes=[("pT", sl)])
            if far_lo < QW:
                S.add(ACT, lambda e: e.activation(
                    out=P3[:, :, far_lo:QW], in_=S3[:, :, far_lo:QW], func=AF.Exp, scale=scale),
                    reads=[("ps", sl)], writes=[("pT", sl)])
            if c_lo > 0:
                S.add(POOL, lambda e: e.memset(P3[:, :, 0:c_lo], 0.0), writes=[("pT", sl)])

        def emit_pv(n, h, t, j):
            sl = n % 2
            nj = 2 * t + 2
            S.add(PE, lambda e: e.matmul(ps[2][:, :], lhsT=vh2[h % 2][:, j, :], rhs=pT[sl][:, :],
                                         start=(j == 0), stop=(j == nj - 1)),
                  reads=[("vh", h % 2), ("pT", sl)], writes=[("ps", 2)])
            S.add(PE, lambda e: e.matmul(ps[3][:, :], lhsT=ones1[:, :], rhs=pT[sl][:, :],
                                         start=(j == 0), stop=(j == nj - 1)),
                  reads=["ones1", ("pT", sl)], writes=[("ps", 3)])

        nout = [0]

        def emit_epilogue(h, t):
            S.add(DVE, lambda e: e.reciprocal(out=r[:, :], in_=ps[3][:, :]), reads=[("ps", 3)], writes=["ar"])
            S.add(DVE, lambda e: e.tensor_tensor(out=o[:, :], in0=ps[2][:, :], in1=r[:, :], op=ALU.mult),
                  reads=[("ps", 2), "ar"], writes=["ao_"])
            S.add(DVE, lambda e: e.scalar_tensor_tensor(out=of[:, :], in0=o[:, QW:2 * QW], scalar=lamc[:, 2:3],
                                                        in1=o[:, 0:QW], op0=ALU.mult, op1=ALU.add),
                  reads=["ao_", ("lamc", 2)], writes=["aof"])
            S.add(ACT, lambda e: e.activation(out=sqb[:, :], in_=of[:, :], func=AF.Square),
                  reads=["aof"], writes=["sqb"])
            S.add(PE, lambda e: e.matmul(ps[3][:, 0:QW], lhsT=ones128[:, :], rhs=sqb[:, :], start=True, stop=True),
                  reads=["ones128", "sqb"], writes=[("ps", 3)])
            S.add(ACT, lambda e: e.activation(out=rstd[:, :], in_=ps[3][:, 0:QW], func=AF.Sqrt, bias=epsc[:, 0:1]),
                  reads=[("ps", 3), "cst0"], writes=["arstd"])
            S.add(DVE, lambda e: e.reciprocal(out=rstd[:, :], in_=rstd[:, :]), reads=["arstd"], writes=["arstd"])
            asl = nout[0] % 2
            nout[0] += 1
            S.add(DVE, lambda e: e.scalar_tensor_tensor(
                out=ao[asl][:, :], in0=of[:, :], scalar=lamc[:, 3:4], in1=rstd[:, :],
                op0=ALU.mult, op1=ALU.mult),
                reads=["aof", "arstd", ("lamc", 3)], writes=[("ao", asl)])
            S.add(POOL, lambda e: e.dma_start(
                out=ATT[h * 128:(h + 1) * 128, t * QW:(t + 1) * QW], in_=ao[asl][:, :]),
                reads=[("ao", asl)], dsem=ao_sem[asl])

        plist = [(h, t, j) for h in range(NH) for t in range(NQ) for j in range(2 * t + 2)]
        emit_qk(0, *plist[0])
        for n, (h, t, j) in enumerate(plist):
            if n + 1 < len(plist):
                emit_qk(n + 1, *plist[n + 1])
            emit_exp(n, h, t, j)
            emit_pv(n, h, t, j)
            if j == 2 * t + 1:
                emit_epilogue(h, t)
            yield

    def ab_mix_phase(l):
        A.reset(persist_end)
        ga = attention_gen(l)
        gs = s5_gen(l)
        NQ = T // 256
        n_a = 1 + NH * NQ * (NQ + 1)
        n_s = 8 * (4 * (1 + NT) + NT)
        ratio = n_a / n_s
        if not INTERLEAVE_AB:
            for _ in ga:
                pass
            for _ in gs:
                pass
            S.barrier()
            return
        acc = 0.0
        done_a = done_s = False
        while not (done_a and done_s):
            if not done_s:
                try:
                    next(gs)
                except StopIteration:
                    done_s = True
            acc += ratio
            while (acc >= 1.0 or done_s) and not done_a:
                acc -= 1.0
                try:
                    next(ga)
                except StopIteration:
                    done_a = True
        S.barrier()

    def s5_gen(l):
        i = l // 2
        abc = AB_COL0 + i * AB_NCOL
        L = TT
        u32 = A.alloc([128, T], F32, "u32")
        ubf = A.alloc([128, T], BF16, "ubf")
        yacc = A.alloc([128, T], F32, "yacc")
        gt1 = [A.alloc([128, TT], F32, "gt1") for _ in range(2)]
        gt2 = [A.alloc([128, TT], F32, "gt2") for _ in range(2)]
        mstage = A.alloc([128, 16, 128], F32, "mstage")
        mats = A.alloc([128, 16, 128], BF16, "mats")
        tb = {nm: A.alloc([128, L], F32, "tb" + nm) for nm in ["pr", "pi", "nr", "ni", "cs", "sn", "mg", "t0"]}
        pc = A.alloc([128, 16], F32, "pc")
        w = {nm: A.alloc([128, L], F32, "w" + nm) for nm in ["a", "b", "a2", "b2", "zr", "zi", "cr", "ci"]}
        xb = [[A.alloc([128, L], BF16, "xb") for _ in range(2)] for _ in range(2)]
        cr = A.alloc([128, 8], F32, "carry")
        u_sem = S.new_dma_sem(f"su{l}")
        m_sem = S.new_dma_sem(f"sm{l}")
        g_sem = [S.new_dma_sem(f"sg{l}_{j}", sw=True) for j in range(2)]
        PI = math.pi
        nx = 0

        ti = A.alloc([128, L], mybir.dt.int32, "ti")
        tf = A.alloc([128, L], F32, "tf")
        tw = A.alloc([128, L], F32, "tw")

        def col(j):
            return pc[:, j:j + 1]

        def tiny(eng, fn, reads, writes):
            S.add(eng, fn, reads=reads, writes=writes)

        def frac_turns(out, r, ti_, tf_, rkey, okey, shift):
            if shift != 0.0:
                S.add(DVE, lambda e: e.tensor_scalar_add(out=tf_, in0=r, scalar1=shift),
                      reads=[rkey], writes=[("trn", "f")])
                src, skey = tf_, ("trn", "f")
            else:
                src, skey = r, rkey
            S.add(DVE, lambda e: e.tensor_copy(out=ti_, in_=src), reads=[skey], writes=[("trn", "i")])
            S.add(DVE, lambda e: e.tensor_copy(out=out, in_=ti_), reads=[("trn", "i")], writes=[okey])
            S.add(DVE, lambda e: e.tensor_tensor(out=out, in0=src, in1=out, op=ALU.subtract),
                  reads=[skey, okey], writes=[okey])

        def sin_turns(out, r, ti_, tf_, tw_, rkey, okey, shift):
            KO = ("trn", "o")
            KF = ("trn", "f")
            frac_turns(tw_, r, ti_, tf_, rkey, KO, shift)
            S.add(DVE, lambda e: e.tensor_single_scalar(out=tf_, in_=tw_, scalar=0.5, op=ALU.is_gt),
                  reads=[KO], writes=[KF])
            S.add(DVE, lambda e: e.tensor_tensor(out=tw_, in0=tw_, in1=tf_, op=ALU.subtract),
                  reads=[KO, KF], writes=[KO])
            S.add(DVE, lambda e: e.tensor_single_scalar(out=tf_, in_=tw_, scalar=-0.5, op=ALU.is_lt),
                  reads=[KO], writes=[KF])
            S.add(DVE, lambda e: e.tensor_tensor(out=tw_, in0=tw_, in1=tf_, op=ALU.add),
                  reads=[KO, KF], writes=[KO])
            S.add(ACT, lambda e: e.activation(out=out, in_=tw_, func=AF.Sin, scale=6.2831845),
                  reads=[KO], writes=[okey])

        for c in range(8):
            S.add(SP, lambda e, c=c: e.dma_start(out=u32[:, :], in_=U32[c * 128:(c + 1) * 128, :]),
                  writes=["u32"], dsem=u_sem)
            S.add(SP, lambda e, c=c: e.dma_start(out=mstage[:, :, :], in_=s5mats_in[i, c]),
                  writes=["mstage"], dsem=m_sem)
            S.add(ACT, lambda e: e.copy(out=ubf[:, :], in_=u32[:, :]), reads=["u32"], writes=["ubf"])
            S.add(DVE, lambda e: e.tensor_copy(out=mats[:, :, :], in_=mstage[:, :, :]),
                  reads=["mstage"], writes=["mats"])
            for pr in range(4):
                gp = c * 4 + pr
                s5c = abc + AB_S5 + 3 * gp
                lre, lim, lst = (cols[:, s5c + j:s5c + j + 1] for j in range(3))
                K = ("pc",)
                tiny(ACT, lambda e, lst=lst: e.activation(out=col(0), in_=lst, func=AF.Exp), ["cols"], [K])
                tiny(DVE, lambda e, lre=lre: e.tensor_tensor(out=col(1), in0=lre, in1=col(0), op=ALU.mult), [K, "cols"], [K])
                tiny(DVE, lambda e, lim=lim: e.tensor_tensor(out=col(2), in0=lim, in1=col(0), op=ALU.mult), [K, "cols"], [K])
                tiny(DVE, lambda e: e.tensor_scalar_mul(out=col(3), in0=col(1), scalar1=-1.0), [K], [K])
                tiny(ACT, lambda e: e.activation(out=col(4), in_=col(1), func=AF.Exp), [K], [K])
                tiny(DVE, lambda e: e.tensor_scalar_mul(out=col(12), in0=col(2), scalar1=1.0 / (2 * PI)), [K], [K])
                frac_turns(col(2), col(12), ti[:, 0:1], tf[:, 0:1], K, K, 0.0)
                sin_turns(col(6), col(2), ti[:, 0:1], tf[:, 0:1], tw[:, 0:1], K, K, 0.0)
                sin_turns(col(5), col(2), ti[:, 0:1], tf[:, 0:1], tw[:, 0:1], K, K, 0.25)
                tiny(DVE, lambda e: e.tensor_tensor(out=col(7), in0=col(4), in1=col(5), op=ALU.mult), [K], [K])
                tiny(DVE, lambda e: e.tensor_tensor(out=col(8), in0=col(4), in1=col(6), op=ALU.mult), [K], [K])
                tiny(DVE, lambda e: e.tensor_scalar_mul(out=col(15), in0=col(8), scalar1=-1.0), [K], [K])
                tiny(DVE, lambda e, lre=lre: e.tensor_tensor(out=col(9), in0=lre, in1=lre, op=ALU.mult), [K, "cols"], [K])
                tiny(DVE, lambda e, lim=lim: e.scalar_tensor_tensor(out=col(9), in0=lim, scalar=lim, in1=col(9),
                                                                    op0=ALU.mult, op1=ALU.add), [K, "cols"], [K])
                tiny(DVE, lambda e: e.reciprocal(out=col(9), in_=col(9)), [K], [K])
                tiny(DVE, lambda e: e.tensor_scalar_add(out=col(12), in0=col(7), scalar1=-1.0), [K], [K])
                tiny(DVE, lambda e, lre=lre: e.tensor_tensor(out=col(10), in0=col(12), in1=lre, op=ALU.mult), [K, "cols"], [K])
                tiny(DVE, lambda e, lim=lim: e.scalar_tensor_tensor(out=col(10), in0=col(8), scalar=lim, in1=col(10),
                                                                    op0=ALU.mult, op1=ALU.add), [K, "cols"], [K])
                tiny(DVE, lambda e: e.tensor_tensor(out=col(10), in0=col(10), in1=col(9), op=ALU.mult), [K], [K])
                tiny(DVE, lambda e, lim=lim: e.tensor_tensor(out=col(13), in0=col(12), in1=lim, op=ALU.mult), [K, "cols"], [K])
                tiny(DVE, lambda e, lre=lre: e.scalar_tensor_tensor(out=col(11), in0=col(8), scalar=lre, in1=col(13),
                                                                    op0=ALU.mult, op1=ALU.subtract), [K, "cols"], [K])
                tiny(DVE, lambda e: e.tensor_tensor(out=col(11), in0=col(11), in1=col(9), op=ALU.mult), [K], [K])
                tiny(DVE, lambda e: e.tensor_scalar_mul(out=col(14), in0=col(11), scalar1=-1.0), [K], [K])
                TK = ("tb",)
                tiny(DVE, lambda e: e.tensor_scalar(out=tb["t0"][:, :], in0=iota, scalar1=col(2), scalar2=None,
                                                    op0=ALU.mult), [K, "consts"], [TK])
                sin_turns(tb["sn"][:, :], tb["t0"][:, :], ti[:, :], tf[:, :], tw[:, :], TK, TK, 0.0)
                sin_turns(tb["cs"][:, :], tb["t0"][:, :], ti[:, :], tf[:, :], tw[:, :], TK, TK, 0.25)
                tiny(ACT, lambda e: e.activation(out=tb["mg"][:, :], in_=iota, func=AF.Exp, scale=col(1)),
                     [K, "consts"], [TK])
                tiny(DVE, lambda e: e.tensor_tensor(out=tb["pr"][:, :], in0=tb["mg"][:, :], in1=tb["cs"][:, :], op=ALU.mult), [TK], [TK])
                tiny(DVE, lambda e: e.tensor_tensor(out=tb["pi"][:, :], in0=tb["mg"][:, :], in1=tb["sn"][:, :], op=ALU.mult), [TK], [TK])
                tiny(ACT, lambda e: e.activation(out=tb["mg"][:, :], in_=iota, func=AF.Exp, scale=col(3)),
                     [K, "consts", TK], [TK])
                tiny(DVE, lambda e: e.tensor_scalar(out=tb["t0"][:, :], in0=tb["cs"][:, :], scalar1=col(10), scalar2=None,
                                                    op0=ALU.mult), [K, TK], [TK])
                tiny(DVE, lambda e: e.scalar_tensor_tensor(out=tb["t0"][:, :], in0=tb["sn"][:, :], scalar=col(11),
                                                           in1=tb["t0"][:, :], op0=ALU.mult, op1=ALU.add), [K, TK], [TK])
                tiny(DVE, lambda e: e.tensor_tensor(out=tb["nr"][:, :], in0=tb["t0"][:, :], in1=tb["mg"][:, :], op=ALU.mult), [TK], [TK])
                tiny(DVE, lambda e: e.tensor_scalar(out=tb["t0"][:, :], in0=tb["cs"][:, :], scalar1=col(11), scalar2=None,
                                                    op0=ALU.mult), [K, TK], [TK])
                tiny(DVE, lambda e: e.scalar_tensor_tensor(out=tb["t0"][:, :], in0=tb["sn"][:, :], scalar=col(10),
                                                           in1=tb["t0"][:, :], op0=ALU.mult, op1=ALU.subtract), [K, TK], [TK])
                tiny(DVE, lambda e: e.scalar_tensor_tensor(out=tb["ni"][:, :], in0=tb["t0"][:, :], scalar=-1.0,
                                                           in1=tb["mg"][:, :], op0=ALU.mult, op1=ALU.mult), [TK], [TK])
                tiny(DVE, lambda e: e.memset(cr[:, :], 0.0), [], ["carry"])
                yield
                for t in range(NT):
                    tsl = slice(t * L, (t + 1) * L)
                    S.add(PE, lambda e, pr=pr, tsl=tsl: e.matmul(ps[4][:, :], lhsT=mats[:, pr * 4 + 0, :], rhs=ubf[:, tsl],
                                                                  start=True, stop=True),
                          reads=["mats", "ubf"], writes=[("ps", 4)])
                    S.add(PE, lambda e, pr=pr, tsl=tsl: e.matmul(ps[5][:, :], lhsT=mats[:, pr * 4 + 1, :], rhs=ubf[:, tsl],
                                                                  start=True, stop=True),
                          reads=["mats", "ubf"], writes=[("ps", 5)])
                    R, I = ps[4], ps[5]
                    WK = ("w",)
                    S.add(DVE, lambda e: e.tensor_tensor(out=w["a"][:, :], in0=R[:, :], in1=tb["nr"][:, :], op=ALU.mult),
                          reads=[("ps", 4), TK], writes=[("w", "a")])
                    S.add(DVE, lambda e: e.tensor_tensor(out=w["b"][:, :], in0=I[:, :], in1=tb["ni"][:, :], op=ALU.mult),
                          reads=[("ps", 5), TK], writes=[("w", "b")])
                    S.add(DVE, lambda e: e.tensor_tensor(out=w["a2"][:, :], in0=I[:, :], in1=tb["nr"][:, :], op=ALU.mult),
                          reads=[("ps", 5), TK], writes=[("w", "a2")])
                    S.add(DVE, lambda e: e.tensor_tensor(out=w["b2"][:, :], in0=R[:, :], in1=tb["ni"][:, :], op=ALU.mult),
                          reads=[("ps", 4), TK], writes=[("w", "b2")])
                    S.add(DVE, lambda e: e.tensor_tensor(out=w["zr"][:, :], in0=w["a"][:, :], in1=w["b"][:, :], op=ALU.subtract),
                          reads=[("w", "a"), ("w", "b")], writes=[("w", "zr")])
                    S.add(DVE, lambda e: e.tensor_tensor(out=w["zi"][:, :], in0=w["a2"][:, :], in1=w["b2"][:, :], op=ALU.add),
                          reads=[("w", "a2"), ("w", "b2")], writes=[("w", "zi")])
                    CK = "carry"
                    tiny(DVE, lambda e: e.tensor_tensor(out=cr[:, 4:5], in0=cr[:, 1:2], in1=col(15), op=ALU.mult), [CK, K], [("carry", 4)])
                    tiny(DVE, lambda e: e.tensor_tensor(out=cr[:, 5:6], in0=cr[:, 1:2], in1=col(7), op=ALU.mult), [CK, K], [("carry", 5)])
                    tiny(DVE, lambda e: e.scalar_tensor_tensor(out=cr[:, 2:3], in0=cr[:, 0:1], scalar=col(7), in1=cr[:, 4:5],
                                                               op0=ALU.mult, op1=ALU.add), [CK, K, ("carry", 4)], [("carry", 2)])
                    tiny(DVE, lambda e: e.scalar_tensor_tensor(out=cr[:, 3:4], in0=cr[:, 0:1], scalar=col(8), in1=cr[:, 5:6],
                                                               op0=ALU.mult, op1=ALU.add), [CK, K, ("carry", 5)], [("carry", 3)])
                    S.add(DVE, lambda e: e.tensor_tensor_scan(out=w["cr"][:, :], data0=iota_ones(), data1=w["zr"][:, :],
                                                              initial=cr[:, 2:3], op0=ALU.mult, op1=ALU.add),
                          reads=[("w", "zr"), ("carry", 2), "onesL"], writes=[("w", "cr")])
                    S.add(DVE, lambda e: e.tensor_tensor_scan(out=w["ci"][:, :], data0=iota_ones(), data1=w["zi"][:, :],
                                                              initial=cr[:, 3:4], op0=ALU.mult, op1=ALU.add),
                          reads=[("w", "zi"), ("carry", 3), "onesL"], writes=[("w", "ci")])
                    sl = nx % 2
                    nx += 1
                    S.add(DVE, lambda e: e.tensor_tensor(out=w["a"][:, :], in0=w["cr"][:, :], in1=tb["pr"][:, :], op=ALU.mult),
                          reads=[("w", "cr"), TK], writes=[("w", "a")])
                    S.add(DVE, lambda e: e.tensor_tensor(out=w["b2"][:, :], in0=w["cr"][:, :], in1=tb["pi"][:, :], op=ALU.mult),
                          reads=[("w", "cr"), TK], writes=[("w", "b2")])
                    S.add(DVE, lambda e: e.tensor_tensor(out=w["b"][:, :], in0=w["ci"][:, :], in1=tb["pi"][:, :], op=ALU.mult),
                          reads=[("w", "ci"), TK], writes=[("w", "b")])
                    S.add(DVE, lambda e: e.tensor_tensor(out=w["a2"][:, :], in0=w["ci"][:, :], in1=tb["pr"][:, :], op=ALU.mult),
                          reads=[("w", "ci"), TK], writes=[("w", "a2")])
                    S.add(DVE, lambda e, sl=sl: e.tensor_tensor(out=xb[0][sl][:, :], in0=w["a"][:, :], in1=w["b"][:, :],
                                                               op=ALU.subtract),
                          reads=[("w", "a"), ("w", "b")], writes=[("xb", 0, sl)])
                    S.add(DVE, lambda e, sl=sl: e.scalar_tensor_tensor(out=xb[1][sl][:, :], in0=w["a2"][:, :], scalar=-1.0,
                                                                      in1=w["b2"][:, :], op0=ALU.mult, op1=ALU.subtract),
                          reads=[("w", "a2"), ("w", "b2")], writes=[("xb", 1, sl)])
                    tiny(DVE, lambda e: e.tensor_tensor(out=cr[:, 0:1], in0=w["a"][:, L - 1:L], in1=w["b"][:, L - 1:L],
                                                        op=ALU.subtract),
                         [("w", "a"), ("w", "b"), ("carry", 2), ("carry", 3), ("carry", 4), ("carry", 5)], [CK])
                    tiny(DVE, lambda e: e.tensor_tensor(out=cr[:, 1:2], in0=w["a2"][:, L - 1:L], in1=w["b2"][:, L - 1:L],
                                                        op=ALU.add), [("w", "a2"), ("w", "b2")], [CK])
                    pby = 6 + (nx % 2)
                    S.add(PE, lambda e, pr=pr, sl=sl, pby=pby: e.matmul(ps[pby][:, :], lhsT=mats[:, pr * 4 + 2, :],
                                                                        rhs=xb[0][sl][:, :], start=True, stop=False),
                          reads=["mats", ("xb", 0, sl)], writes=[("ps", pby)])
                    S.add(PE, lambda e, pr=pr, sl=sl, pby=pby: e.matmul(ps[pby][:, :], lhsT=mats[:, pr * 4 + 3, :],
                                                                        rhs=xb[1][sl][:, :], start=False, stop=True),
                          reads=["mats", ("xb", 1, sl)], writes=[("ps", pby)])
                    if pr == 0:
                        S.add(ACT, lambda e, tsl=tsl, pby=pby: e.copy(out=yacc[:, tsl], in_=ps[pby][:, :]),
                              reads=[("ps", pby)], writes=[("yacc", t)])
                    else:
                        S.add(DVE, lambda e, tsl=tsl, pby=pby: e.tensor_tensor(out=yacc[:, tsl], in0=yacc[:, tsl],
                                                                              in1=ps[pby][:, :], op=ALU.add),
                              reads=[("ps", pby), ("yacc", t)], writes=[("yacc", t)])
                    yield
            dcol = cols[:, abc + AB_D + c:abc + AB_D + c + 1]
            for t in range(NT):
                tsl = slice(t * L, (t + 1) * L)
                gs = t % 2
                yv = yacc[:, tsl]
                a1, a2 = gt1[gs], gt2[gs]
                S.add(DVE, lambda e, yv=yv, tsl=tsl, dcol=dcol: e.scalar_tensor_tensor(
                    out=yv, in0=u32[:, tsl], scalar=dcol, in1=yv, op0=ALU.mult, op1=ALU.add),
                    reads=[("yacc", t), "u32", "cols"], writes=[("yacc", t)])
                S.add(DVE, lambda e, yv=yv, a1=a1: e.tensor_tensor(out=a1[:, :], in0=yv, in1=yv, op=ALU.mult),
                      reads=[("yacc", t)], writes=[("gt1", gs)])
                S.add(DVE, lambda e, a1=a1: e.tensor_scalar(out=a1[:, :], in0=a1[:, :], scalar1=0.044715, scalar2=1.0,
                                                           op0=ALU.mult, op1=ALU.add),
                      reads=[("gt1", gs)], writes=[("gt1", gs)])
                S.add(DVE, lambda e, yv=yv, a1=a1: e.tensor_tensor(out=a1[:, :], in0=a1[:, :], in1=yv, op=ALU.mult),
                      reads=[("gt1", gs), ("yacc", t)], writes=[("gt1", gs)])
                S.add(ACT, lambda e, a1=a1: e.activation(out=a1[:, :], in_=a1[:, :], func=AF.Sigmoid,
                                                         scale=2.0 * 0.7978845608028654),
                      reads=[("gt1", gs)], writes=[("gt1", gs)])
                S.add(DVE, lambda e, yv=yv, a1=a1, a2=a2: e.tensor_tensor(out=a2[:, :], in0=a1[:, :], in1=yv, op=ALU.mult),
                      reads=[("gt1", gs), ("yacc", t)], writes=[("gt2", gs)])
                S.add(POOL, lambda e, a2=a2, tsl=tsl, c=c: e.dma_start(out=G32[c * 128:(c + 1) * 128, tsl], in_=a2[:, :]),
                      reads=[("gt2", gs)], dsem=g_sem[gs])
                yield

    onesL_t = A.alloc([128, TT], F32, "onesL")
    persist_end = A.off
    S.add(POOL, lambda e: e.memset(onesL_t[:, :], 1.0), writes=["onesL"])

    def iota_ones():
        return onesL_t[:, :]

    def outproj_phase(l, src, dst):
        i = l // 2
        A.reset(persist_end)
        abc = AB_COL0 + i * AB_NCOL
        xt = A.alloc([128, NCH, TT], F32, "xt")
        g32 = A.alloc([128, 8, TT], F32, "g32")
        mixT = A.alloc([128, NCH, TT], BF16, "mixT")
        gbf = A.alloc([128, 8, TT], BF16, "gbf")
        sgm = [A.alloc([128, TT], F32, "sgm") for _ in range(2)]
        ws = WStream(f"op{l}", NCH, 3)
        x_sem = S.new_dma_sem(f"ox{l}")
        xs_sem = S.new_dma_sem(f"oxs{l}", sw=True)
        g_sem = S.new_dma_sem(f"og{l}")
        a_sem = S.new_dma_sem(f"oa{l}")
        for t in range(NT):
            tsl = slice(t * TT, (t + 1) * TT)
            load_x(xt, src, t, x_sem)
            S.add(SP, lambda e, tsl=tsl: e.dma_start(out=g32[:, :, :], in_=cview(G32)[:, :, tsl]),
                  writes=["g32"], dsem=g_sem)
            S.add(SP, lambda e, tsl=tsl: e.dma_start(out=mixT[:, 0:8, :], in_=cview(ATT)[:, :, tsl]),
                  writes=["mixA"], dsem=a_sem)
            S.add(ACT, lambda e: e.copy(out=gbf[:, :, :], in_=g32[:, :, :]), reads=["g32"], writes=["gbf"])
            for oc in range(8):
                pb = oc % 2
                linear_block(ws, WGLU[i, oc], 8, lambda k: gbf[:, k, :], lambda k: ["gbf"], pb)
                S.add(ACT, lambda e, oc=oc, pb=pb: e.activation(
                    out=sgm[pb][:, :], in_=ps[pb][:, :], func=AF.Sigmoid,
                    bias=cols[:, abc + AB_BGLU + oc:abc + AB_BGLU + oc + 1]),
                    reads=[("ps", pb), "cols"], writes=[("sgm", pb)])
                S.add(DVE, lambda e, oc=oc, pb=pb: e.tensor_tensor(out=mixT[:, 8 + oc, :], in0=sgm[pb][:, :],
                                                                  in1=g32[:, oc, :], op=ALU.mult),
                      reads=[("sgm", pb), "g32"], writes=[("mixS", oc)])
            for oc in range(NCH):
                pb = 2 + oc % 2
                linear_block(ws, WOUT[i, oc], NCH, lambda k: mixT[:, k, :],
                             lambda k: ["mixA"] if k < 8 else [("mixS", k - 8)], pb)
                S.add(DVE, lambda e, oc=oc, pb=pb: e.tensor_tensor(
                    out=xt[:, oc, :], in0=xt[:, oc, :], in1=ps[pb][:, :], op=ALU.add),
                    reads=[("ps", pb), "xt"], writes=["xt"])
            store_x(xt, dst, t, xs_sem)
        S.barrier()

    prepass()
    cur = xT_in
    for l in range(n_layers):
        if l % 2 == 0:
            inproj_phase(l, cur)
            ab_mix_phase(l)
            outproj_phase(l, cur, XT)
        else:
            conformer_phase(l, cur, XT)
        cur = XT
        last = (l == n_layers - 1)
        ffn_phase(l, XT, outT if last else XT, final_norm=last)
    S.emit(final_waits=[st_out])
    return nc


_NC_CACHE = {}


def make_in_map(inputs, b, T=SEQ):
    x = np.asarray(inputs["x"], np.float32)
    m = {
        "xT": np.ascontiguousarray(x[b].T),
        "cols": pack_cols(inputs),
        "biasT": pack_bias(inputs),
        "consts": pack_consts(),
        "s5mats": pack_s5mats(inputs),
    }
    for nm in ["ffn_w_up", "ffn_w_down", "ab_w_in", "ab_w_out", "s5_w_glu", "conv_w_pw1", "conv_w_pw2"]:
        m[nm] = np.ascontiguousarray(np.asarray(inputs[nm], np.float32))
    return m


def kernel(**inputs):
    x = np.asarray(inputs["x"], np.float32)
    B = x.shape[0]
    if "full" not in _NC_CACHE:
        _NC_CACHE["full"] = build_program()
    nc = _NC_CACHE["full"]
    base = make_in_map(inputs, 0)
    in_maps = []
    for c in range(8):
        m = dict(base)
        m["xT"] = np.ascontiguousarray(x[c % B].T)
        in_maps.append(m)
    res = run_bass_kernel_spmd(nc, in_maps, core_ids=list(range(8)))
    out = np.stack([np.ascontiguousarray(res.results[b]["outT"].T) for b in range(B)], axis=0)
    return out.astype(np.float32)
```

```python
import math
import numpy as np
import concourse.bass as bass
import concourse.mybir as mybir
from concourse.bass_utils import run_bass_kernel_spmd

F32 = mybir.dt.float32
BF16 = mybir.dt.bfloat16
ALU = mybir.AluOpType
AF = mybir.ActivationFunctionType

PE, ACT, DVE, POOL, SP = "tensor", "scalar", "vector", "gpsimd", "sync"
ENGS = (PE, ACT, DVE, POOL, SP)

D = 2048
SEQ = 4096
DEPTH = 4
NCH = D // 128
TT = 512
FH = 5632
NFH = FH // 128
EPS = 1e-6
N_BUCKETS = 32
MAX_DISTANCE = 128
NH = 8
HD = 64
SBUF_LO = 24576
SBUF_HI = 229344


class DmaSem:
    def __init__(self, sem):
        self.sem = sem
        self.count = 0
        self.open_ops = []


class Op:
    __slots__ = ("eng", "fn", "deps", "need_inc", "dsem", "incval", "wait_val", "idx")

    def __init__(self, eng, fn, dsem):
        self.eng = eng
        self.fn = fn
        self.deps = []
        self.need_inc = False
        self.dsem = dsem
        self.incval = None
        self.wait_val = None


class Sched:
    def __init__(self, nc):
        self.nc = nc
        self.ops = {e: [] for e in ENGS}
        self.last_w = {}
        self.readers = {}
        self.barrier_deps = {e: [] for e in ENGS}
        self.dma_sems = []
        self.free_sems = []
        self.phase_sems = []
        self.n_ops = 0

    def new_dma_sem(self, name, persistent=False, sw=False):
        pool = [d for d in self.free_sems if d.sw == sw]
        if not persistent and pool:
            ds = pool[-1]
            self.free_sems.remove(ds)
        else:
            ds = DmaSem(self.nc.alloc_semaphore(name))
            self.dma_sems.append(ds)
        ds.sw = sw
        ds.persistent = persistent
        if not persistent:
            self.phase_sems.append(ds)
        return ds

    def add(self, eng, fn, reads=(), writes=(), dsem=None):
        op = Op(eng, fn, dsem)
        deps = []
        for k in reads:
            w = self.last_w.get(k)
            if w is not None:
                deps.append(w)
        for k in writes:
            w = self.last_w.get(k)
            if w is not None:
                deps.append(w)
            deps.extend(self.readers.get(k, ()))
        for k in reads:
            lst = self.readers.setdefault(k, [])
            if dsem is None:
                lst[:] = [r for r in lst if not (r.dsem is None and r.eng == eng)]
            lst.append(op)
        for k in writes:
            self.last_w[k] = op
            self.readers[k] = []
        if self.barrier_deps[eng]:
            deps.extend(self.barrier_deps[eng])
            self.barrier_deps[eng] = []
        seen = set()
        for d in deps:
            if d is op or id(d) in seen:
                continue
            seen.add(id(d))
            if d.eng == PE and eng == PE and d.dsem is None and dsem is None:
                continue
            d.need_inc = True
            op.deps.append(d)
        if dsem is not None:
            dsem.count += 16
            op.incval = dsem.count
            dsem.open_ops.append(op)
            op.need_inc = True
        self.ops[eng].append(op)
        self.n_ops += 1
        return op

    def close_group(self, dsem):
        for o in dsem.open_ops:
            o.wait_val = dsem.count
        dsem.open_ops = []

    def barrier(self):
        lasts = []
        for e in ENGS:
            for o in reversed(self.ops[e]):
                if o.dsem is None:
                    lasts.append(o)
                    break
        for ds in self.dma_sems:
            ds.open_ops = []
            if ds.count:
                po = Op(None, None, ds)
                po.wait_val = ds.count
                lasts.append(po)
        for e in ENGS:
            self.barrier_deps[e] = list(lasts)
        self.last_w = {}
        self.readers = {}
        self.free_sems.extend(self.phase_sems)
        self.phase_sems = []

    def emit(self, final_waits=()):
        nc = self.nc
        esem = {e: nc.alloc_semaphore("prog_" + e) for e in ENGS}
        for e in ENGS:
            c = 0
            for o in self.ops[e]:
                if o.dsem is None and o.need_inc:
                    c += 1
                    o.incval = c
        ops = self.ops

        def stream(e):
            def body(eng):
                waited = {}
                for o in ops[e]:
                    for d in o.deps:
                        if d.dsem is not None:
                            sem = d.dsem.sem
                            val = d.wait_val if d.wait_val is not None else d.incval
                        else:
                            sem = esem[d.eng]
                            val = d.incval
                        key = id(sem)
                        if waited.get(key, 0) >= val:
                            continue
                        waited[key] = val
                        eng.wait_ge(sem, val)
                    ins = o.fn(eng)
                    if o.dsem is not None:
                        ins.then_inc(o.dsem.sem, 16)
                    elif o.need_inc:
                        ins.then_inc(esem[e], 1)
                if e == SP:
                    for ds in final_waits:
                        eng.wait_ge(ds.sem, ds.count)
            return body

        with nc.Block() as block:
            block.tensor(stream(PE))
            block.scalar(stream(ACT))
            block.vector(stream(DVE))
            block.gpsimd(stream(POOL))
            block.sync(stream(SP))


class Arena:
    def __init__(self, nc):
        self.nc = nc
        self.off = SBUF_LO
        self.n = 0

    def reset(self, to=SBUF_LO):
        self.off = to

    def alloc(self, shape, dtype, name="t"):
        size = int(np.prod(shape[1:])) * (2 if dtype == BF16 else 4)
        size = (size + 63) // 64 * 64
        assert self.off + size <= SBUF_HI, f"SBUF overflow {name} {self.off} {size}"
        self.n += 1
        t = self.nc.alloc_sbuf_tensor_at(f"{name}{self.n}", list(shape), dtype, offset=self.off)
        self.off += size
        return t


ATT_W = 1024
SSM_W = 1024
CONV_K = 31
NPAIR = 32
NEG = -30000.0
LAM_INIT = [0.8 - 0.6 * math.exp(-0.3 * l) for l in range(DEPTH)]

NORM_MIX_COL0 = 0
NORM_FFN_COL0 = NORM_MIX_COL0 + DEPTH * NCH
NORM_FINAL_COL0 = NORM_FFN_COL0 + DEPTH * NCH
FFN_COL0 = NORM_FINAL_COL0 + NCH
FFN_NCOL = 2 * NFH * 4
CF_COL0 = FFN_COL0 + DEPTH * FFN_NCOL
CF_NCOL = NCH * 32 + 2 * NCH
AB_COL0 = CF_COL0 + 2 * CF_NCOL
AB_BGLU, AB_D, AB_HN, AB_LAM, AB_S5 = 0, 8, 16, 17, 17 + 256
AB_NCOL = 17 + 256 + 3 * NPAIR
CH_COL0 = AB_COL0 + 2 * AB_NCOL
NCOLS = CH_COL0 + NH


def t5_bucket_np(rel):
    n = np.maximum(rel, 0)
    max_exact = N_BUCKETS // 2
    nf = np.maximum(n, 1).astype(np.float32)
    large = max_exact + (np.log(nf / np.float32(max_exact)) / np.float32(math.log(MAX_DISTANCE / max_exact))
                         * np.float32(N_BUCKETS - max_exact)).astype(np.int32)
    large = np.minimum(large, N_BUCKETS - 1)
    return np.where(n < max_exact, n, large)


def pack_cols(inp):
    cols = np.zeros((128, NCOLS), np.float32)

    def put(c0, vec):
        v = np.asarray(vec, np.float32).reshape(-1, 128).T
        cols[:, c0:c0 + v.shape[1]] = v

    for l in range(DEPTH):
        put(NORM_MIX_COL0 + l * NCH, inp["norm_mix"][l])
        put(NORM_FFN_COL0 + l * NCH, inp["norm_ffn"][l])
        w = np.asarray(inp["ffn_w_dw"][l], np.float32)
        b = np.asarray(inp["ffn_b_dw"][l], np.float32)
        blk = np.stack([w[0], w[1], w[2], b], axis=-1)
        blk = blk.reshape(2 * NFH, 128, 4).transpose(1, 0, 2).reshape(128, 2 * NFH * 4)
        cols[:, FFN_COL0 + l * FFN_NCOL: FFN_COL0 + (l + 1) * FFN_NCOL] = blk
    put(NORM_FINAL_COL0, inp["norm_final"])
    for i in range(2):
        c0 = CF_COL0 + i * CF_NCOL
        w = np.asarray(inp["conv_w_dw"][i], np.float32)
        b = np.asarray(inp["conv_b_dw"][i], np.float32)
        blk = np.concatenate([w.T, b[:, None]], axis=1)
        blk = blk.reshape(NCH, 128, 32).transpose(1, 0, 2).reshape(128, NCH * 32)
        cols[:, c0:c0 + NCH * 32] = blk
        put(c0 + NCH * 32, inp["conv_ln_g"][i])
        put(c0 + NCH * 32 + NCH, inp["conv_ln_b"][i])
    for i in range(2):
        c0 = AB_COL0 + i * AB_NCOL
        put(c0 + AB_BGLU, inp["s5_b_glu"][i])
        put(c0 + AB_D, np.asarray(inp["s5_d"][i]).reshape(-1))
        cols[:, c0 + AB_HN] = np.asarray(inp["diff_head_norm"][i], np.float32)
        for q, nm in enumerate(["diff_lq1", "diff_lk1", "diff_lq2", "diff_lk2"]):
            cols[:, c0 + AB_LAM + q * 64: c0 + AB_LAM + (q + 1) * 64] = np.asarray(inp[nm][i], np.float32)[None, :]
        lre = np.asarray(inp["s5_lambda_re"][i], np.float32).reshape(NPAIR, 128)
        lim = np.asarray(inp["s5_lambda_im"][i], np.float32).reshape(NPAIR, 128)
        lst = np.repeat(np.asarray(inp["s5_log_step"][i], np.float32), 64).reshape(NPAIR, 128)
        for gp in range(NPAIR):
            cols[:, c0 + AB_S5 + 3 * gp + 0] = lre[gp]
            cols[:, c0 + AB_S5 + 3 * gp + 1] = lim[gp]
            cols[:, c0 + AB_S5 + 3 * gp + 2] = lst[gp]
    cols[:, CH_COL0:CH_COL0 + NH] = np.asarray(inp["rel_bias"], np.float32)[N_BUCKETS - 1][None, :]
    return cols


def pack_bias(inp):
    rb = np.asarray(inp["rel_bias"], np.float32)
    k = np.arange(128)[:, None]
    q = np.arange(128)[None, :]
    b0 = t5_bucket_np(q - k)
    b1 = t5_bucket_np(128 + q - k)
    out = np.zeros((128, NH, 2, 128), np.float32)
    for h in range(NH):
        out[:, h, 0, :] = rb[b0, h]
        out[:, h, 1, :] = rb[b1, h]
    return out


def pack_consts():
    c = np.zeros((128, 128 + 512), np.float32)
    k = np.arange(128)[:, None]
    q = np.arange(128)[None, :]
    c[:, 0:128] = np.where(q >= k, 0.0, NEG)
    c[:, 128:] = np.arange(512, dtype=np.float32)[None, :]
    return c


def pack_s5mats(inp):
    out = np.zeros((2, 8, 128, 16, 128), np.float32)
    for i in range(2):
        B = [np.asarray(inp["s5_b_re"][i], np.float32), np.asarray(inp["s5_b_im"][i], np.float32)]
        C = [np.asarray(inp["s5_c_re"][i], np.float32), np.asarray(inp["s5_c_im"][i], np.float32)]
        for c in range(8):
            for pr in range(4):
                for g2 in range(2):
                    g = (c * 4 + pr) * 2 + g2
                    ch0 = pr * 32 + g2 * 16
                    for m in range(2):
                        out[i, c, ch0:ch0 + 16, pr * 4 + m, g2 * 64:(g2 + 1) * 64] = B[m][g].T
                        out[i, c, g2 * 64:(g2 + 1) * 64, pr * 4 + 2 + m, ch0:ch0 + 16] = C[m][g].T
    return out


def build_program(T=SEQ, n_layers=DEPTH):
    nc = bass.Bass("TRN2", target_bir_lowering=False)
    S = Sched(nc)
    A = Arena(nc)
    NT = T // TT
    NKB = T // 128
    n_ab = (n_layers + 1) // 2
    n_c = n_layers // 2

    def din(name, shape):
        return nc.dram_tensor(name, list(shape), F32, kind="ExternalInput").ap()

    def dscr(name, shape, dt):
        return nc.dram_tensor(name, list(shape), dt).ap()

    xT_in = din("xT", (D, T))
    outT = nc.dram_tensor("outT", [D, T], F32, kind="ExternalOutput").ap()
    cols_in = din("cols", (128, NCOLS))
    bias_in = din("biasT", (128, NH, 2, 128))
    consts_in = din("consts", (128, 640))
    s5mats_in = din("s5mats", (2, 8, 128, 16, 128))
    ffn_w_up = din("ffn_w_up", (DEPTH, D, 2 * FH))
    ffn_w_down = din("ffn_w_down", (DEPTH, FH, D))
    ab_w_in = din("ab_w_in", (2, D, 4096))
    ab_w_out = din("ab_w_out", (2, D, D))
    s5_w_glu = din("s5_w_glu", (2, SSM_W, SSM_W))
    conv_w_pw1 = din("conv_w_pw1", (2, D, 2 * D))
    conv_w_pw2 = din("conv_w_pw2", (2, D, D))

    XT = dscr("XT", [D, T], F32)
    WUP = dscr("WUP", [DEPTH, NFH, 128, 2, NCH, 128], BF16)
    WDN = dscr("WDN", [DEPTH, NCH, 128, NFH, 128], BF16)
    WIN = dscr("WIN", [2, 32, 128, NCH, 128], BF16)
    WV = dscr("WV", [2, 2, 128, NCH, 512], BF16)
    WOUT = dscr("WOUT", [2, NCH, 128, NCH, 128], BF16)
    WGLU = dscr("WGLU", [2, 8, 128, 8, 128], BF16)
    WP1 = dscr("WP1", [2, 32, 128, NCH, 128], BF16)
    WP2 = dscr("WP2", [2, NCH, 128, NCH, 128], BF16)
    QT = dscr("QT", [ATT_W, T], BF16)
    KT = dscr("KT", [ATT_W, T], BF16)
    VTM = dscr("VTM", [T, ATT_W], BF16)
    U32 = dscr("U32", [SSM_W, T], F32)
    ATT = dscr("ATT", [ATT_W, T], BF16)
    G32 = dscr("G32", [SSM_W, T], F32)

    ps = [nc.alloc_psum_tensor(f"ps{i}", [128, 512], F32) for i in range(8)]

    cols = A.alloc([128, NCOLS], F32, "cols")
    ones_bf = A.alloc([128, 128], BF16, "onesD")
    ones1 = A.alloc([128, 128], BF16, "ones1")
    ones128 = A.alloc([128, 128], BF16, "ones128")
    onesf = A.alloc([128, 128], F32, "onesf")
    cst = A.alloc([128, 8], F32, "cst")
    consts = A.alloc([128, 640], F32, "consts")
    persist_end = A.off

    ld_const = S.new_dma_sem("ld_const", persistent=True)
    S.add(SP, lambda e: e.dma_start(out=cols[:, :], in_=cols_in), writes=["cols"], dsem=ld_const)
    ld_const2 = S.new_dma_sem("ld_const2", persistent=True)
    S.add(SP, lambda e: e.dma_start(out=consts[:, :], in_=consts_in), writes=["consts"], dsem=ld_const2)
    S.add(POOL, lambda e: e.memset(ones_bf[:, :], 1.0 / D), writes=["ones"])
    S.add(POOL, lambda e: e.memset(ones1[:, :], 1.0), writes=["ones1"])
    S.add(POOL, lambda e: e.memset(ones128[:, :], 1.0 / 128), writes=["ones128"])
    S.add(POOL, lambda e: e.memset(onesf[:, :], 1.0 / D), writes=["onesf"])
    S.add(POOL, lambda e: e.memset(cst[:, 0:1], EPS), writes=["cst0"])
    S.add(POOL, lambda e: e.memset(cst[:, 1:2], -math.pi), writes=["cst1"])
    epsc = cst
    maskT = consts[:, 0:128]
    iota = consts[:, 128:640]

    st_out = S.new_dma_sem("st_out", persistent=True, sw=True)

    def cview(ap2d):
        return ap2d.rearrange("(c p) t -> p c t", p=128)

    def prepass():
        A.reset(persist_end)
        NS = 3
        STG = 8192
        stage = [A.alloc([128, STG], F32, "stg") for _ in range(NS)]
        stb = [A.alloc([128, STG], BF16, "stb") for _ in range(NS)]
        lsem = [S.new_dma_sem(f"pp_ld{i}") for i in range(NS)]
        ssem = [S.new_dma_sem(f"pp_st{i}", sw=True) for i in range(NS)]
        cnt = [0]
        cast_engs = [DVE, ACT]

        def convert(src_ap, dst_ap, kc, nb, permute=True):
            i = cnt[0] % NS
            ce = cast_engs[cnt[0] % len(cast_engs)]
            cnt[0] += 1
            n = kc * nb * 128
            sv = stage[i][:, 0:n].rearrange("p (k m) -> p k m", k=kc)
            S.add(SP, lambda e: e.dma_start(out=sv, in_=src_ap), writes=[("stg", i)], dsem=lsem[i])
            if permute:
                cin = stage[i][:, 0:n].rearrange("p (k n b) -> p n k b", k=kc, n=nb)
                cout = stb[i][:, 0:n].rearrange("p (n k b) -> p n k b", n=nb, k=kc)
                bv = stb[i][:, 0:n].rearrange("p (n m) -> p n m", n=nb)
            else:
                cin = stage[i][:, 0:n]
                cout = stb[i][:, 0:n]
                bv = stb[i][:, 0:n]
            if ce == ACT:
                S.add(ACT, lambda e: e.copy(out=cout, in_=cin), reads=[("stg", i)], writes=[("stb", i)])
            else:
                S.add(ce, lambda e: e.tensor_copy(out=cout, in_=cin), reads=[("stg", i)], writes=[("stb", i)])
            S.add(POOL, lambda e: e.dma_start(out=dst_ap, in_=bv), reads=[("stb", i)], dsem=ssem[i])

        def conv_w(src2d, dst, K, N, blocks=None):
            kc = K // 128
            nb = 8192 // (kc * 128)
            sv = src2d.rearrange("(k p) n -> p k n", p=128)
            for oc0 in range(0, N // 128, nb):
                if blocks is not None and oc0 not in blocks:
                    continue
                convert(sv[:, :, oc0 * 128:(oc0 + nb) * 128],
                        dst[oc0:oc0 + nb].rearrange("n p k b -> p n (k b)"), kc, nb)

        for l in range(n_layers):
            i = l // 2
            if l % 2 == 0:
                conv_w(ab_w_in[i], WIN[i], D, 4096, blocks=[0, 4, 8, 12, 24, 28])
                sv = ab_w_in[i].rearrange("(k p) n -> p k n", p=128)
                for hf in range(2):
                    convert(sv[:, :, 2048 + hf * 512: 2048 + (hf + 1) * 512],
                            WV[i, hf].rearrange("p k n -> p (k n)"), NCH, 4, permute=False)
                conv_w(ab_w_out[i], WOUT[i], D, D)
                conv_w(s5_w_glu[i], WGLU[i], SSM_W, SSM_W)
            else:
                conv_w(conv_w_pw1[i], WP1[i], D, 2 * D)
                conv_w(conv_w_pw2[i], WP2[i], D, D)
            wu_src = ffn_w_up[l].rearrange("(k p) n -> p k n", p=128)
            for path in range(2):
                for j0 in range(0, NFH, 4):
                    c0 = path * FH + j0 * 128
                    convert(wu_src[:, :, c0:c0 + 512],
                            WUP[l, j0:j0 + 4, :, path].rearrange("n p k b -> p n (k b)"), NCH, 4)
            wd_src = ffn_w_down[l].rearrange("(k p) n -> p k n", p=128)
            for oc0 in range(0, NCH, 2):
                for k0 in range(0, NFH, 22):
                    convert(wd_src[:, k0:k0 + 22, oc0 * 128:oc0 * 128 + 256],
                            WDN[l, oc0:oc0 + 2, :, k0:k0 + 22, :].rearrange("n p k b -> p n (k b)"), 22, 2)
        S.barrier()

    def rsqrt_ps(rstd, psb, key="rstd"):
        S.add(ACT, lambda e: e.activation(out=rstd[:, :], in_=ps[psb][:, :], func=AF.Sqrt, bias=epsc[:, 0:1]),
              reads=[("ps", psb), "cst0"], writes=[key])
        S.add(DVE, lambda e: e.reciprocal(out=rstd[:, :], in_=rstd[:, :]), reads=[key], writes=[key])

    def rmsnorm_tile(xt, hT, gcol0, psb, rstd, sq, out_key="hT"):
        for c in range(NCH):
            S.add(ACT, lambda e, c=c: e.activation(out=sq[c % 2][:, :], in_=xt[:, c, :], func=AF.Square),
                  reads=["xt"], writes=[("sq", c % 2)])
            S.add(PE, lambda e, c=c: e.matmul(ps[psb][:, :], lhsT=ones_bf[:, :], rhs=sq[c % 2][:, :],
                                              start=(c == 0), stop=(c == NCH - 1)),
                  reads=[("sq", c % 2), "ones"], writes=[("ps", psb)])
        rsqrt_ps(rstd, psb)
        for c in range(NCH):
            S.add(DVE, lambda e, c=c: e.scalar_tensor_tensor(
                out=hT[:, c, :], in0=xt[:, c, :], scalar=cols[:, gcol0 + c:gcol0 + c + 1], in1=rstd[:, :],
                op0=ALU.mult, op1=ALU.mult),
                reads=["xt", "rstd", "cols"], writes=[out_key])

    class WStream:
        def __init__(self, tag, kc_max, n=3):
            self.slots = [A.alloc([128, kc_max, 128], BF16, "wl") for _ in range(n)]
            self.sems = [S.new_dma_sem(f"wl_{tag}_{i}") for i in range(n)]
            self.n = n
            self.i = 0
            self.tag = tag

        def load(self, src_blk, kc):
            s = self.i % self.n
            self.i += 1
            sl = self.slots[s]
            S.add(SP, lambda e: e.dma_start(out=sl[:, 0:kc, :], in_=src_blk),
                  writes=[("wl", self.tag, s)], dsem=self.sems[s])
            return sl, ("wl", self.tag, s)

    def linear_block(ws, wsrc_blk, kc, rhs_fn, rhs_keys, pb):
        sl, key = ws.load(wsrc_blk, kc)
        for k in range(kc):
            S.add(PE, lambda e, k=k: e.matmul(ps[pb][:, :], lhsT=sl[:, k, :], rhs=rhs_fn(k),
                                              start=(k == 0), stop=(k == kc - 1)),
                  reads=[key] + list(rhs_keys(k)), writes=[("ps", pb)])

    def load_x(xt, src, t, sem):
        tsl = slice(t * TT, (t + 1) * TT)
        S.add(SP, lambda e: e.dma_start(out=xt[:, :, :], in_=cview(src)[:, :, tsl]),
              writes=["xt"], dsem=sem)

    def store_x(xt, dst, t, sem):
        tsl = slice(t * TT, (t + 1) * TT)
        S.add(POOL, lambda e: e.dma_start(out=cview(dst)[:, :, tsl], in_=xt[:, :, :]),
              reads=["xt"], dsem=sem)

    def ffn_phase(l, src, dst, final_norm=False):
        A.reset(persist_end)
        xt = A.alloc([128, NCH, TT], F32, "xt")
        hT = A.alloc([128, NCH, TT], BF16, "hT")
        hid = A.alloc([128, NFH, TT], BF16, "hid")
        NWU, NWD = 3, 2
        wu = [A.alloc([128, 2, NCH, 128], BF16, "wu") for _ in range(NWU)]
        wd = [A.alloc([128, NFH, 128], BF16, "wd") for _ in range(NWD)]
        wu_sem = [S.new_dma_sem(f"wu{l}_{i}") for i in range(NWU)]
        wd_sem = [S.new_dma_sem(f"wd{l}_{i}") for i in range(NWD)]
        x_sem = S.new_dma_sem(f"fx{l}")
        xs_sem = st_out if dst is outT else S.new_dma_sem(f"fxs{l}", sw=True)
        gb = [[A.alloc([128, TT + 2], F32, "gb") for _ in range(2)] for _ in range(2)]
        acc = [[A.alloc([128, TT], F32, "acc") for _ in range(2)] for _ in range(2)]
        sg = [A.alloc([128, TT], F32, "sg") for _ in range(2)]
        halo = A.alloc([128, 2 * NFH, 2], F32, "halo")
        rstd = A.alloc([128, TT], F32, "rstd")
        sq = [A.alloc([128, TT], BF16, "sq") for _ in range(2)]
        cw = FFN_COL0 + l * FFN_NCOL

        S.add(POOL, lambda e: e.memset(halo[:, :, :], 0.0), writes=["halo"])
        nwu = nwd = 0
        it = 0
        for t in range(NT):
            load_x(xt, src, t, x_sem)
            rmsnorm_tile(xt, hT, NORM_FFN_COL0 + l * NCH, 7, rstd, sq)
            for j in range(NFH):
                ws = nwu % NWU
                nwu += 1
                S.add(SP, lambda e, ws=ws, j=j: e.dma_start(out=wu[ws][:, :, :, :], in_=WUP[l, j]),
                      writes=[("wu", ws)], dsem=wu_sem[ws])
                for path in range(2):
                    sl = it % 2
                    pb = (it % 2) * 2 + path
                    ch = path * NFH + j
                    for k in range(NCH):
                        S.add(PE, lambda e, ws=ws, k=k, path=path, pb=pb: e.matmul(
                            ps[pb][:, :], lhsT=wu[ws][:, path, k, :], rhs=hT[:, k, :],
                            start=(k == 0), stop=(k == NCH - 1)),
                            reads=[("wu", ws), "hT"], writes=[("ps", pb)])
                    g = gb[path][sl]
                    a = acc[path][sl]
                    S.add(ACT, lambda e, g=g, pb=pb: e.copy(out=g[:, 2:TT + 2], in_=ps[pb][:, :]),
                          reads=[("ps", pb)], writes=[("gb", path, sl)])
                    S.add(POOL, lambda e, g=g, ch=ch: e.tensor_copy(out=g[:, 0:2], in_=halo[:, ch, :]),
                          reads=["halo"], writes=[("gbh", path, sl)])
                    wc = cw + ch * 4
                    S.add(DVE, lambda e, g=g, a=a, wc=wc: e.tensor_scalar(
                        out=a[:, :], in0=g[:, 2:TT + 2], scalar1=cols[:, wc + 2:wc + 3],
                        scalar2=cols[:, wc + 3:wc + 4], op0=ALU.mult, op1=ALU.add),
                        reads=[("gb", path, sl), "cols"], writes=[("acc", path, sl)])
                    S.add(DVE, lambda e, g=g, a=a, wc=wc: e.scalar_tensor_tensor(
                        out=a[:, :], in0=g[:, 1:TT + 1], scalar=cols[:, wc + 1:wc + 2], in1=a[:, :],
                        op0=ALU.mult, op1=ALU.add),
                        reads=[("gb", path, sl), ("gbh", path, sl), ("acc", path, sl)], writes=[("acc", path, sl)])
                    S.add(DVE, lambda e, g=g, a=a, wc=wc: e.scalar_tensor_tensor(
                        out=a[:, :], in0=g[:, 0:TT], scalar=cols[:, wc:wc + 1], in1=a[:, :],
                        op0=ALU.mult, op1=ALU.add),
                        reads=[("gb", path, sl), ("gbh", path, sl), ("acc", path, sl)], writes=[("acc", path, sl)])
                    S.add(POOL, lambda e, g=g, ch=ch: e.tensor_copy(out=halo[:, ch, :], in_=g[:, TT:TT + 2]),
                          reads=[("gb", path, sl)], writes=["halo"])
                sl = it % 2
                S.add(ACT, lambda e, sl=sl: e.activation(out=sg[sl][:, :], in_=acc[0][sl][:, :], func=AF.Silu),
                      reads=[("acc", 0, sl)], writes=[("sg", sl)])
                S.add(DVE, lambda e, sl=sl, j=j: e.tensor_tensor(
                    out=hid[:, j, :], in0=sg[sl][:, :], in1=acc[1][sl][:, :], op=ALU.mult),
                    reads=[("sg", sl), ("acc", 1, sl)], writes=[("hid", j)])
                it += 1
            for oc in range(NCH):
                ws = nwd % NWD
                nwd += 1
                pb = 4 + (oc % 2)
                S.add(SP, lambda e, ws=ws, oc=oc: e.dma_start(out=wd[ws][:, :, :], in_=WDN[l, oc]),
                      writes=[("wd", ws)], dsem=wd_sem[ws])
                for k in range(NFH):
                    S.add(PE, lambda e, ws=ws, k=k, pb=pb: e.matmul(
                        ps[pb][:, :], lhsT=wd[ws][:, k, :], rhs=hid[:, k, :],
                        start=(k == 0), stop=(k == NFH - 1)),
                        reads=[("wd", ws), ("hid", k)], writes=[("ps", pb)])
                S.add(DVE, lambda e, oc=oc, pb=pb: e.tensor_tensor(
                    out=xt[:, oc, :], in0=xt[:, oc, :], in1=ps[pb][:, :], op=ALU.add),
                    reads=[("ps", pb), "xt"], writes=["xt"])
            if final_norm:
                for c in range(NCH):
                    S.add(ACT, lambda e, c=c: e.activation(out=sq[c % 2][:, :], in_=xt[:, c, :], func=AF.Square),
                          reads=["xt"], writes=[("sq", c % 2)])
                    S.add(PE, lambda e, c=c: e.matmul(ps[6][:, :], lhsT=ones_bf[:, :], rhs=sq[c % 2][:, :],
                                                      start=(c == 0), stop=(c == NCH - 1)),
                          reads=[("sq", c % 2), "ones"], writes=[("ps", 6)])
                rsqrt_ps(rstd, 6)
                for c in range(NCH):
                    S.add(DVE, lambda e, c=c: e.scalar_tensor_tensor(
                        out=xt[:, c, :], in0=xt[:, c, :],
                        scalar=cols[:, NORM_FINAL_COL0 + c:NORM_FINAL_COL0 + c + 1],
                        in1=rstd[:, :], op0=ALU.mult, op1=ALU.mult),
                        reads=["xt", "rstd", "cols"], writes=["xt"])
            store_x(xt, dst, t, xs_sem)
        S.barrier()

    def conformer_phase(l, src, dst):
        i = l // 2
        A.reset(persist_end)
        xt = A.alloc([128, NCH, TT], F32, "xt")
        hT = A.alloc([128, NCH, TT], BF16, "hT")
        zb = A.alloc([128, NCH, TT + 30], F32, "zb")
        zc = A.alloc([128, NCH, TT], F32, "zc")
        sgm = [A.alloc([128, TT], F32, "sgm") for _ in range(2)]
        sqf = [A.alloc([128, TT], F32, "sqf") for _ in range(2)]
        sq = [A.alloc([128, TT], BF16, "sq") for _ in range(2)]
        rstd = A.alloc([128, TT], F32, "rstd")
        mean = A.alloc([128, TT], F32, "mean")
        lnr = A.alloc([128, TT], F32, "lnr")
        ut = [A.alloc([128, TT], F32, "ut") for _ in range(2)]
        ws = WStream(f"cf{l}", NCH, 3)
        x_sem = S.new_dma_sem(f"cx{l}")
        xs_sem = S.new_dma_sem(f"cxs{l}", sw=True)
        c0 = CF_COL0 + i * CF_NCOL

        S.add(POOL, lambda e: e.memset(zb[:, :, 0:30], 0.0), writes=["zbh"])
        for t in range(NT):
            load_x(xt, src, t, x_sem)
            rmsnorm_tile(xt, hT, NORM_MIX_COL0 + l * NCH, 7, rstd, sq)
            G = 4

            def pw1_group(g0):
                for c in range(g0, g0 + G):
                    pa, pg = (c % 2) * 2, (c % 2) * 2 + 1
                    linear_block(ws, WP1[i, c], NCH, lambda k: hT[:, k, :], lambda k: ["hT"], pa)
                    linear_block(ws, WP1[i, NCH + c], NCH, lambda k: hT[:, k, :], lambda k: ["hT"], pg)
                    sl = c % 2
                    S.add(ACT, lambda e, sl=sl, pg=pg: e.activation(out=sgm[sl][:, :], in_=ps[pg][:, :],
                                                                    func=AF.Sigmoid),
                          reads=[("ps", pg)], writes=[("sgm", sl)])
                    S.add(DVE, lambda e, sl=sl, pa=pa, c=c: e.tensor_tensor(
                        out=zb[:, c, 30:30 + TT], in0=ps[pa][:, :], in1=sgm[sl][:, :], op=ALU.mult),
                        reads=[("ps", pa), ("sgm", sl)], writes=[("zb", c)])

            def conv_group(g0):
                for k in range(CONV_K):
                    for c in range(g0, g0 + G):
                        wc = c0 + c * 32
                        if k == 0:
                            S.add(DVE, lambda e, c=c, wc=wc: e.tensor_scalar(
                                out=zc[:, c, :], in0=zb[:, c, 0:TT], scalar1=cols[:, wc:wc + 1],
                                scalar2=cols[:, wc + 31:wc + 32], op0=ALU.mult, op1=ALU.add),
                                reads=[("zb", c), "zbh", "cols"], writes=[("zc", c)])
                        else:
                            S.add(DVE, lambda e, c=c, wc=wc, k=k: e.scalar_tensor_tensor(
                                out=zc[:, c, :], in0=zb[:, c, k:k + TT], scalar=cols[:, wc + k:wc + k + 1],
                                in1=zc[:, c, :], op0=ALU.mult, op1=ALU.add),
                                reads=[("zb", c), "zbh", ("zc", c)], writes=[("zc", c)])

            def stats_group(g0):
                for c in range(g0, g0 + G):
                    sl = c % 2
                    S.add(ACT, lambda e, c=c, sl=sl: e.activation(out=sqf[sl][:, :], in_=zc[:, c, :], func=AF.Square),
                          reads=[("zc", c)], writes=[("sqf", sl)])
                    S.add(PE, lambda e, c=c: e.matmul(ps[4][:, :], lhsT=onesf[:, :], rhs=zc[:, c, :],
                                                      start=(c == 0), stop=(c == NCH - 1)),
                          reads=[("zc", c), "onesf"], writes=[("ps", 4)])
                    S.add(PE, lambda e, c=c, sl=sl: e.matmul(ps[5][:, :], lhsT=onesf[:, :], rhs=sqf[sl][:, :],
                                                             start=(c == 0), stop=(c == NCH - 1)),
                          reads=[("sqf", sl), "onesf"], writes=[("ps", 5)])

            pw1_group(0)
            for g0 in range(0, NCH, G):
                conv_group(g0)
                if g0 + G < NCH:
                    pw1_group(g0 + G)
                stats_group(g0)
            S.add(POOL, lambda e: e.tensor_copy(out=zb[:, :, 0:30], in_=zb[:, :, TT:TT + 30]),
                  reads=[("zb", c) for c in range(NCH)], writes=["zbh"])
            S.add(ACT, lambda e: e.copy(out=mean[:, :], in_=ps[4][:, :]), reads=[("ps", 4)], writes=["mean"])
            S.add(DVE, lambda e: e.tensor_tensor(out=lnr[:, :], in0=mean[:, :], in1=mean[:, :], op=ALU.mult),
                  reads=["mean"], writes=["lnr"])
            S.add(DVE, lambda e: e.tensor_tensor(out=lnr[:, :], in0=ps[5][:, :], in1=lnr[:, :], op=ALU.subtract),
                  reads=[("ps", 5), "lnr"], writes=["lnr"])
            S.add(DVE, lambda e: e.tensor_scalar_max(out=lnr[:, :], in0=lnr[:, :], scalar1=0.0),
                  reads=["lnr"], writes=["lnr"])
            S.add(ACT, lambda e: e.activation(out=lnr[:, :], in_=lnr[:, :], func=AF.Sqrt, bias=epsc[:, 0:1]),
                  reads=["lnr", "cst0"], writes=["lnr"])
            S.add(DVE, lambda e: e.reciprocal(out=lnr[:, :], in_=lnr[:, :]), reads=["lnr"], writes=["lnr"])
            for c in range(NCH):
                sl = c % 2
                S.add(DVE, lambda e, c=c, sl=sl: e.tensor_tensor(out=ut[sl][:, :], in0=zc[:, c, :], in1=mean[:, :],
                                                                op=ALU.subtract),
                      reads=[("zc", c), "mean"], writes=[("ut", sl)])
                S.add(DVE, lambda e, sl=sl: e.tensor_tensor(out=ut[sl][:, :], in0=ut[sl][:, :], in1=lnr[:, :],
                                                           op=ALU.mult),
                      reads=[("ut", sl), "lnr"], writes=[("ut", sl)])
                gcol = c0 + NCH * 32 + c
                S.add(ACT, lambda e, c=c, sl=sl, gcol=gcol: e.activation(
                    out=hT[:, c, :], in_=ut[sl][:, :], func=AF.Silu,
                    scale=cols[:, gcol:gcol + 1], bias=cols[:, gcol + NCH:gcol + NCH + 1]),
                    reads=[("ut", sl), "cols"], writes=["hT"])
            for oc in range(NCH):
                pb = oc % 2
                linear_block(ws, WP2[i, oc], NCH, lambda k: hT[:, k, :], lambda k: ["hT"], pb)
                S.add(DVE, lambda e, oc=oc, pb=pb: e.tensor_tensor(
                    out=xt[:, oc, :], in0=xt[:, oc, :], in1=ps[pb][:, :], op=ALU.add),
                    reads=[("ps", pb), "xt"], writes=["xt"])
            store_x(xt, dst, t, xs_sem)
        S.barrier()

    def inproj_phase(l, src):
        i = l // 2
        A.reset(persist_end)
        xt = A.alloc([128, NCH, TT], F32, "xt")
        hT = A.alloc([128, NCH, TT], BF16, "hT")
        rstd = A.alloc([128, TT], F32, "rstd")
        sq = [A.alloc([128, TT], BF16, "sq") for _ in range(2)]
        ob = [A.alloc([128, TT], BF16, "ob") for _ in range(2)]
        of = [A.alloc([128, TT], F32, "of") for _ in range(2)]
        wv = A.alloc([128, NCH, 512], BF16, "wv")
        ws = WStream(f"ip{l}", NCH, 3)
        x_sem = S.new_dma_sem(f"ix{l}")
        ob_sem = [S.new_dma_sem(f"iob{l}_{j}", sw=True) for j in range(2)]
        of_sem = [S.new_dma_sem(f"iof{l}_{j}", sw=True) for j in range(2)]
        wv_sem = S.new_dma_sem(f"iwv{l}")
        n = 0
        for t in range(NT):
            tsl = slice(t * TT, (t + 1) * TT)
            load_x(xt, src, t, x_sem)
            rmsnorm_tile(xt, hT, NORM_MIX_COL0 + l * NCH, 7, rstd, sq)
            for blk in list(range(16)) + list(range(24, 32)):
                pb = n % 2
                sl = n % 2
                n += 1
                linear_block(ws, WIN[i, blk], NCH, lambda k: hT[:, k, :], lambda k: ["hT"], pb)
                if blk < 16:
                    dst = (QT if blk < 8 else KT)[(blk % 8) * 128:(blk % 8 + 1) * 128, tsl]
                    S.add(ACT, lambda e, sl=sl, pb=pb: e.copy(out=ob[sl][:, :], in_=ps[pb][:, :]),
                          reads=[("ps", pb)], writes=[("ob", sl)])
                    S.add(POOL, lambda e, sl=sl, dst=dst: e.dma_start(out=dst, in_=ob[sl][:, :]),
                          reads=[("ob", sl)], dsem=ob_sem[sl])
                else:
                    dst = U32[(blk - 24) * 128:(blk - 23) * 128, tsl]
                    S.add(ACT, lambda e, sl=sl, pb=pb: e.copy(out=of[sl][:, :], in_=ps[pb][:, :]),
                          reads=[("ps", pb)], writes=[("of", sl)])
                    S.add(POOL, lambda e, sl=sl, dst=dst: e.dma_start(out=dst, in_=of[sl][:, :]),
                          reads=[("of", sl)], dsem=of_sem[sl])
            for hf in range(2):
                S.add(SP, lambda e, hf=hf: e.dma_start(out=wv[:, :, :], in_=WV[i, hf]),
                      writes=["wv"], dsem=wv_sem)
                for tb in range(4):
                    pb = 2 + (n % 2)
                    sl = n % 2
                    n += 1
                    for k in range(NCH):
                        S.add(PE, lambda e, k=k, tb=tb, pb=pb: e.matmul(
                            ps[pb][:, :], lhsT=hT[:, k, tb * 128:(tb + 1) * 128], rhs=wv[:, k, :],
                            start=(k == 0), stop=(k == NCH - 1)),
                            reads=["hT", "wv"], writes=[("ps", pb)])
                    tok0 = t * TT + tb * 128
                    dst = VTM[tok0:tok0 + 128, hf * 512:(hf + 1) * 512]
                    S.add(ACT, lambda e, sl=sl, pb=pb: e.copy(out=ob[sl][:, :], in_=ps[pb][:, :]),
                          reads=[("ps", pb)], writes=[("ob", sl)])
                    S.add(POOL, lambda e, sl=sl, dst=dst: e.dma_start(out=dst, in_=ob[sl][:, :]),
                          reads=[("ob", sl)], dsem=ob_sem[sl])
        S.barrier()

    def attention_phase(l):
        i = l // 2
        A.reset(persist_end)
        abc = AB_COL0 + i * AB_NCOL
        qh = [A.alloc([128, T], BF16, "qh") for _ in range(2)]
        kh = [A.alloc([128, T], BF16, "kh") for _ in range(2)]
        vh = [A.alloc([128, NKB, 128], BF16, "vh") for _ in range(2)]
        ld_sem = [[S.new_dma_sem(f"ah{l}_{j}_{q}") for j in range(2)] for q in range(3)]
        biasT = A.alloc([128, NH, 2, 128], F32, "biasT")
        pT = [[A.alloc([128, TT], BF16, "pT") for _ in range(2)] for _ in range(2)]
        tmp = [A.alloc([128, 128], F32, "tmp") for _ in range(4)]
        lamt = A.alloc([128, 64], F32, "lamt")
        lamc = A.alloc([128, 8], F32, "lamc")
        r = [A.alloc([128, TT], F32, "r") for _ in range(2)]
        o = [A.alloc([128, TT], F32, "o") for _ in range(2)]
        sqb = A.alloc([128, TT], BF16, "sqb")
        rstd = A.alloc([128, TT], F32, "rstd")
        ao = [A.alloc([128, TT], BF16, "ao") for _ in range(2)]
        ao_sem = [S.new_dma_sem(f"ao{l}_{j}", sw=True) for j in range(2)]
        b_sem = S.new_dma_sem(f"ab{l}")

        S.add(SP, lambda e: e.dma_start(out=biasT[:, :, :, :], in_=bias_in), writes=["biasT"], dsem=b_sem)
        for h in range(NH):
            for m in range(2):
                S.add(DVE, lambda e, h=h, m=m: e.tensor_scalar(
                    out=biasT[:, h, m, :], in0=biasT[:, h, m, :], scalar1=cols[:, CH_COL0 + h:CH_COL0 + h + 1],
                    scalar2=None, op0=ALU.subtract),
                    reads=["biasT", "cols"], writes=["biasT"])
            S.add(DVE, lambda e, h=h: e.tensor_tensor(out=biasT[:, h, 0, :], in0=biasT[:, h, 0, :], in1=maskT,
                                                     op=ALU.add),
                  reads=["biasT", "consts"], writes=["biasT"])
        for m in range(2):
            a0 = abc + AB_LAM + m * 128
            S.add(DVE, lambda e, a0=a0: e.tensor_tensor(out=lamt[:, :], in0=cols[:, a0:a0 + 64],
                                                       in1=cols[:, a0 + 64:a0 + 128], op=ALU.mult),
                  reads=["cols"], writes=["lamt"])
            S.add(DVE, lambda e, m=m: e.reduce_sum(out=lamc[:, 4 + m:5 + m], in_=lamt[:, :],
                                                  axis=mybir.AxisListType.X),
                  reads=["lamt"], writes=[("lamc", 4 + m)])
            S.add(ACT, lambda e, m=m: e.activation(out=lamc[:, m:m + 1], in_=lamc[:, 4 + m:5 + m], func=AF.Exp),
                  reads=[("lamc", 4 + m)], writes=[("lamc", m)])
        S.add(DVE, lambda e: e.tensor_tensor(out=lamc[:, 2:3], in0=lamc[:, 1:2], in1=lamc[:, 0:1], op=ALU.subtract),
              reads=[("lamc", 0), ("lamc", 1)], writes=[("lamc", 2)])
        S.add(DVE, lambda e: e.tensor_scalar_add(out=lamc[:, 2:3], in0=lamc[:, 2:3], scalar1=-LAM_INIT[l]),
              reads=[("lamc", 2)], writes=[("lamc", 2)])
        S.add(DVE, lambda e: e.tensor_scalar_mul(out=lamc[:, 3:4], in0=cols[:, abc + AB_HN:abc + AB_HN + 1],
                                                 scalar1=1.0 - LAM_INIT[l]),
              reads=["cols"], writes=[("lamc", 3)])

        scale = HD ** -0.5
        loaded = set()

        def emit_loads(h):
            if h in loaded:
                return
            loaded.add(h)
            hs = h % 2
            S.add(SP, lambda e: e.dma_start(out=qh[hs][:, :], in_=QT[h * 128:(h + 1) * 128, :]),
                  writes=[("qh", hs)], dsem=ld_sem[0][hs])
            S.add(SP, lambda e: e.dma_start(out=kh[hs][:, :], in_=KT[h * 128:(h + 1) * 128, :]),
                  writes=[("kh", hs)], dsem=ld_sem[1][hs])
            S.add(SP, lambda e: e.dma_start(
                out=vh[hs][:, :, :], in_=VTM[:, h * 128:(h + 1) * 128].rearrange("(j p) d -> p j d", p=128)),
                writes=[("vh", hs)], dsem=ld_sem[2][hs])

        def emit_qk(n, h, t, j):
            emit_loads(h)
            hs = h % 2
            sl = n % 2
            c_lo = max(j - 4 * t, 0) * 128
            for m in range(2):
                pb = sl * 2 + m
                S.add(PE, lambda e, m=m, pb=pb: e.matmul(
                    ps[pb][:, c_lo:TT], lhsT=kh[hs][m * 64:(m + 1) * 64, j * 128:(j + 1) * 128],
                    rhs=qh[hs][m * 64:(m + 1) * 64, t * TT + c_lo:(t + 1) * TT], start=True, stop=True),
                    reads=[("kh", hs), ("qh", hs)], writes=[("ps", pb)])

        def emit_exp(n, h, t, j):
            sl = n % 2
            jj = j - 4 * t
            c_lo = max(jj, 0) * 128
            near = []
            if jj >= 0:
                near.append((jj * 128, 0))
                if jj < 3:
                    near.append(((jj + 1) * 128, 1))
            elif jj == -1:
                near.append((0, 1))
            far_lo = c_lo + 128 * len(near) if jj >= 0 else (128 if jj == -1 else 0)
            for m in range(2):
                pb = sl * 2 + m
                p = pT[m][sl]
                for ni, (cq, which) in enumerate(near):
                    tk = m * 2 + ni
                    S.add(DVE, lambda e, pb=pb, cq=cq, which=which, tk=tk: e.scalar_tensor_tensor(
                        out=tmp[tk][:, :], in0=ps[pb][:, cq:cq + 128], scalar=scale,
                        in1=biasT[:, h, which, :], op0=ALU.mult, op1=ALU.add),
                        reads=[("ps", pb), "biasT"], writes=[("tmp", tk)])
                    S.add(ACT, lambda e, p=p, cq=cq, tk=tk: e.activation(
                        out=p[:, cq:cq + 128], in_=tmp[tk][:, :], func=AF.Exp),
                        reads=[("tmp", tk)], writes=[("pT", m, sl)])
                if far_lo < TT:
                    S.add(ACT, lambda e, p=p, pb=pb: e.activation(
                        out=p[:, far_lo:TT], in_=ps[pb][:, far_lo:TT], func=AF.Exp, scale=scale),
                        reads=[("ps", pb)], writes=[("pT", m, sl)])
                if c_lo > 0:
                    S.add(POOL, lambda e, p=p: e.memset(p[:, 0:c_lo], 0.0), writes=[("pT", m, sl)])

        def emit_pv(n, h, t, j):
            hs = h % 2
            sl = n % 2
            nj = 4 * t + 4
            for m in range(2):
                p = pT[m][sl]
                S.add(PE, lambda e, m=m, p=p: e.matmul(
                    ps[4 + m][:, :], lhsT=vh[hs][:, j, :], rhs=p[:, :],
                    start=(j == 0), stop=(j == nj - 1)),
                    reads=[("vh", hs), ("pT", m, sl)], writes=[("ps", 4 + m)])
                S.add(PE, lambda e, m=m, p=p: e.matmul(
                    ps[6 + m][:, :], lhsT=ones1[:, :], rhs=p[:, :],
                    start=(j == 0), stop=(j == nj - 1)),
                    reads=["ones1", ("pT", m, sl)], writes=[("ps", 6 + m)])

        nout = [0]

        def emit_epilogue(h, t):
            for m in range(2):
                S.add(DVE, lambda e, m=m: e.reciprocal(out=r[m][:, :], in_=ps[6 + m][:, :]),
                      reads=[("ps", 6 + m)], writes=[("r", m)])
            for m in range(2):
                S.add(DVE, lambda e, m=m: e.tensor_tensor(out=o[m][:, :], in0=ps[4 + m][:, :], in1=r[m][:, :],
                                                         op=ALU.mult),
                      reads=[("ps", 4 + m), ("r", m)], writes=[("o", m)])
            S.add(DVE, lambda e: e.scalar_tensor_tensor(out=o[0][:, :], in0=o[1][:, :], scalar=lamc[:, 2:3],
                                                        in1=o[0][:, :], op0=ALU.mult, op1=ALU.add),
                  reads=[("o", 0), ("o", 1), ("lamc", 2)], writes=[("o", 0)])
            S.add(ACT, lambda e: e.activation(out=sqb[:, :], in_=o[0][:, :], func=AF.Square),
                  reads=[("o", 0)], writes=["sqb"])
            S.add(PE, lambda e: e.matmul(ps[6][:, :], lhsT=ones128[:, :], rhs=sqb[:, :], start=True, stop=True),
                  reads=["ones128", "sqb"], writes=[("ps", 6)])
            rsqrt_ps(rstd, 6)
            asl = nout[0] % 2
            nout[0] += 1
            S.add(DVE, lambda e: e.scalar_tensor_tensor(
                out=ao[asl][:, :], in0=o[0][:, :], scalar=lamc[:, 3:4], in1=rstd[:, :],
                op0=ALU.mult, op1=ALU.mult),
                reads=[("o", 0), "rstd", ("lamc", 3)], writes=[("ao", asl)])
            S.add(POOL, lambda e: e.dma_start(
                out=ATT[h * 128:(h + 1) * 128, t * TT:(t + 1) * TT], in_=ao[asl][:, :]),
                reads=[("ao", asl)], dsem=ao_sem[asl])

        plist = [(h, t, j) for h in range(NH) for t in range(NT) for j in range(4 * t + 4)]
        emit_qk(0, *plist[0])
        for n, (h, t, j) in enumerate(plist):
            if n + 1 < len(plist):
                emit_qk(n + 1, *plist[n + 1])
            emit_exp(n, h, t, j)
            emit_pv(n, h, t, j)
            if j == 4 * t + 3:
                emit_epilogue(h, t)
        S.barrier()

    def s5_phase(l):
        i = l // 2
        A.reset(persist_end)
        abc = AB_COL0 + i * AB_NCOL
        L = TT
        u32 = A.alloc([128, T], F32, "u32")
        ubf = A.alloc([128, T], BF16, "ubf")
        yacc = A.alloc([128, T], F32, "yacc")
        g1 = A.alloc([128, T], F32, "g1")
        g2 = A.alloc([128, T], F32, "g2")
        mstage = A.alloc([128, 16, 128], F32, "mstage")
        mats = A.alloc([128, 16, 128], BF16, "mats")
        tb = {nm: A.alloc([128, L], F32, "tb" + nm) for nm in ["pr", "pi", "nr", "ni", "cs", "sn", "mg", "t0"]}
        pc = A.alloc([128, 16], F32, "pc")
        w = {nm: A.alloc([128, L], F32, "w" + nm) for nm in ["a", "b", "a2", "b2", "zr", "zi", "cr", "ci"]}
        xb = [[A.alloc([128, L], BF16, "xb") for _ in range(2)] for _ in range(2)]
        cr = A.alloc([128, 8], F32, "carry")
        u_sem = S.new_dma_sem(f"su{l}")
        m_sem = S.new_dma_sem(f"sm{l}")
        g_sem = S.new_dma_sem(f"sg{l}", sw=True)
        PI = math.pi
        nx = 0

        ti = A.alloc([128, L], mybir.dt.int32, "ti")
        tf = A.alloc([128, L], F32, "tf")
        tw = A.alloc([128, L], F32, "tw")

        def col(j):
            return pc[:, j:j + 1]

        def tiny(eng, fn, reads, writes):
            S.add(eng, fn, reads=reads, writes=writes)

        def frac_turns(out, r, ti_, tf_, rkey, okey, shift):
            if shift != 0.0:
                S.add(DVE, lambda e: e.tensor_scalar_add(out=tf_, in0=r, scalar1=shift),
                      reads=[rkey], writes=[("trn", "f")])
                src, skey = tf_, ("trn", "f")
            else:
                src, skey = r, rkey
            S.add(DVE, lambda e: e.tensor_copy(out=ti_, in_=src), reads=[skey], writes=[("trn", "i")])
            S.add(DVE, lambda e: e.tensor_copy(out=out, in_=ti_), reads=[("trn", "i")], writes=[okey])
            S.add(DVE, lambda e: e.tensor_tensor(out=out, in0=src, in1=out, op=ALU.subtract),
                  reads=[skey, okey], writes=[okey])

        def sin_turns(out, r, ti_, tf_, tw_, rkey, okey, shift):
            KO = ("trn", "o")
            KF = ("trn", "f")
            frac_turns(tw_, r, ti_, tf_, rkey, KO, shift)
            S.add(DVE, lambda e: e.tensor_single_scalar(out=tf_, in_=tw_, scalar=0.5, op=ALU.is_gt),
                  reads=[KO], writes=[KF])
            S.add(DVE, lambda e: e.tensor_tensor(out=tw_, in0=tw_, in1=tf_, op=ALU.subtract),
                  reads=[KO, KF], writes=[KO])
            S.add(DVE, lambda e: e.tensor_single_scalar(out=tf_, in_=tw_, scalar=-0.5, op=ALU.is_lt),
                  reads=[KO], writes=[KF])
            S.add(DVE, lambda e: e.tensor_tensor(out=tw_, in0=tw_, in1=tf_, op=ALU.add),
                  reads=[KO, KF], writes=[KO])
            S.add(ACT, lambda e: e.activation(out=out, in_=tw_, func=AF.Sin, scale=6.2831845),
                  reads=[KO], writes=[okey])

        for c in range(8):
            S.add(SP, lambda e, c=c: e.dma_start(out=u32[:, :], in_=U32[c * 128:(c + 1) * 128, :]),
                  writes=["u32"], dsem=u_sem)
            S.add(SP, lambda e, c=c: e.dma_start(out=mstage[:, :, :], in_=s5mats_in[i, c]),
                  writes=["mstage"], dsem=m_sem)
            S.add(ACT, lambda e: e.copy(out=ubf[:, :], in_=u32[:, :]), reads=["u32"], writes=["ubf"])
            S.add(DVE, lambda e: e.tensor_copy(out=mats[:, :, :], in_=mstage[:, :, :]),
                  reads=["mstage"], writes=["mats"])
            for pr in range(4):
                gp = c * 4 + pr
                s5c = abc + AB_S5 + 3 * gp
                lre, lim, lst = (cols[:, s5c + j:s5c + j + 1] for j in range(3))
                K = ("pc",)
                tiny(ACT, lambda e, lst=lst: e.activation(out=col(0), in_=lst, func=AF.Exp), ["cols"], [K])
                tiny(DVE, lambda e, lre=lre: e.tensor_tensor(out=col(1), in0=lre, in1=col(0), op=ALU.mult), [K, "cols"], [K])
                tiny(DVE, lambda e, lim=lim: e.tensor_tensor(out=col(2), in0=lim, in1=col(0), op=ALU.mult), [K, "cols"], [K])
                tiny(DVE, lambda e: e.tensor_scalar_mul(out=col(3), in0=col(1), scalar1=-1.0), [K], [K])
                tiny(ACT, lambda e: e.activation(out=col(4), in_=col(1), func=AF.Exp), [K], [K])
                tiny(DVE, lambda e: e.tensor_scalar_mul(out=col(12), in0=col(2), scalar1=1.0 / (2 * PI)), [K], [K])
                frac_turns(col(2), col(12), ti[:, 0:1], tf[:, 0:1], K, K, 0.0)
                sin_turns(col(6), col(2), ti[:, 0:1], tf[:, 0:1], tw[:, 0:1], K, K, 0.0)
                sin_turns(col(5), col(2), ti[:, 0:1], tf[:, 0:1], tw[:, 0:1], K, K, 0.25)
                tiny(DVE, lambda e: e.tensor_tensor(out=col(7), in0=col(4), in1=col(5), op=ALU.mult), [K], [K])
                tiny(DVE, lambda e: e.tensor_tensor(out=col(8), in0=col(4), in1=col(6), op=ALU.mult), [K], [K])
                tiny(DVE, lambda e: e.tensor_scalar_mul(out=col(15), in0=col(8), scalar1=-1.0), [K], [K])
                tiny(DVE, lambda e, lre=lre: e.tensor_tensor(out=col(9), in0=lre, in1=lre, op=ALU.mult), [K, "cols"], [K])
                tiny(DVE, lambda e, lim=lim: e.scalar_tensor_tensor(out=col(9), in0=lim, scalar=lim, in1=col(9),
                                                                    op0=ALU.mult, op1=ALU.add), [K, "cols"], [K])
                tiny(DVE, lambda e: e.reciprocal(out=col(9), in_=col(9)), [K], [K])
                tiny(DVE, lambda e: e.tensor_scalar_add(out=col(12), in0=col(7), scalar1=-1.0), [K], [K])
                tiny(DVE, lambda e, lre=lre: e.tensor_tensor(out=col(10), in0=col(12), in1=lre, op=ALU.mult), [K, "cols"], [K])
                tiny(DVE, lambda e, lim=lim: e.scalar_tensor_tensor(out=col(10), in0=col(8), scalar=lim, in1=col(10),
                                                                    op0=ALU.mult, op1=ALU.add), [K, "cols"], [K])
                tiny(DVE, lambda e: e.tensor_tensor(out=col(10), in0=col(10), in1=col(9), op=ALU.mult), [K], [K])
                tiny(DVE, lambda e, lim=lim: e.tensor_tensor(out=col(13), in0=col(12), in1=lim, op=ALU.mult), [K, "cols"], [K])
                tiny(DVE, lambda e, lre=lre: e.scalar_tensor_tensor(out=col(11), in0=col(8), scalar=lre, in1=col(13),
                                                                    op0=ALU.mult, op1=ALU.subtract), [K, "cols"], [K])
                tiny(DVE, lambda e: e.tensor_tensor(out=col(11), in0=col(11), in1=col(9), op=ALU.mult), [K], [K])
                tiny(DVE, lambda e: e.tensor_scalar_mul(out=col(14), in0=col(11), scalar1=-1.0), [K], [K])
                TK = ("tb",)
                tiny(DVE, lambda e: e.tensor_scalar(out=tb["t0"][:, :], in0=iota, scalar1=col(2), scalar2=None,
                                                    op0=ALU.mult), [K, "consts"], [TK])
                sin_turns(tb["sn"][:, :], tb["t0"][:, :], ti[:, :], tf[:, :], tw[:, :], TK, TK, 0.0)
                sin_turns(tb["cs"][:, :], tb["t0"][:, :], ti[:, :], tf[:, :], tw[:, :], TK, TK, 0.25)
                tiny(ACT, lambda e: e.activation(out=tb["mg"][:, :], in_=iota, func=AF.Exp, scale=col(1)),
                     [K, "consts"], [TK])
                tiny(DVE, lambda e: e.tensor_tensor(out=tb["pr"][:, :], in0=tb["mg"][:, :], in1=tb["cs"][:, :], op=ALU.mult), [TK], [TK])
                tiny(DVE, lambda e: e.tensor_tensor(out=tb["pi"][:, :], in0=tb["mg"][:, :], in1=tb["sn"][:, :], op=ALU.mult), [TK], [TK])
                tiny(ACT, lambda e: e.activation(out=tb["mg"][:, :], in_=iota, func=AF.Exp, scale=col(3)),
                     [K, "consts", TK], [TK])
                tiny(DVE, lambda e: e.tensor_scalar(out=tb["t0"][:, :], in0=tb["cs"][:, :], scalar1=col(10), scalar2=None,
                                                    op0=ALU.mult), [K, TK], [TK])
                tiny(DVE, lambda e: e.scalar_tensor_tensor(out=tb["t0"][:, :], in0=tb["sn"][:, :], scalar=col(11),
                                                           in1=tb["t0"][:, :], op0=ALU.mult, op1=ALU.add), [K, TK], [TK])
                tiny(DVE, lambda e: e.tensor_tensor(out=tb["nr"][:, :], in0=tb["t0"][:, :], in1=tb["mg"][:, :], op=ALU.mult), [TK], [TK])
                tiny(DVE, lambda e: e.tensor_scalar(out=tb["t0"][:, :], in0=tb["cs"][:, :], scalar1=col(11), scalar2=None,
                                                    op0=ALU.mult), [K, TK], [TK])
                tiny(DVE, lambda e: e.scalar_tensor_tensor(out=tb["t0"][:, :], in0=tb["sn"][:, :], scalar=col(10),
                                                           in1=tb["t0"][:, :], op0=ALU.mult, op1=ALU.subtract), [K, TK], [TK])
                tiny(DVE, lambda e: e.scalar_tensor_tensor(out=tb["ni"][:, :], in0=tb["t0"][:, :], scalar=-1.0,
                                                           in1=tb["mg"][:, :], op0=ALU.mult, op1=ALU.mult), [TK], [TK])
                tiny(DVE, lambda e: e.memset(cr[:, :], 0.0), [], ["carry"])
                for t in range(NT):
                    tsl = slice(t * L, (t + 1) * L)
                    S.add(PE, lambda e, pr=pr, tsl=tsl: e.matmul(ps[0][:, :], lhsT=mats[:, pr * 4 + 0, :], rhs=ubf[:, tsl],
                                                                  start=True, stop=True),
                          reads=["mats", "ubf"], writes=[("ps", 0)])
                    S.add(PE, lambda e, pr=pr, tsl=tsl: e.matmul(ps[1][:, :], lhsT=mats[:, pr * 4 + 1, :], rhs=ubf[:, tsl],
                                                                  start=True, stop=True),
                          reads=["mats", "ubf"], writes=[("ps", 1)])
                    R, I = ps[0], ps[1]
                    WK = ("w",)
                    S.add(DVE, lambda e: e.tensor_tensor(out=w["a"][:, :], in0=R[:, :], in1=tb["nr"][:, :], op=ALU.mult),
                          reads=[("ps", 0), TK], writes=[("w", "a")])
                    S.add(DVE, lambda e: e.tensor_tensor(out=w["b"][:, :], in0=I[:, :], in1=tb["ni"][:, :], op=ALU.mult),
                          reads=[("ps", 1), TK], writes=[("w", "b")])
                    S.add(DVE, lambda e: e.tensor_tensor(out=w["a2"][:, :], in0=I[:, :], in1=tb["nr"][:, :], op=ALU.mult),
                          reads=[("ps", 1), TK], writes=[("w", "a2")])
                    S.add(DVE, lambda e: e.tensor_tensor(out=w["b2"][:, :], in0=R[:, :], in1=tb["ni"][:, :], op=ALU.mult),
                          reads=[("ps", 0), TK], writes=[("w", "b2")])
                    S.add(DVE, lambda e: e.tensor_tensor(out=w["zr"][:, :], in0=w["a"][:, :], in1=w["b"][:, :], op=ALU.subtract),
                          reads=[("w", "a"), ("w", "b")], writes=[("w", "zr")])
                    S.add(DVE, lambda e: e.tensor_tensor(out=w["zi"][:, :], in0=w["a2"][:, :], in1=w["b2"][:, :], op=ALU.add),
                          reads=[("w", "a2"), ("w", "b2")], writes=[("w", "zi")])
                    CK = "carry"
                    tiny(DVE, lambda e: e.tensor_tensor(out=cr[:, 4:5], in0=cr[:, 1:2], in1=col(15), op=ALU.mult), [CK, K], [("carry", 4)])
                    tiny(DVE, lambda e: e.tensor_tensor(out=cr[:, 5:6], in0=cr[:, 1:2], in1=col(7), op=ALU.mult), [CK, K], [("carry", 5)])
                    tiny(DVE, lambda e: e.scalar_tensor_tensor(out=cr[:, 2:3], in0=cr[:, 0:1], scalar=col(7), in1=cr[:, 4:5],
                                                               op0=ALU.mult, op1=ALU.add), [CK, K, ("carry", 4)], [("carry", 2)])
                    tiny(DVE, lambda e: e.scalar_tensor_tensor(out=cr[:, 3:4], in0=cr[:, 0:1], scalar=col(8), in1=cr[:, 5:6],
                                                               op0=ALU.mult, op1=ALU.add), [CK, K, ("carry", 5)], [("carry", 3)])
                    S.add(DVE, lambda e: e.tensor_tensor_scan(out=w["cr"][:, :], data0=iota_ones(), data1=w["zr"][:, :],
                                                              initial=cr[:, 2:3], op0=ALU.mult, op1=ALU.add),
                          reads=[("w", "zr"), ("carry", 2), "onesL"], writes=[("w", "cr")])
                    S.add(DVE, lambda e: e.tensor_tensor_scan(out=w["ci"][:, :], data0=iota_ones(), data1=w["zi"][:, :],
                                                              initial=cr[:, 3:4], op0=ALU.mult, op1=ALU.add),
                          reads=[("w", "zi"), ("carry", 3), "onesL"], writes=[("w", "ci")])
                    sl = nx % 2
                    nx += 1
                    S.add(DVE, lambda e: e.tensor_tensor(out=w["a"][:, :], in0=w["cr"][:, :], in1=tb["pr"][:, :], op=ALU.mult),
                          reads=[("w", "cr"), TK], writes=[("w", "a")])
                    S.add(DVE, lambda e: e.tensor_tensor(out=w["b2"][:, :], in0=w["cr"][:, :], in1=tb["pi"][:, :], op=ALU.mult),
                          reads=[("w", "cr"), TK], writes=[("w", "b2")])
                    S.add(DVE, lambda e: e.tensor_tensor(out=w["b"][:, :], in0=w["ci"][:, :], in1=tb["pi"][:, :], op=ALU.mult),
                          reads=[("w", "ci"), TK], writes=[("w", "b")])
                    S.add(DVE, lambda e: e.tensor_tensor(out=w["a2"][:, :], in0=w["ci"][:, :], in1=tb["pr"][:, :], op=ALU.mult),
                          reads=[("w", "ci"), TK], writes=[("w", "a2")])
                    S.add(DVE, lambda e, sl=sl: e.tensor_tensor(out=xb[0][sl][:, :], in0=w["a"][:, :], in1=w["b"][:, :],
                                                               op=ALU.subtract),
                          reads=[("w", "a"), ("w", "b")], writes=[("xb", 0, sl)])
                    S.add(DVE, lambda e, sl=sl: e.scalar_tensor_tensor(out=xb[1][sl][:, :], in0=w["a2"][:, :], scalar=-1.0,
                                                                      in1=w["b2"][:, :], op0=ALU.mult, op1=ALU.subtract),
                          reads=[("w", "a2"), ("w", "b2")], writes=[("xb", 1, sl)])
                    tiny(DVE, lambda e: e.tensor_tensor(out=cr[:, 0:1], in0=w["a"][:, L - 1:L], in1=w["b"][:, L - 1:L],
                                                        op=ALU.subtract),
                         [("w", "a"), ("w", "b"), ("carry", 2), ("carry", 3), ("carry", 4), ("carry", 5)], [CK])
                    tiny(DVE, lambda e: e.tensor_tensor(out=cr[:, 1:2], in0=w["a2"][:, L - 1:L], in1=w["b2"][:, L - 1:L],
                                                        op=ALU.add), [("w", "a2"), ("w", "b2")], [CK])
                    pby = 2 + (nx % 2)
                    S.add(PE, lambda e, pr=pr, sl=sl, pby=pby: e.matmul(ps[pby][:, :], lhsT=mats[:, pr * 4 + 2, :],
                                                                        rhs=xb[0][sl][:, :], start=True, stop=False),
                          reads=["mats", ("xb", 0, sl)], writes=[("ps", pby)])
                    S.add(PE, lambda e, pr=pr, sl=sl, pby=pby: e.matmul(ps[pby][:, :], lhsT=mats[:, pr * 4 + 3, :],
                                                                        rhs=xb[1][sl][:, :], start=False, stop=True),
                          reads=["mats", ("xb", 1, sl)], writes=[("ps", pby)])
                    if pr == 0:
                        S.add(ACT, lambda e, tsl=tsl, pby=pby: e.copy(out=yacc[:, tsl], in_=ps[pby][:, :]),
                              reads=[("ps", pby)], writes=[("yacc", t)])
                    else:
                        S.add(DVE, lambda e, tsl=tsl, pby=pby: e.tensor_tensor(out=yacc[:, tsl], in0=yacc[:, tsl],
                                                                              in1=ps[pby][:, :], op=ALU.add),
                              reads=[("ps", pby), ("yacc", t)], writes=[("yacc", t)])
            YK = [("yacc", t) for t in range(NT)]
            dcol = cols[:, abc + AB_D + c:abc + AB_D + c + 1]
            S.add(DVE, lambda e, dcol=dcol: e.scalar_tensor_tensor(out=yacc[:, :], in0=u32[:, :], scalar=dcol, in1=yacc[:, :],
                                                                   op0=ALU.mult, op1=ALU.add),
                  reads=YK + ["u32", "cols"], writes=YK)
            S.add(DVE, lambda e: e.tensor_tensor(out=g1[:, :], in0=yacc[:, :], in1=yacc[:, :], op=ALU.mult),
                  reads=YK + ["g2dma"], writes=["g1"])
            S.add(DVE, lambda e: e.tensor_scalar(out=g1[:, :], in0=g1[:, :], scalar1=0.044715, scalar2=1.0,
                                                 op0=ALU.mult, op1=ALU.add), reads=["g1"], writes=["g1"])
            S.add(DVE, lambda e: e.tensor_tensor(out=g1[:, :], in0=g1[:, :], in1=yacc[:, :], op=ALU.mult),
                  reads=["g1"] + YK, writes=["g1"])
            S.add(ACT, lambda e: e.activation(out=g1[:, :], in_=g1[:, :], func=AF.Sigmoid, scale=2.0 * 0.7978845608028654),
                  reads=["g1"], writes=["g1"])
            S.add(DVE, lambda e: e.tensor_tensor(out=g2[:, :], in0=g1[:, :], in1=yacc[:, :], op=ALU.mult),
                  reads=["g1"] + YK, writes=["g2"])
            S.add(POOL, lambda e, c=c: e.dma_start(out=G32[c * 128:(c + 1) * 128, :], in_=g2[:, :]),
                  reads=["g2"], writes=["g2dma"], dsem=g_sem)
        S.barrier()

    onesL_t = A.alloc([128, TT], F32, "onesL")
    persist_end = A.off
    S.add(POOL, lambda e: e.memset(onesL_t[:, :], 1.0), writes=["onesL"])

    def iota_ones():
        return onesL_t[:, :]

    def outproj_phase(l, src, dst):
        i = l // 2
        A.reset(persist_end)
        abc = AB_COL0 + i * AB_NCOL
        xt = A.alloc([128, NCH, TT], F32, "xt")
        g32 = A.alloc([128, 8, TT], F32, "g32")
        mixT = A.alloc([128, NCH, TT], BF16, "mixT")
        gbf = A.alloc([128, 8, TT], BF16, "gbf")
        sgm = [A.alloc([128, TT], F32, "sgm") for _ in range(2)]
        ws = WStream(f"op{l}", NCH, 3)
        x_sem = S.new_dma_sem(f"ox{l}")
        xs_sem = S.new_dma_sem(f"oxs{l}", sw=True)
        g_sem = S.new_dma_sem(f"og{l}")
        a_sem = S.new_dma_sem(f"oa{l}")
        for t in range(NT):
            tsl = slice(t * TT, (t + 1) * TT)
            load_x(xt, src, t, x_sem)
            S.add(SP, lambda e, tsl=tsl: e.dma_start(out=g32[:, :, :], in_=cview(G32)[:, :, tsl]),
                  writes=["g32"], dsem=g_sem)
            S.add(SP, lambda e, tsl=tsl: e.dma_start(out=mixT[:, 0:8, :], in_=cview(ATT)[:, :, tsl]),
                  writes=["mixA"], dsem=a_sem)
            S.add(ACT, lambda e: e.copy(out=gbf[:, :, :], in_=g32[:, :, :]), reads=["g32"], writes=["gbf"])
            for oc in range(8):
                pb = oc % 2
                linear_block(ws, WGLU[i, oc], 8, lambda k: gbf[:, k, :], lambda k: ["gbf"], pb)
                S.add(ACT, lambda e, oc=oc, pb=pb: e.activation(
                    out=sgm[pb][:, :], in_=ps[pb][:, :], func=AF.Sigmoid,
                    bias=cols[:, abc + AB_BGLU + oc:abc + AB_BGLU + oc + 1]),
                    reads=[("ps", pb), "cols"], writes=[("sgm", pb)])
                S.add(DVE, lambda e, oc=oc, pb=pb: e.tensor_tensor(out=mixT[:, 8 + oc, :], in0=sgm[pb][:, :],
                                                                  in1=g32[:, oc, :], op=ALU.mult),
                      reads=[("sgm", pb), "g32"], writes=[("mixS", oc)])
            for oc in range(NCH):
                pb = 2 + oc % 2
                linear_block(ws, WOUT[i, oc], NCH, lambda k: mixT[:, k, :],
                             lambda k: ["mixA"] if k < 8 else [("mixS", k - 8)], pb)
                S.add(DVE, lambda e, oc=oc, pb=pb: e.tensor_tensor(
                    out=xt[:, oc, :], in0=xt[:, oc, :], in1=ps[pb][:, :], op=ALU.add),
                    reads=[("ps", pb), "xt"], writes=["xt"])
            store_x(xt, dst, t, xs_sem)
        S.barrier()

    prepass()
    cur = xT_in
    for l in range(n_layers):
        if l % 2 == 0:
            inproj_phase(l, cur)
            attention_phase(l)
            s5_phase(l)
            outproj_phase(l, cur, XT)
        else:
            conformer_phase(l, cur, XT)
        cur = XT
        last = (l == n_layers - 1)
        ffn_phase(l, XT, outT if last else XT, final_norm=last)
    S.emit(final_waits=[st_out])
    return nc


_NC_CACHE = {}


def make_in_map(inputs, b, T=SEQ):
    x = np.asarray(inputs["x"], np.float32)
    m = {
        "xT": np.ascontiguousarray(x[b].T),
        "cols": pack_cols(inputs),
        "biasT": pack_bias(inputs),
        "consts": pack_consts(),
        "s5mats": pack_s5mats(inputs),
    }
    for nm in ["ffn_w_up", "ffn_w_down", "ab_w_in", "ab_w_out", "s5_w_glu", "conv_w_pw1", "conv_w_pw2"]:
        m[nm] = np.ascontiguousarray(np.asarray(inputs[nm], np.float32))
    return m


def kernel(**inputs):
    x = np.asarray(inputs["x"], np.float32)
    B = x.shape[0]
    if "full" not in _NC_CACHE:
        _NC_CACHE["full"] = build_program()
    nc = _NC_CACHE["full"]
    base = make_in_map(inputs, 0)
    in_maps = []
    for c in range(8):
        m = dict(base)
        m["xT"] = np.ascontiguousarray(x[c % B].T)
        in_maps.append(m)
    res = run_bass_kernel_spmd(nc, in_maps, core_ids=list(range(8)))
    out = np.stack([np.ascontiguousarray(res.results[b]["outT"].T) for b in range(B)], axis=0)
    return out.astype(np.float32)
```

```python
import math
import numpy as np
import concourse.bass as bass
import concourse.mybir as mybir
from concourse.bass_utils import run_bass_kernel_spmd

F32 = mybir.dt.float32
BF16 = mybir.dt.bfloat16
ALU = mybir.AluOpType
AF = mybir.ActivationFunctionType

PE, ACT, DVE, POOL, SP = "tensor", "scalar", "vector", "gpsimd", "sync"
ENGS = (PE, ACT, DVE, POOL, SP)

D = 2048
SEQ = 4096
DEPTH = 4
NCH = D // 128
TT = 512
FH = 5632
NFH = FH // 128
EPS = 1e-6
N_BUCKETS = 32
MAX_DISTANCE = 128
NH = 8
HD = 64
SBUF_LO = 24576
SBUF_HI = 229344


class DmaSem:
    def __init__(self, sem):
        self.sem = sem
        self.count = 0
        self.open_ops = []


class Op:
    __slots__ = ("eng", "fn", "deps", "need_inc", "dsem", "incval", "wait_val", "idx")

    def __init__(self, eng, fn, dsem):
        self.eng = eng
        self.fn = fn
        self.deps = []
        self.need_inc = False
        self.dsem = dsem
        self.incval = None
        self.wait_val = None


class Sched:
    def __init__(self, nc):
        self.nc = nc
        self.ops = {e: [] for e in ENGS}
        self.last_w = {}
        self.readers = {}
        self.barrier_deps = {e: [] for e in ENGS}
        self.dma_sems = []
        self.free_sems = []
        self.phase_sems = []
        self.n_ops = 0

    def new_dma_sem(self, name, persistent=False, sw=False):
        pool = [d for d in self.free_sems if d.sw == sw]
        if not persistent and pool:
            ds = pool[-1]
            self.free_sems.remove(ds)
        else:
            ds = DmaSem(self.nc.alloc_semaphore(name))
            self.dma_sems.append(ds)
        ds.sw = sw
        ds.persistent = persistent
        if not persistent:
            self.phase_sems.append(ds)
        return ds

    def add(self, eng, fn, reads=(), writes=(), dsem=None):
        op = Op(eng, fn, dsem)
        deps = []
        for k in reads:
            w = self.last_w.get(k)
            if w is not None:
                deps.append(w)
        for k in writes:
            w = self.last_w.get(k)
            if w is not None:
                deps.append(w)
            deps.extend(self.readers.get(k, ()))
        for k in reads:
            lst = self.readers.setdefault(k, [])
            if dsem is None:
                lst[:] = [r for r in lst if not (r.dsem is None and r.eng == eng)]
            lst.append(op)
        for k in writes:
            self.last_w[k] = op
            self.readers[k] = []
        if self.barrier_deps[eng]:
            deps.extend(self.barrier_deps[eng])
            self.barrier_deps[eng] = []
        seen = set()
        for d in deps:
            if d is op or id(d) in seen:
                continue
            seen.add(id(d))
            if d.eng == PE and eng == PE and d.dsem is None and dsem is None:
                continue
            d.need_inc = True
            op.deps.append(d)
        if dsem is not None:
            dsem.count += 16
            op.incval = dsem.count
            dsem.open_ops.append(op)
            op.need_inc = True
        self.ops[eng].append(op)
        self.n_ops += 1
        return op

    def close_group(self, dsem):
        for o in dsem.open_ops:
            o.wait_val = dsem.count
        dsem.open_ops = []

    def barrier(self):
        lasts = []
        for e in ENGS:
            for o in reversed(self.ops[e]):
                if o.dsem is None:
                    lasts.append(o)
                    break
        for ds in self.dma_sems:
            ds.open_ops = []
            if ds.count:
                po = Op(None, None, ds)
                po.wait_val = ds.count
                lasts.append(po)
        for e in ENGS:
            self.barrier_deps[e] = list(lasts)
        self.last_w = {}
        self.readers = {}
        self.free_sems.extend(self.phase_sems)
        self.phase_sems = []

    def emit(self, final_waits=()):
        nc = self.nc
        esem = {e: nc.alloc_semaphore("prog_" + e) for e in ENGS}
        for e in ENGS:
            c = 0
            for o in self.ops[e]:
                if o.dsem is None and o.need_inc:
                    c += 1
                    o.incval = c
        ops = self.ops

        def stream(e):
            def body(eng):
                waited = {}
                for o in ops[e]:
                    for d in o.deps:
                        if d.dsem is not None:
                            sem = d.dsem.sem
                            val = d.wait_val if d.wait_val is not None else d.incval
                        else:
                            sem = esem[d.eng]
                            val = d.incval
                        key = id(sem)
                        if waited.get(key, 0) >= val:
                            continue
                        waited[key] = val
                        eng.wait_ge(sem, val)
                    ins = o.fn(eng)
                    if o.dsem is not None:
                        ins.then_inc(o.dsem.sem, 16)
                    elif o.need_inc:
                        ins.then_inc(esem[e], 1)
                if e == SP:
                    for ds in final_waits:
                        eng.wait_ge(ds.sem, ds.count)
            return body

        with nc.Block() as block:
            block.tensor(stream(PE))
            block.scalar(stream(ACT))
            block.vector(stream(DVE))
            block.gpsimd(stream(POOL))
            block.sync(stream(SP))


class Arena:
    def __init__(self, nc):
        self.nc = nc
        self.off = SBUF_LO
        self.n = 0

    def reset(self, to=SBUF_LO):
        self.off = to

    def alloc(self, shape, dtype, name="t"):
        size = int(np.prod(shape[1:])) * (2 if dtype == BF16 else 4)
        size = (size + 63) // 64 * 64
        assert self.off + size <= SBUF_HI, f"SBUF overflow {name} {self.off} {size}"
        self.n += 1
        t = self.nc.alloc_sbuf_tensor_at(f"{name}{self.n}", list(shape), dtype, offset=self.off)
        self.off += size
        return t


ATT_W = 1024
SSM_W = 1024
CONV_K = 31
NPAIR = 32
NEG = -30000.0
LAM_INIT = [0.8 - 0.6 * math.exp(-0.3 * l) for l in range(DEPTH)]

NORM_MIX_COL0 = 0
NORM_FFN_COL0 = NORM_MIX_COL0 + DEPTH * NCH
NORM_FINAL_COL0 = NORM_FFN_COL0 + DEPTH * NCH
FFN_COL0 = NORM_FINAL_COL0 + NCH
FFN_NCOL = 2 * NFH * 4
CF_COL0 = FFN_COL0 + DEPTH * FFN_NCOL
CF_NCOL = NCH * 32 + 2 * NCH
AB_COL0 = CF_COL0 + 2 * CF_NCOL
AB_BGLU, AB_D, AB_HN, AB_LAM, AB_S5 = 0, 8, 16, 17, 17 + 256
AB_NCOL = 17 + 256 + 3 * NPAIR
CH_COL0 = AB_COL0 + 2 * AB_NCOL
NCOLS = CH_COL0 + NH


def t5_bucket_np(rel):
    n = np.maximum(rel, 0)
    max_exact = N_BUCKETS // 2
    nf = np.maximum(n, 1).astype(np.float32)
    large = max_exact + (np.log(nf / np.float32(max_exact)) / np.float32(math.log(MAX_DISTANCE / max_exact))
                         * np.float32(N_BUCKETS - max_exact)).astype(np.int32)
    large = np.minimum(large, N_BUCKETS - 1)
    return np.where(n < max_exact, n, large)


def pack_cols(inp):
    cols = np.zeros((128, NCOLS), np.float32)

    def put(c0, vec):
        v = np.asarray(vec, np.float32).reshape(-1, 128).T
        cols[:, c0:c0 + v.shape[1]] = v

    for l in range(DEPTH):
        put(NORM_MIX_COL0 + l * NCH, inp["norm_mix"][l])
        put(NORM_FFN_COL0 + l * NCH, inp["norm_ffn"][l])
        w = np.asarray(inp["ffn_w_dw"][l], np.float32)
        b = np.asarray(inp["ffn_b_dw"][l], np.float32)
        blk = np.stack([w[0], w[1], w[2], b], axis=-1)
        blk = blk.reshape(2 * NFH, 128, 4).transpose(1, 0, 2).reshape(128, 2 * NFH * 4)
        cols[:, FFN_COL0 + l * FFN_NCOL: FFN_COL0 + (l + 1) * FFN_NCOL] = blk
    put(NORM_FINAL_COL0, inp["norm_final"])
    for i in range(2):
        c0 = CF_COL0 + i * CF_NCOL
        w = np.asarray(inp["conv_w_dw"][i], np.float32)
        b = np.asarray(inp["conv_b_dw"][i], np.float32)
        blk = np.concatenate([w.T, b[:, None]], axis=1)
        blk = blk.reshape(NCH, 128, 32).transpose(1, 0, 2).reshape(128, NCH * 32)
        cols[:, c0:c0 + NCH * 32] = blk
        put(c0 + NCH * 32, inp["conv_ln_g"][i])
        put(c0 + NCH * 32 + NCH, inp["conv_ln_b"][i])
    for i in range(2):
        c0 = AB_COL0 + i * AB_NCOL
        put(c0 + AB_BGLU, inp["s5_b_glu"][i])
        put(c0 + AB_D, np.asarray(inp["s5_d"][i]).reshape(-1))
        cols[:, c0 + AB_HN] = np.asarray(inp["diff_head_norm"][i], np.float32)
        for q, nm in enumerate(["diff_lq1", "diff_lk1", "diff_lq2", "diff_lk2"]):
            cols[:, c0 + AB_LAM + q * 64: c0 + AB_LAM + (q + 1) * 64] = np.asarray(inp[nm][i], np.float32)[None, :]
        lre = np.asarray(inp["s5_lambda_re"][i], np.float32).reshape(NPAIR, 128)
        lim = np.asarray(inp["s5_lambda_im"][i], np.float32).reshape(NPAIR, 128)
        lst = np.repeat(np.asarray(inp["s5_log_step"][i], np.float32), 64).reshape(NPAIR, 128)
        for gp in range(NPAIR):
            cols[:, c0 + AB_S5 + 3 * gp + 0] = lre[gp]
            cols[:, c0 + AB_S5 + 3 * gp + 1] = lim[gp]
            cols[:, c0 + AB_S5 + 3 * gp + 2] = lst[gp]
    cols[:, CH_COL0:CH_COL0 + NH] = np.asarray(inp["rel_bias"], np.float32)[N_BUCKETS - 1][None, :]
    return cols


def pack_bias(inp):
    rb = np.asarray(inp["rel_bias"], np.float32)
    k = np.arange(128)[:, None]
    q = np.arange(128)[None, :]
    b0 = t5_bucket_np(q - k)
    b1 = t5_bucket_np(128 + q - k)
    out = np.zeros((128, NH, 2, 128), np.float32)
    for h in range(NH):
        out[:, h, 0, :] = rb[b0, h]
        out[:, h, 1, :] = rb[b1, h]
    return out


def pack_consts():
    c = np.zeros((128, 128 + 512), np.float32)
    k = np.arange(128)[:, None]
    q = np.arange(128)[None, :]
    c[:, 0:128] = np.where(q >= k, 0.0, NEG)
    c[:, 128:] = np.arange(512, dtype=np.float32)[None, :]
    return c


def pack_s5mats(inp):
    out = np.zeros((2, 8, 128, 16, 128), np.float32)
    for i in range(2):
        B = [np.asarray(inp["s5_b_re"][i], np.float32), np.asarray(inp["s5_b_im"][i], np.float32)]
        C = [np.asarray(inp["s5_c_re"][i], np.float32), np.asarray(inp["s5_c_im"][i], np.float32)]
        for c in range(8):
            for pr in range(4):
                for g2 in range(2):
                    g = (c * 4 + pr) * 2 + g2
                    ch0 = pr * 32 + g2 * 16
                    for m in range(2):
                        out[i, c, ch0:ch0 + 16, pr * 4 + m, g2 * 64:(g2 + 1) * 64] = B[m][g].T
                        out[i, c, g2 * 64:(g2 + 1) * 64, pr * 4 + 2 + m, ch0:ch0 + 16] = C[m][g].T
    return out


def build_program(T=SEQ, n_layers=DEPTH):
    nc = bass.Bass("TRN2", target_bir_lowering=False)
    S = Sched(nc)
    A = Arena(nc)
    NT = T // TT
    NKB = T // 128
    n_ab = (n_layers + 1) // 2
    n_c = n_layers // 2

    def din(name, shape):
        return nc.dram_tensor(name, list(shape), F32, kind="ExternalInput").ap()

    def dscr(name, shape, dt):
        return nc.dram_tensor(name, list(shape), dt).ap()

    xT_in = din("xT", (D, T))
    outT = nc.dram_tensor("outT", [D, T], F32, kind="ExternalOutput").ap()
    cols_in = din("cols", (128, NCOLS))
    bias_in = din("biasT", (128, NH, 2, 128))
    consts_in = din("consts", (128, 640))
    s5mats_in = din("s5mats", (2, 8, 128, 16, 128))
    ffn_w_up = din("ffn_w_up", (DEPTH, D, 2 * FH))
    ffn_w_down = din("ffn_w_down", (DEPTH, FH, D))
    ab_w_in = din("ab_w_in", (2, D, 4096))
    ab_w_out = din("ab_w_out", (2, D, D))
    s5_w_glu = din("s5_w_glu", (2, SSM_W, SSM_W))
    conv_w_pw1 = din("conv_w_pw1", (2, D, 2 * D))
    conv_w_pw2 = din("conv_w_pw2", (2, D, D))

    XT = dscr("XT", [D, T], F32)
    WUP = dscr("WUP", [DEPTH, NFH, 128, 2, NCH, 128], BF16)
    WDN = dscr("WDN", [DEPTH, NCH, 128, NFH, 128], BF16)
    WIN = dscr("WIN", [2, 32, 128, NCH, 128], BF16)
    WV = dscr("WV", [2, 2, 128, NCH, 512], BF16)
    WOUT = dscr("WOUT", [2, NCH, 128, NCH, 128], BF16)
    WGLU = dscr("WGLU", [2, 8, 128, 8, 128], BF16)
    WP1 = dscr("WP1", [2, 32, 128, NCH, 128], BF16)
    WP2 = dscr("WP2", [2, NCH, 128, NCH, 128], BF16)
    QT = dscr("QT", [ATT_W, T], BF16)
    KT = dscr("KT", [ATT_W, T], BF16)
    VTM = dscr("VTM", [T, ATT_W], BF16)
    U32 = dscr("U32", [SSM_W, T], F32)
    ATT = dscr("ATT", [ATT_W, T], BF16)
    G32 = dscr("G32", [SSM_W, T], F32)

    ps = [nc.alloc_psum_tensor(f"ps{i}", [128, 512], F32) for i in range(8)]

    cols = A.alloc([128, NCOLS], F32, "cols")
    ones_bf = A.alloc([128, 128], BF16, "onesD")
    ones1 = A.alloc([128, 128], BF16, "ones1")
    ones128 = A.alloc([128, 128], BF16, "ones128")
    onesf = A.alloc([128, 128], F32, "onesf")
    cst = A.alloc([128, 8], F32, "cst")
    consts = A.alloc([128, 640], F32, "consts")
    persist_end = A.off

    ld_const = S.new_dma_sem("ld_const", persistent=True)
    S.add(SP, lambda e: e.dma_start(out=cols[:, :], in_=cols_in), writes=["cols"], dsem=ld_const)
    ld_const2 = S.new_dma_sem("ld_const2", persistent=True)
    S.add(SP, lambda e: e.dma_start(out=consts[:, :], in_=consts_in), writes=["consts"], dsem=ld_const2)
    S.add(POOL, lambda e: e.memset(ones_bf[:, :], 1.0 / D), writes=["ones"])
    S.add(POOL, lambda e: e.memset(ones1[:, :], 1.0), writes=["ones1"])
    S.add(POOL, lambda e: e.memset(ones128[:, :], 1.0 / 128), writes=["ones128"])
    S.add(POOL, lambda e: e.memset(onesf[:, :], 1.0 / D), writes=["onesf"])
    S.add(POOL, lambda e: e.memset(cst[:, 0:1], EPS), writes=["cst0"])
    S.add(POOL, lambda e: e.memset(cst[:, 1:2], -math.pi), writes=["cst1"])
    epsc = cst
    maskT = consts[:, 0:128]
    iota = consts[:, 128:640]

    st_out = S.new_dma_sem("st_out", persistent=True, sw=True)

    def cview(ap2d):
        return ap2d.rearrange("(c p) t -> p c t", p=128)

    class Converter:
        def __init__(self, tag, stg, ns, cast_engs):
            self.stage = [A.alloc([128, stg], F32, "stg") for _ in range(ns)]
            self.stb = [A.alloc([128, stg], BF16, "stb") for _ in range(ns)]
            self.lsem = [S.new_dma_sem(f"{tag}_ld{i}") for i in range(ns)]
            self.ssem = [S.new_dma_sem(f"{tag}_st{i}", sw=True) for i in range(ns)]
            self.ns = ns
            self.cnt = 0
            self.cast_engs = cast_engs
            self.tag = tag

        def convert(self, src_ap, dst_ap, kc, nb, permute=True):
            i = self.cnt % self.ns
            ce = self.cast_engs[self.cnt % len(self.cast_engs)]
            self.cnt += 1
            stage, stb = self.stage[i], self.stb[i]
            kin, kout = (self.tag, "stg", i), (self.tag, "stb", i)
            n = kc * nb * 128
            sv = stage[:, 0:n].rearrange("p (k m) -> p k m", k=kc)
            S.add(SP, lambda e: e.dma_start(out=sv, in_=src_ap), writes=[kin], dsem=self.lsem[i])
            if permute:
                cin = stage[:, 0:n].rearrange("p (k n b) -> p n k b", k=kc, n=nb)
                cout = stb[:, 0:n].rearrange("p (n k b) -> p n k b", n=nb, k=kc)
                bv = stb[:, 0:n].rearrange("p (n m) -> p n m", n=nb)
            else:
                cin = stage[:, 0:n]
                cout = stb[:, 0:n]
                bv = stb[:, 0:n]
            if ce == ACT:
                S.add(ACT, lambda e: e.copy(out=cout, in_=cin), reads=[kin], writes=[kout])
            else:
                S.add(ce, lambda e: e.tensor_copy(out=cout, in_=cin), reads=[kin], writes=[kout])
            S.add(POOL, lambda e: e.dma_start(out=dst_ap, in_=bv), reads=[kout], dsem=self.ssem[i])

    def weight_tasks(l, stg):
        tasks = []

        def conv_w(src2d, dst, K, N, blocks=None):
            kc = K // 128
            nb = stg // (kc * 128)
            sv = src2d.rearrange("(k p) n -> p k n", p=128)
            for oc0 in range(0, N // 128, nb):
                if blocks is not None and not blocks(oc0):
                    continue
                tasks.append((sv[:, :, oc0 * 128:(oc0 + nb) * 128],
                              dst[oc0:oc0 + nb].rearrange("n p k b -> p n (k b)"), kc, nb, True))

        i = l // 2
        if l % 2 == 0:
            conv_w(ab_w_in[i], WIN[i], D, 4096, blocks=lambda oc: oc < 16 or oc >= 24)
            conv_w(ab_w_out[i], WOUT[i], D, D)
            conv_w(s5_w_glu[i], WGLU[i], SSM_W, SSM_W)
        else:
            conv_w(conv_w_pw1[i], WP1[i], D, 2 * D)
            conv_w(conv_w_pw2[i], WP2[i], D, D)
        nbu = stg // (NCH * 128)
        wu_src = ffn_w_up[l].rearrange("(k p) n -> p k n", p=128)
        for path in range(2):
            for j0 in range(0, NFH, nbu):
                c0 = path * FH + j0 * 128
                tasks.append((wu_src[:, :, c0:c0 + nbu * 128],
                              WUP[l, j0:j0 + nbu, :, path].rearrange("n p k b -> p n (k b)"), NCH, nbu, True))
        nbd = stg // (22 * 128)
        wd_src = ffn_w_down[l].rearrange("(k p) n -> p k n", p=128)
        for oc0 in range(0, NCH, nbd):
            for k0 in range(0, NFH, 22):
                tasks.append((wd_src[:, k0:k0 + 22, oc0 * 128:(oc0 + nbd) * 128],
                              WDN[l, oc0:oc0 + nbd, :, k0:k0 + 22, :].rearrange("n p k b -> p n (k b)"), 22, nbd, True))
        return tasks

    BG_STG = 4096
    bg_tasks = []

    def prepass():
        A.reset(persist_end)
        cv = Converter("pp", 8192, 3, [DVE, ACT])
        for i in range(n_ab):
            sv = ab_w_in[i].rearrange("(k p) n -> p k n", p=128)
            for hf in range(2):
                cv.convert(sv[:, :, 2048 + hf * 512: 2048 + (hf + 1) * 512],
                           WV[i, hf].rearrange("p k n -> p (k n)"), NCH, 4, permute=False)
        for tk in weight_tasks(0, 8192):
            cv.convert(*tk)
        for l in range(1, n_layers):
            bg_tasks.extend(weight_tasks(l, BG_STG))
        S.barrier()

    def rsqrt_ps(rstd, psb, key="rstd"):
        S.add(ACT, lambda e: e.activation(out=rstd[:, :], in_=ps[psb][:, :], func=AF.Sqrt, bias=epsc[:, 0:1]),
              reads=[("ps", psb), "cst0"], writes=[key])
        S.add(DVE, lambda e: e.reciprocal(out=rstd[:, :], in_=rstd[:, :]), reads=[key], writes=[key])

    def rmsnorm_tile(xt, hT, gcol0, psb, rstd, sq, out_key="hT"):
        for c in range(NCH):
            S.add(ACT, lambda e, c=c: e.activation(out=sq[c % 2][:, :], in_=xt[:, c, :], func=AF.Square),
                  reads=["xt"], writes=[("sq", c % 2)])
            S.add(PE, lambda e, c=c: e.matmul(ps[psb][:, :], lhsT=ones_bf[:, :], rhs=sq[c % 2][:, :],
                                              start=(c == 0), stop=(c == NCH - 1)),
                  reads=[("sq", c % 2), "ones"], writes=[("ps", psb)])
        rsqrt_ps(rstd, psb)
        for c in range(NCH):
            S.add(DVE, lambda e, c=c: e.scalar_tensor_tensor(
                out=hT[:, c, :], in0=xt[:, c, :], scalar=cols[:, gcol0 + c:gcol0 + c + 1], in1=rstd[:, :],
                op0=ALU.mult, op1=ALU.mult),
                reads=["xt", "rstd", "cols"], writes=[out_key])

    class WStream:
        def __init__(self, tag, kc_max, n=3):
            self.slots = [A.alloc([128, kc_max, 128], BF16, "wl") for _ in range(n)]
            self.sems = [S.new_dma_sem(f"wl_{tag}_{i}") for i in range(n)]
            self.n = n
            self.i = 0
            self.tag = tag

        def load(self, src_blk, kc):
            s = self.i % self.n
            self.i += 1
            sl = self.slots[s]
            S.add(SP, lambda e: e.dma_start(out=sl[:, 0:kc, :], in_=src_blk),
                  writes=[("wl", self.tag, s)], dsem=self.sems[s])
            return sl, ("wl", self.tag, s)

    def linear_block(ws, wsrc_blk, kc, rhs_fn, rhs_keys, pb):
        sl, key = ws.load(wsrc_blk, kc)
        for k in range(kc):
            S.add(PE, lambda e, k=k: e.matmul(ps[pb][:, :], lhsT=sl[:, k, :], rhs=rhs_fn(k),
                                              start=(k == 0), stop=(k == kc - 1)),
                  reads=[key] + list(rhs_keys(k)), writes=[("ps", pb)])

    def load_x(xt, src, t, sem):
        tsl = slice(t * TT, (t + 1) * TT)
        S.add(SP, lambda e: e.dma_start(out=xt[:, :, :], in_=cview(src)[:, :, tsl]),
              writes=["xt"], dsem=sem)

    def store_x(xt, dst, t, sem):
        tsl = slice(t * TT, (t + 1) * TT)
        S.add(POOL, lambda e: e.dma_start(out=cview(dst)[:, :, tsl], in_=xt[:, :, :]),
              reads=["xt"], dsem=sem)

    def ffn_phase(l, src, dst, final_norm=False):
        A.reset(persist_end)
        xt = A.alloc([128, NCH, TT], F32, "xt")
        hT = A.alloc([128, NCH, TT], BF16, "hT")
        hid = A.alloc([128, NFH, TT], BF16, "hid")
        NWU, NWD = 3, 2
        wu = [A.alloc([128, 2, NCH, 128], BF16, "wu") for _ in range(NWU)]
        wd = [A.alloc([128, NFH, 128], BF16, "wd") for _ in range(NWD)]
        wu_sem = [S.new_dma_sem(f"wu{l}_{i}") for i in range(NWU)]
        wd_sem = [S.new_dma_sem(f"wd{l}_{i}") for i in range(NWD)]
        x_sem = S.new_dma_sem(f"fx{l}")
        xs_sem = st_out if dst is outT else S.new_dma_sem(f"fxs{l}", sw=True)
        gb = [[A.alloc([128, TT + 2], F32, "gb") for _ in range(2)] for _ in range(2)]
        acc = [[A.alloc([128, TT], F32, "acc") for _ in range(2)] for _ in range(2)]
        sg = [A.alloc([128, TT], F32, "sg") for _ in range(2)]
        halo = A.alloc([128, 2 * NFH, 2], F32, "halo")
        rstd = A.alloc([128, TT], F32, "rstd")
        sq = [A.alloc([128, TT], BF16, "sq") for _ in range(2)]
        cw = FFN_COL0 + l * FFN_NCOL

        S.add(POOL, lambda e: e.memset(halo[:, :, :], 0.0), writes=["halo"])
        nwu = nwd = 0
        it = 0
        for t in range(NT):
            load_x(xt, src, t, x_sem)
            rmsnorm_tile(xt, hT, NORM_FFN_COL0 + l * NCH, 7, rstd, sq)
            for j in range(NFH):
                ws = nwu % NWU
                nwu += 1
                S.add(SP, lambda e, ws=ws, j=j: e.dma_start(out=wu[ws][:, :, :, :], in_=WUP[l, j]),
                      writes=[("wu", ws)], dsem=wu_sem[ws])
                for path in range(2):
                    sl = it % 2
                    pb = (it % 2) * 2 + path
                    ch = path * NFH + j
                    for k in range(NCH):
                        S.add(PE, lambda e, ws=ws, k=k, path=path, pb=pb: e.matmul(
                            ps[pb][:, :], lhsT=wu[ws][:, path, k, :], rhs=hT[:, k, :],
                            start=(k == 0), stop=(k == NCH - 1)),
                            reads=[("wu", ws), "hT"], writes=[("ps", pb)])
                    g = gb[path][sl]
                    a = acc[path][sl]
                    S.add(ACT, lambda e, g=g, pb=pb: e.copy(out=g[:, 2:TT + 2], in_=ps[pb][:, :]),
                          reads=[("ps", pb)], writes=[("gb", path, sl)])
                    S.add(POOL, lambda e, g=g, ch=ch: e.tensor_copy(out=g[:, 0:2], in_=halo[:, ch, :]),
                          reads=["halo"], writes=[("gbh", path, sl)])
                    wc = cw + ch * 4
                    S.add(DVE, lambda e, g=g, a=a, wc=wc: e.tensor_scalar(
                        out=a[:, :], in0=g[:, 2:TT + 2], scalar1=cols[:, wc + 2:wc + 3],
                        scalar2=cols[:, wc + 3:wc + 4], op0=ALU.mult, op1=ALU.add),
                        reads=[("gb", path, sl), "cols"], writes=[("acc", path, sl)])
                    S.add(DVE, lambda e, g=g, a=a, wc=wc: e.scalar_tensor_tensor(
                        out=a[:, :], in0=g[:, 1:TT + 1], scalar=cols[:, wc + 1:wc + 2], in1=a[:, :],
                        op0=ALU.mult, op1=ALU.add),
                        reads=[("gb", path, sl), ("gbh", path, sl), ("acc", path, sl)], writes=[("acc", path, sl)])
                    S.add(DVE, lambda e, g=g, a=a, wc=wc: e.scalar_tensor_tensor(
                        out=a[:, :], in0=g[:, 0:TT], scalar=cols[:, wc:wc + 1], in1=a[:, :],
                        op0=ALU.mult, op1=ALU.add),
                        reads=[("gb", path, sl), ("gbh", path, sl), ("acc", path, sl)], writes=[("acc", path, sl)])
                    S.add(POOL, lambda e, g=g, ch=ch: e.tensor_copy(out=halo[:, ch, :], in_=g[:, TT:TT + 2]),
                          reads=[("gb", path, sl)], writes=["halo"])
                sl = it % 2
                S.add(ACT, lambda e, sl=sl: e.activation(out=sg[sl][:, :], in_=acc[0][sl][:, :], func=AF.Silu),
                      reads=[("acc", 0, sl)], writes=[("sg", sl)])
                S.add(DVE, lambda e, sl=sl, j=j: e.tensor_tensor(
                    out=hid[:, j, :], in0=sg[sl][:, :], in1=acc[1][sl][:, :], op=ALU.mult),
                    reads=[("sg", sl), ("acc", 1, sl)], writes=[("hid", j)])
                it += 1
            for oc in range(NCH):
                ws = nwd % NWD
                nwd += 1
                pb = 4 + (oc % 2)
                S.add(SP, lambda e, ws=ws, oc=oc: e.dma_start(out=wd[ws][:, :, :], in_=WDN[l, oc]),
                      writes=[("wd", ws)], dsem=wd_sem[ws])
                for k in range(NFH):
                    S.add(PE, lambda e, ws=ws, k=k, pb=pb: e.matmul(
                        ps[pb][:, :], lhsT=wd[ws][:, k, :], rhs=hid[:, k, :],
                        start=(k == 0), stop=(k == NFH - 1)),
                        reads=[("wd", ws), ("hid", k)], writes=[("ps", pb)])
                S.add(DVE, lambda e, oc=oc, pb=pb: e.tensor_tensor(
                    out=xt[:, oc, :], in0=xt[:, oc, :], in1=ps[pb][:, :], op=ALU.add),
                    reads=[("ps", pb), "xt"], writes=["xt"])
            if final_norm:
                for c in range(NCH):
                    S.add(ACT, lambda e, c=c: e.activation(out=sq[c % 2][:, :], in_=xt[:, c, :], func=AF.Square),
                          reads=["xt"], writes=[("sq", c % 2)])
                    S.add(PE, lambda e, c=c: e.matmul(ps[6][:, :], lhsT=ones_bf[:, :], rhs=sq[c % 2][:, :],
                                                      start=(c == 0), stop=(c == NCH - 1)),
                          reads=[("sq", c % 2), "ones"], writes=[("ps", 6)])
                rsqrt_ps(rstd, 6)
                for c in range(NCH):
                    S.add(DVE, lambda e, c=c: e.scalar_tensor_tensor(
                        out=xt[:, c, :], in0=xt[:, c, :],
                        scalar=cols[:, NORM_FINAL_COL0 + c:NORM_FINAL_COL0 + c + 1],
                        in1=rstd[:, :], op0=ALU.mult, op1=ALU.mult),
                        reads=["xt", "rstd", "cols"], writes=["xt"])
            store_x(xt, dst, t, xs_sem)
        S.barrier()

    def conformer_phase(l, src, dst):
        i = l // 2
        A.reset(persist_end)
        xt = A.alloc([128, NCH, TT], F32, "xt")
        hT = A.alloc([128, NCH, TT], BF16, "hT")
        zb = A.alloc([128, NCH, TT + 30], F32, "zb")
        zc = A.alloc([128, NCH, TT], F32, "zc")
        sgm = [A.alloc([128, TT], F32, "sgm") for _ in range(2)]
        sqf = [A.alloc([128, TT], F32, "sqf") for _ in range(2)]
        sq = [A.alloc([128, TT], BF16, "sq") for _ in range(2)]
        rstd = A.alloc([128, TT], F32, "rstd")
        mean = A.alloc([128, TT], F32, "mean")
        lnr = A.alloc([128, TT], F32, "lnr")
        ut = [A.alloc([128, TT], F32, "ut") for _ in range(2)]
        ws = WStream(f"cf{l}", NCH, 3)
        x_sem = S.new_dma_sem(f"cx{l}")
        xs_sem = S.new_dma_sem(f"cxs{l}", sw=True)
        c0 = CF_COL0 + i * CF_NCOL

        S.add(POOL, lambda e: e.memset(zb[:, :, 0:30], 0.0), writes=["zbh"])
        for t in range(NT):
            load_x(xt, src, t, x_sem)
            rmsnorm_tile(xt, hT, NORM_MIX_COL0 + l * NCH, 7, rstd, sq)
            G = 4

            def pw1_group(g0):
                for c in range(g0, g0 + G):
                    pa, pg = (c % 2) * 2, (c % 2) * 2 + 1
                    linear_block(ws, WP1[i, c], NCH, lambda k: hT[:, k, :], lambda k: ["hT"], pa)
                    linear_block(ws, WP1[i, NCH + c], NCH, lambda k: hT[:, k, :], lambda k: ["hT"], pg)
                    sl = c % 2
                    S.add(ACT, lambda e, sl=sl, pg=pg: e.activation(out=sgm[sl][:, :], in_=ps[pg][:, :],
                                                                    func=AF.Sigmoid),
                          reads=[("ps", pg)], writes=[("sgm", sl)])
                    S.add(DVE, lambda e, sl=sl, pa=pa, c=c: e.tensor_tensor(
                        out=zb[:, c, 30:30 + TT], in0=ps[pa][:, :], in1=sgm[sl][:, :], op=ALU.mult),
                        reads=[("ps", pa), ("sgm", sl)], writes=[("zb", c)])

            def conv_group(g0):
                for k in range(CONV_K):
                    for c in range(g0, g0 + G):
                        wc = c0 + c * 32
                        if k == 0:
                            S.add(DVE, lambda e, c=c, wc=wc: e.tensor_scalar(
                                out=zc[:, c, :], in0=zb[:, c, 0:TT], scalar1=cols[:, wc:wc + 1],
                                scalar2=cols[:, wc + 31:wc + 32], op0=ALU.mult, op1=ALU.add),
                                reads=[("zb", c), "zbh", "cols"], writes=[("zc", c)])
                        else:
                            S.add(DVE, lambda e, c=c, wc=wc, k=k: e.scalar_tensor_tensor(
                                out=zc[:, c, :], in0=zb[:, c, k:k + TT], scalar=cols[:, wc + k:wc + k + 1],
                                in1=zc[:, c, :], op0=ALU.mult, op1=ALU.add),
                                reads=[("zb", c), "zbh", ("zc", c)], writes=[("zc", c)])

            def stats_group(g0):
                for c in range(g0, g0 + G):
                    sl = c % 2
                    S.add(ACT, lambda e, c=c, sl=sl: e.activation(out=sqf[sl][:, :], in_=zc[:, c, :], func=AF.Square),
                          reads=[("zc", c)], writes=[("sqf", sl)])
                    S.add(PE, lambda e, c=c: e.matmul(ps[4][:, :], lhsT=onesf[:, :], rhs=zc[:, c, :],
                                                      start=(c == 0), stop=(c == NCH - 1)),
                          reads=[("zc", c), "onesf"], writes=[("ps", 4)])
                    S.add(PE, lambda e, c=c, sl=sl: e.matmul(ps[5][:, :], lhsT=onesf[:, :], rhs=sqf[sl][:, :],
                                                             start=(c == 0), stop=(c == NCH - 1)),
                          reads=[("sqf", sl), "onesf"], writes=[("ps", 5)])

            pw1_group(0)
            for g0 in range(0, NCH, G):
                conv_group(g0)
                if g0 + G < NCH:
                    pw1_group(g0 + G)
                stats_group(g0)
            S.add(POOL, lambda e: e.tensor_copy(out=zb[:, :, 0:30], in_=zb[:, :, TT:TT + 30]),
                  reads=[("zb", c) for c in range(NCH)], writes=["zbh"])
            S.add(ACT, lambda e: e.copy(out=mean[:, :], in_=ps[4][:, :]), reads=[("ps", 4)], writes=["mean"])
            S.add(DVE, lambda e: e.tensor_tensor(out=lnr[:, :], in0=mean[:, :], in1=mean[:, :], op=ALU.mult),
                  reads=["mean"], writes=["lnr"])
            S.add(DVE, lambda e: e.tensor_tensor(out=lnr[:, :], in0=ps[5][:, :], in1=lnr[:, :], op=ALU.subtract),
                  reads=[("ps", 5), "lnr"], writes=["lnr"])
            S.add(DVE, lambda e: e.tensor_scalar_max(out=lnr[:, :], in0=lnr[:, :], scalar1=0.0),
                  reads=["lnr"], writes=["lnr"])
            S.add(ACT, lambda e: e.activation(out=lnr[:, :], in_=lnr[:, :], func=AF.Sqrt, bias=epsc[:, 0:1]),
                  reads=["lnr", "cst0"], writes=["lnr"])
            S.add(DVE, lambda e: e.reciprocal(out=lnr[:, :], in_=lnr[:, :]), reads=["lnr"], writes=["lnr"])
            for c in range(NCH):
                sl = c % 2
                S.add(DVE, lambda e, c=c, sl=sl: e.tensor_tensor(out=ut[sl][:, :], in0=zc[:, c, :], in1=mean[:, :],
                                                                op=ALU.subtract),
                      reads=[("zc", c), "mean"], writes=[("ut", sl)])
                S.add(DVE, lambda e, sl=sl: e.tensor_tensor(out=ut[sl][:, :], in0=ut[sl][:, :], in1=lnr[:, :],
                                                           op=ALU.mult),
                      reads=[("ut", sl), "lnr"], writes=[("ut", sl)])
                gcol = c0 + NCH * 32 + c
                S.add(ACT, lambda e, c=c, sl=sl, gcol=gcol: e.activation(
                    out=hT[:, c, :], in_=ut[sl][:, :], func=AF.Silu,
                    scale=cols[:, gcol:gcol + 1], bias=cols[:, gcol + NCH:gcol + NCH + 1]),
                    reads=[("ut", sl), "cols"], writes=["hT"])
            for oc in range(NCH):
                pb = oc % 2
                linear_block(ws, WP2[i, oc], NCH, lambda k: hT[:, k, :], lambda k: ["hT"], pb)
                S.add(DVE, lambda e, oc=oc, pb=pb: e.tensor_tensor(
                    out=xt[:, oc, :], in0=xt[:, oc, :], in1=ps[pb][:, :], op=ALU.add),
                    reads=[("ps", pb), "xt"], writes=["xt"])
            store_x(xt, dst, t, xs_sem)
        S.barrier()

    def inproj_phase(l, src):
        i = l // 2
        A.reset(persist_end)
        xt = A.alloc([128, NCH, TT], F32, "xt")
        hT = A.alloc([128, NCH, TT], BF16, "hT")
        rstd = A.alloc([128, TT], F32, "rstd")
        sq = [A.alloc([128, TT], BF16, "sq") for _ in range(2)]
        ob = [A.alloc([128, TT], BF16, "ob") for _ in range(2)]
        of = [A.alloc([128, TT], F32, "of") for _ in range(2)]
        wv = A.alloc([128, NCH, 512], BF16, "wv")
        ws = WStream(f"ip{l}", NCH, 3)
        x_sem = S.new_dma_sem(f"ix{l}")
        ob_sem = [S.new_dma_sem(f"iob{l}_{j}", sw=True) for j in range(2)]
        of_sem = [S.new_dma_sem(f"iof{l}_{j}", sw=True) for j in range(2)]
        wv_sem = S.new_dma_sem(f"iwv{l}")
        n = 0
        for t in range(NT):
            tsl = slice(t * TT, (t + 1) * TT)
            load_x(xt, src, t, x_sem)
            rmsnorm_tile(xt, hT, NORM_MIX_COL0 + l * NCH, 7, rstd, sq)
            for blk in list(range(16)) + list(range(24, 32)):
                pb = n % 2
                sl = n % 2
                n += 1
                linear_block(ws, WIN[i, blk], NCH, lambda k: hT[:, k, :], lambda k: ["hT"], pb)
                if blk < 16:
                    dst = (QT if blk < 8 else KT)[(blk % 8) * 128:(blk % 8 + 1) * 128, tsl]
                    S.add(ACT, lambda e, sl=sl, pb=pb: e.copy(out=ob[sl][:, :], in_=ps[pb][:, :]),
                          reads=[("ps", pb)], writes=[("ob", sl)])
                    S.add(POOL, lambda e, sl=sl, dst=dst: e.dma_start(out=dst, in_=ob[sl][:, :]),
                          reads=[("ob", sl)], dsem=ob_sem[sl])
                else:
                    dst = U32[(blk - 24) * 128:(blk - 23) * 128, tsl]
                    S.add(ACT, lambda e, sl=sl, pb=pb: e.copy(out=of[sl][:, :], in_=ps[pb][:, :]),
                          reads=[("ps", pb)], writes=[("of", sl)])
                    S.add(POOL, lambda e, sl=sl, dst=dst: e.dma_start(out=dst, in_=of[sl][:, :]),
                          reads=[("of", sl)], dsem=of_sem[sl])
            for hf in range(2):
                S.add(SP, lambda e, hf=hf: e.dma_start(out=wv[:, :, :], in_=WV[i, hf]),
                      writes=["wv"], dsem=wv_sem)
                for tb in range(4):
                    pb = 2 + (n % 2)
                    sl = n % 2
                    n += 1
                    for k in range(NCH):
                        S.add(PE, lambda e, k=k, tb=tb, pb=pb: e.matmul(
                            ps[pb][:, :], lhsT=hT[:, k, tb * 128:(tb + 1) * 128], rhs=wv[:, k, :],
                            start=(k == 0), stop=(k == NCH - 1)),
                            reads=["hT", "wv"], writes=[("ps", pb)])
                    tok0 = t * TT + tb * 128
                    dst = VTM[tok0:tok0 + 128, hf * 512:(hf + 1) * 512]
                    S.add(ACT, lambda e, sl=sl, pb=pb: e.copy(out=ob[sl][:, :], in_=ps[pb][:, :]),
                          reads=[("ps", pb)], writes=[("ob", sl)])
                    S.add(POOL, lambda e, sl=sl, dst=dst: e.dma_start(out=dst, in_=ob[sl][:, :]),
                          reads=[("ob", sl)], dsem=ob_sem[sl])
        S.barrier()

    def attention_phase(l):
        i = l // 2
        A.reset(persist_end)
        abc = AB_COL0 + i * AB_NCOL
        qh = [A.alloc([128, T], BF16, "qh") for _ in range(2)]
        kh = [A.alloc([128, T], BF16, "kh") for _ in range(2)]
        vh = [A.alloc([128, NKB, 128], BF16, "vh") for _ in range(2)]
        ld_sem = [[S.new_dma_sem(f"ah{l}_{j}_{q}") for j in range(2)] for q in range(3)]
        biasT = A.alloc([128, NH, 2, 128], F32, "biasT")
        pT = [[A.alloc([128, TT], BF16, "pT") for _ in range(2)] for _ in range(2)]
        tmp = [A.alloc([128, 128], F32, "tmp") for _ in range(4)]
        lamt = A.alloc([128, 64], F32, "lamt")
        lamc = A.alloc([128, 8], F32, "lamc")
        r = [A.alloc([128, TT], F32, "r") for _ in range(2)]
        o = [A.alloc([128, TT], F32, "o") for _ in range(2)]
        sqb = A.alloc([128, TT], BF16, "sqb")
        rstd = A.alloc([128, TT], F32, "rstd")
        ao = [A.alloc([128, TT], BF16, "ao") for _ in range(2)]
        ao_sem = [S.new_dma_sem(f"ao{l}_{j}", sw=True) for j in range(2)]
        b_sem = S.new_dma_sem(f"ab{l}")

        S.add(SP, lambda e: e.dma_start(out=biasT[:, :, :, :], in_=bias_in), writes=["biasT"], dsem=b_sem)
        for h in range(NH):
            for m in range(2):
                S.add(DVE, lambda e, h=h, m=m: e.tensor_scalar(
                    out=biasT[:, h, m, :], in0=biasT[:, h, m, :], scalar1=cols[:, CH_COL0 + h:CH_COL0 + h + 1],
                    scalar2=None, op0=ALU.subtract),
                    reads=["biasT", "cols"], writes=["biasT"])
            S.add(DVE, lambda e, h=h: e.tensor_tensor(out=biasT[:, h, 0, :], in0=biasT[:, h, 0, :], in1=maskT,
                                                     op=ALU.add),
                  reads=["biasT", "consts"], writes=["biasT"])
        for m in range(2):
            a0 = abc + AB_LAM + m * 128
            S.add(DVE, lambda e, a0=a0: e.tensor_tensor(out=lamt[:, :], in0=cols[:, a0:a0 + 64],
                                                       in1=cols[:, a0 + 64:a0 + 128], op=ALU.mult),
                  reads=["cols"], writes=["lamt"])
            S.add(DVE, lambda e, m=m: e.reduce_sum(out=lamc[:, 4 + m:5 + m], in_=lamt[:, :],
                                                  axis=mybir.AxisListType.X),
                  reads=["lamt"], writes=[("lamc", 4 + m)])
            S.add(ACT, lambda e, m=m: e.activation(out=lamc[:, m:m + 1], in_=lamc[:, 4 + m:5 + m], func=AF.Exp),
                  reads=[("lamc", 4 + m)], writes=[("lamc", m)])
        S.add(DVE, lambda e: e.tensor_tensor(out=lamc[:, 2:3], in0=lamc[:, 1:2], in1=lamc[:, 0:1], op=ALU.subtract),
              reads=[("lamc", 0), ("lamc", 1)], writes=[("lamc", 2)])
        S.add(DVE, lambda e: e.tensor_scalar_add(out=lamc[:, 2:3], in0=lamc[:, 2:3], scalar1=-LAM_INIT[l]),
              reads=[("lamc", 2)], writes=[("lamc", 2)])
        S.add(DVE, lambda e: e.tensor_scalar_mul(out=lamc[:, 3:4], in0=cols[:, abc + AB_HN:abc + AB_HN + 1],
                                                 scalar1=1.0 - LAM_INIT[l]),
              reads=["cols"], writes=[("lamc", 3)])

        scale = HD ** -0.5
        loaded = set()

        def emit_loads(h):
            if h in loaded:
                return
            loaded.add(h)
            hs = h % 2
            S.add(SP, lambda e: e.dma_start(out=qh[hs][:, :], in_=QT[h * 128:(h + 1) * 128, :]),
                  writes=[("qh", hs)], dsem=ld_sem[0][hs])
            S.add(SP, lambda e: e.dma_start(out=kh[hs][:, :], in_=KT[h * 128:(h + 1) * 128, :]),
                  writes=[("kh", hs)], dsem=ld_sem[1][hs])
            S.add(SP, lambda e: e.dma_start(
                out=vh[hs][:, :, :], in_=VTM[:, h * 128:(h + 1) * 128].rearrange("(j p) d -> p j d", p=128)),
                writes=[("vh", hs)], dsem=ld_sem[2][hs])

        def emit_qk(n, h, t, j):
            emit_loads(h)
            hs = h % 2
            sl = n % 2
            c_lo = max(j - 4 * t, 0) * 128
            for m in range(2):
                pb = sl * 2 + m
                S.add(PE, lambda e, m=m, pb=pb: e.matmul(
                    ps[pb][:, c_lo:TT], lhsT=kh[hs][m * 64:(m + 1) * 64, j * 128:(j + 1) * 128],
                    rhs=qh[hs][m * 64:(m + 1) * 64, t * TT + c_lo:(t + 1) * TT], start=True, stop=True),
                    reads=[("kh", hs), ("qh", hs)], writes=[("ps", pb)])

        def emit_exp(n, h, t, j):
            sl = n % 2
            jj = j - 4 * t
            c_lo = max(jj, 0) * 128
            near = []
            if jj >= 0:
                near.append((jj * 128, 0))
                if jj < 3:
                    near.append(((jj + 1) * 128, 1))
            elif jj == -1:
                near.append((0, 1))
            far_lo = c_lo + 128 * len(near) if jj >= 0 else (128 if jj == -1 else 0)
            for m in range(2):
                pb = sl * 2 + m
                p = pT[m][sl]
                for ni, (cq, which) in enumerate(near):
                    tk = m * 2 + ni
                    S.add(DVE, lambda e, pb=pb, cq=cq, which=which, tk=tk: e.scalar_tensor_tensor(
                        out=tmp[tk][:, :], in0=ps[pb][:, cq:cq + 128], scalar=scale,
                        in1=biasT[:, h, which, :], op0=ALU.mult, op1=ALU.add),
                        reads=[("ps", pb), "biasT"], writes=[("tmp", tk)])
                    S.add(ACT, lambda e, p=p, cq=cq, tk=tk: e.activation(
                        out=p[:, cq:cq + 128], in_=tmp[tk][:, :], func=AF.Exp),
                        reads=[("tmp", tk)], writes=[("pT", m, sl)])
                if far_lo < TT:
                    S.add(ACT, lambda e, p=p, pb=pb: e.activation(
                        out=p[:, far_lo:TT], in_=ps[pb][:, far_lo:TT], func=AF.Exp, scale=scale),
                        reads=[("ps", pb)], writes=[("pT", m, sl)])
                if c_lo > 0:
                    S.add(POOL, lambda e, p=p: e.memset(p[:, 0:c_lo], 0.0), writes=[("pT", m, sl)])

        def emit_pv(n, h, t, j):
            hs = h % 2
            sl = n % 2
            nj = 4 * t + 4
            for m in range(2):
                p = pT[m][sl]
                S.add(PE, lambda e, m=m, p=p: e.matmul(
                    ps[4 + m][:, :], lhsT=vh[hs][:, j, :], rhs=p[:, :],
                    start=(j == 0), stop=(j == nj - 1)),
                    reads=[("vh", hs), ("pT", m, sl)], writes=[("ps", 4 + m)])
                S.add(PE, lambda e, m=m, p=p: e.matmul(
                    ps[6 + m][:, :], lhsT=ones1[:, :], rhs=p[:, :],
                    start=(j == 0), stop=(j == nj - 1)),
                    reads=["ones1", ("pT", m, sl)], writes=[("ps", 6 + m)])

        nout = [0]

        def emit_epilogue(h, t):
            for m in range(2):
                S.add(DVE, lambda e, m=m: e.reciprocal(out=r[m][:, :], in_=ps[6 + m][:, :]),
                      reads=[("ps", 6 + m)], writes=[("r", m)])
            for m in range(2):
                S.add(DVE, lambda e, m=m: e.tensor_tensor(out=o[m][:, :], in0=ps[4 + m][:, :], in1=r[m][:, :],
                                                         op=ALU.mult),
                      reads=[("ps", 4 + m), ("r", m)], writes=[("o", m)])
            S.add(DVE, lambda e: e.scalar_tensor_tensor(out=o[0][:, :], in0=o[1][:, :], scalar=lamc[:, 2:3],
                                                        in1=o[0][:, :], op0=ALU.mult, op1=ALU.add),
                  reads=[("o", 0), ("o", 1), ("lamc", 2)], writes=[("o", 0)])
            S.add(ACT, lambda e: e.activation(out=sqb[:, :], in_=o[0][:, :], func=AF.Square),
                  reads=[("o", 0)], writes=["sqb"])
            S.add(PE, lambda e: e.matmul(ps[6][:, :], lhsT=ones128[:, :], rhs=sqb[:, :], start=True, stop=True),
                  reads=["ones128", "sqb"], writes=[("ps", 6)])
            rsqrt_ps(rstd, 6)
            asl = nout[0] % 2
            nout[0] += 1
            S.add(DVE, lambda e: e.scalar_tensor_tensor(
                out=ao[asl][:, :], in0=o[0][:, :], scalar=lamc[:, 3:4], in1=rstd[:, :],
                op0=ALU.mult, op1=ALU.mult),
                reads=[("o", 0), "rstd", ("lamc", 3)], writes=[("ao", asl)])
            S.add(POOL, lambda e: e.dma_start(
                out=ATT[h * 128:(h + 1) * 128, t * TT:(t + 1) * TT], in_=ao[asl][:, :]),
                reads=[("ao", asl)], dsem=ao_sem[asl])

        plist = [(h, t, j) for h in range(NH) for t in range(NT) for j in range(4 * t + 4)]
        emit_qk(0, *plist[0])
        for n, (h, t, j) in enumerate(plist):
            if n + 1 < len(plist):
                emit_qk(n + 1, *plist[n + 1])
            emit_exp(n, h, t, j)
            emit_pv(n, h, t, j)
            if j == 4 * t + 3:
                emit_epilogue(h, t)
        S.barrier()

    def s5_phase(l):
        i = l // 2
        A.reset(persist_end)
        abc = AB_COL0 + i * AB_NCOL
        L = TT
        u32 = A.alloc([128, T], F32, "u32")
        ubf = A.alloc([128, T], BF16, "ubf")
        yacc = A.alloc([128, T], F32, "yacc")
        g1 = A.alloc([128, T], F32, "g1")
        g2 = g1
        bgc = Converter(f"bg{l}", BG_STG, 2, [ACT]) if bg_tasks else None
        bg_per_step = -(-len(bg_tasks) // (NPAIR * NT)) if bg_tasks else 0

        def bg_step(k):
            for _ in range(k):
                if bg_tasks:
                    bgc.convert(*bg_tasks.pop(0))
        mstage = A.alloc([128, 16, 128], F32, "mstage")
        mats = A.alloc([128, 16, 128], BF16, "mats")
        tb = {nm: A.alloc([128, L], F32, "tb" + nm) for nm in ["pr", "pi", "nr", "ni", "cs", "sn", "mg", "t0"]}
        pc = A.alloc([128, 16], F32, "pc")
        w = {nm: A.alloc([128, L], F32, "w" + nm) for nm in ["a", "b", "a2", "b2", "zr", "zi", "cr", "ci"]}
        xb = [[A.alloc([128, L], BF16, "xb") for _ in range(2)] for _ in range(2)]
        cr = A.alloc([128, 8], F32, "carry")
        u_sem = S.new_dma_sem(f"su{l}")
        m_sem = S.new_dma_sem(f"sm{l}")
        g_sem = S.new_dma_sem(f"sg{l}", sw=True)
        PI = math.pi
        nx = 0

        ti = A.alloc([128, L], mybir.dt.int32, "ti")
        tf = A.alloc([128, L], F32, "tf")
        tw = A.alloc([128, L], F32, "tw")

        def col(j):
            return pc[:, j:j + 1]

        def tiny(eng, fn, reads, writes):
            S.add(eng, fn, reads=reads, writes=writes)

        def frac_turns(out, r, ti_, tf_, rkey, okey, shift):
            if shift != 0.0:
                S.add(DVE, lambda e: e.tensor_scalar_add(out=tf_, in0=r, scalar1=shift),
                      reads=[rkey], writes=[("trn", "f")])
                src, skey = tf_, ("trn", "f")
            else:
                src, skey = r, rkey
            S.add(DVE, lambda e: e.tensor_copy(out=ti_, in_=src), reads=[skey], writes=[("trn", "i")])
            S.add(DVE, lambda e: e.tensor_copy(out=out, in_=ti_), reads=[("trn", "i")], writes=[okey])
            S.add(DVE, lambda e: e.tensor_tensor(out=out, in0=src, in1=out, op=ALU.subtract),
                  reads=[skey, okey], writes=[okey])

        def sin_turns(out, r, ti_, tf_, tw_, rkey, okey, shift):
            KO = ("trn", "o")
            KF = ("trn", "f")
            frac_turns(tw_, r, ti_, tf_, rkey, KO, shift)
            S.add(DVE, lambda e: e.tensor_single_scalar(out=tf_, in_=tw_, scalar=0.5, op=ALU.is_gt),
                  reads=[KO], writes=[KF])
            S.add(DVE, lambda e: e.tensor_tensor(out=tw_, in0=tw_, in1=tf_, op=ALU.subtract),
                  reads=[KO, KF], writes=[KO])
            S.add(DVE, lambda e: e.tensor_single_scalar(out=tf_, in_=tw_, scalar=-0.5, op=ALU.is_lt),
                  reads=[KO], writes=[KF])
            S.add(DVE, lambda e: e.tensor_tensor(out=tw_, in0=tw_, in1=tf_, op=ALU.add),
                  reads=[KO, KF], writes=[KO])
            S.add(ACT, lambda e: e.activation(out=out, in_=tw_, func=AF.Sin, scale=6.2831845),
                  reads=[KO], writes=[okey])

        for c in range(8):
            S.add(SP, lambda e, c=c: e.dma_start(out=u32[:, :], in_=U32[c * 128:(c + 1) * 128, :]),
                  writes=["u32"], dsem=u_sem)
            S.add(SP, lambda e, c=c: e.dma_start(out=mstage[:, :, :], in_=s5mats_in[i, c]),
                  writes=["mstage"], dsem=m_sem)
            S.add(ACT, lambda e: e.copy(out=ubf[:, :], in_=u32[:, :]), reads=["u32"], writes=["ubf"])
            S.add(DVE, lambda e: e.tensor_copy(out=mats[:, :, :], in_=mstage[:, :, :]),
                  reads=["mstage"], writes=["mats"])
            for pr in range(4):
                gp = c * 4 + pr
                s5c = abc + AB_S5 + 3 * gp
                lre, lim, lst = (cols[:, s5c + j:s5c + j + 1] for j in range(3))
                K = ("pc",)
                tiny(ACT, lambda e, lst=lst: e.activation(out=col(0), in_=lst, func=AF.Exp), ["cols"], [K])
                tiny(DVE, lambda e, lre=lre: e.tensor_tensor(out=col(1), in0=lre, in1=col(0), op=ALU.mult), [K, "cols"], [K])
                tiny(DVE, lambda e, lim=lim: e.tensor_tensor(out=col(2), in0=lim, in1=col(0), op=ALU.mult), [K, "cols"], [K])
                tiny(DVE, lambda e: e.tensor_scalar_mul(out=col(3), in0=col(1), scalar1=-1.0), [K], [K])
                tiny(ACT, lambda e: e.activation(out=col(4), in_=col(1), func=AF.Exp), [K], [K])
                tiny(DVE, lambda e: e.tensor_scalar_mul(out=col(12), in0=col(2), scalar1=1.0 / (2 * PI)), [K], [K])
                frac_turns(col(2), col(12), ti[:, 0:1], tf[:, 0:1], K, K, 0.0)
                sin_turns(col(6), col(2), ti[:, 0:1], tf[:, 0:1], tw[:, 0:1], K, K, 0.0)
                sin_turns(col(5), col(2), ti[:, 0:1], tf[:, 0:1], tw[:, 0:1], K, K, 0.25)
                tiny(DVE, lambda e: e.tensor_tensor(out=col(7), in0=col(4), in1=col(5), op=ALU.mult), [K], [K])
                tiny(DVE, lambda e: e.tensor_tensor(out=col(8), in0=col(4), in1=col(6), op=ALU.mult), [K], [K])
                tiny(DVE, lambda e: e.tensor_scalar_mul(out=col(15), in0=col(8), scalar1=-1.0), [K], [K])
                tiny(DVE, lambda e, lre=lre: e.tensor_tensor(out=col(9), in0=lre, in1=lre, op=ALU.mult), [K, "cols"], [K])
                tiny(DVE, lambda e, lim=lim: e.scalar_tensor_tensor(out=col(9), in0=lim, scalar=lim, in1=col(9),
                                                                    op0=ALU.mult, op1=ALU.add), [K, "cols"], [K])
                tiny(DVE, lambda e: e.reciprocal(out=col(9), in_=col(9)), [K], [K])
                tiny(DVE, lambda e: e.tensor_scalar_add(out=col(12), in0=col(7), scalar1=-1.0), [K], [K])
                tiny(DVE, lambda e, lre=lre: e.tensor_tensor(out=col(10), in0=col(12), in1=lre, op=ALU.mult), [K, "cols"], [K])
                tiny(DVE, lambda e, lim=lim: e.scalar_tensor_tensor(out=col(10), in0=col(8), scalar=lim, in1=col(10),
                                                                    op0=ALU.mult, op1=ALU.add), [K, "cols"], [K])
                tiny(DVE, lambda e: e.tensor_tensor(out=col(10), in0=col(10), in1=col(9), op=ALU.mult), [K], [K])
                tiny(DVE, lambda e, lim=lim: e.tensor_tensor(out=col(13), in0=col(12), in1=lim, op=ALU.mult), [K, "cols"], [K])
                tiny(DVE, lambda e, lre=lre: e.scalar_tensor_tensor(out=col(11), in0=col(8), scalar=lre, in1=col(13),
                                                                    op0=ALU.mult, op1=ALU.subtract), [K, "cols"], [K])
                tiny(DVE, lambda e: e.tensor_tensor(out=col(11), in0=col(11), in1=col(9), op=ALU.mult), [K], [K])
                tiny(DVE, lambda e: e.tensor_scalar_mul(out=col(14), in0=col(11), scalar1=-1.0), [K], [K])
                TK = ("tb",)
                tiny(DVE, lambda e: e.tensor_scalar(out=tb["t0"][:, :], in0=iota, scalar1=col(2), scalar2=None,
                                                    op0=ALU.mult), [K, "consts"], [TK])
                sin_turns(tb["sn"][:, :], tb["t0"][:, :], ti[:, :], tf[:, :], tw[:, :], TK, TK, 0.0)
                sin_turns(tb["cs"][:, :], tb["t0"][:, :], ti[:, :], tf[:, :], tw[:, :], TK, TK, 0.25)
                tiny(ACT, lambda e: e.activation(out=tb["mg"][:, :], in_=iota, func=AF.Exp, scale=col(1)),
                     [K, "consts"], [TK])
                tiny(DVE, lambda e: e.tensor_tensor(out=tb["pr"][:, :], in0=tb["mg"][:, :], in1=tb["cs"][:, :], op=ALU.mult), [TK], [TK])
                tiny(DVE, lambda e: e.tensor_tensor(out=tb["pi"][:, :], in0=tb["mg"][:, :], in1=tb["sn"][:, :], op=ALU.mult), [TK], [TK])
                tiny(ACT, lambda e: e.activation(out=tb["mg"][:, :], in_=iota, func=AF.Exp, scale=col(3)),
                     [K, "consts", TK], [TK])
                tiny(DVE, lambda e: e.tensor_scalar(out=tb["t0"][:, :], in0=tb["cs"][:, :], scalar1=col(10), scalar2=None,
                                                    op0=ALU.mult), [K, TK], [TK])
                tiny(DVE, lambda e: e.scalar_tensor_tensor(out=tb["t0"][:, :], in0=tb["sn"][:, :], scalar=col(11),
                                                           in1=tb["t0"][:, :], op0=ALU.mult, op1=ALU.add), [K, TK], [TK])
                tiny(DVE, lambda e: e.tensor_tensor(out=tb["nr"][:, :], in0=tb["t0"][:, :], in1=tb["mg"][:, :], op=ALU.mult), [TK], [TK])
                tiny(DVE, lambda e: e.tensor_scalar(out=tb["t0"][:, :], in0=tb["cs"][:, :], scalar1=col(11), scalar2=None,
                                                    op0=ALU.mult), [K, TK], [TK])
                tiny(DVE, lambda e: e.scalar_tensor_tensor(out=tb["t0"][:, :], in0=tb["sn"][:, :], scalar=col(10),
                                                           in1=tb["t0"][:, :], op0=ALU.mult, op1=ALU.subtract), [K, TK], [TK])
                tiny(DVE, lambda e: e.scalar_tensor_tensor(out=tb["ni"][:, :], in0=tb["t0"][:, :], scalar=-1.0,
                                                           in1=tb["mg"][:, :], op0=ALU.mult, op1=ALU.mult), [TK], [TK])
                tiny(DVE, lambda e: e.memset(cr[:, :], 0.0), [], ["carry"])
                for t in range(NT):
                    tsl = slice(t * L, (t + 1) * L)
                    S.add(PE, lambda e, pr=pr, tsl=tsl: e.matmul(ps[0][:, :], lhsT=mats[:, pr * 4 + 0, :], rhs=ubf[:, tsl],
                                                                  start=True, stop=True),
                          reads=["mats", "ubf"], writes=[("ps", 0)])
                    S.add(PE, lambda e, pr=pr, tsl=tsl: e.matmul(ps[1][:, :], lhsT=mats[:, pr * 4 + 1, :], rhs=ubf[:, tsl],
                                                                  start=True, stop=True),
                          reads=["mats", "ubf"], writes=[("ps", 1)])
                    R, I = ps[0], ps[1]
                    WK = ("w",)
                    S.add(DVE, lambda e: e.tensor_tensor(out=w["a"][:, :], in0=R[:, :], in1=tb["nr"][:, :], op=ALU.mult),
                          reads=[("ps", 0), TK], writes=[("w", "a")])
                    S.add(DVE, lambda e: e.tensor_tensor(out=w["b"][:, :], in0=I[:, :], in1=tb["ni"][:, :], op=ALU.mult),
                          reads=[("ps", 1), TK], writes=[("w", "b")])
                    S.add(DVE, lambda e: e.tensor_tensor(out=w["a2"][:, :], in0=I[:, :], in1=tb["nr"][:, :], op=ALU.mult),
                          reads=[("ps", 1), TK], writes=[("w", "a2")])
                    S.add(DVE, lambda e: e.tensor_tensor(out=w["b2"][:, :], in0=R[:, :], in1=tb["ni"][:, :], op=ALU.mult),
                          reads=[("ps", 0), TK], writes=[("w", "b2")])
                    S.add(DVE, lambda e: e.tensor_tensor(out=w["zr"][:, :], in0=w["a"][:, :], in1=w["b"][:, :], op=ALU.subtract),
                          reads=[("w", "a"), ("w", "b")], writes=[("w", "zr")])
                    S.add(DVE, lambda e: e.tensor_tensor(out=w["zi"][:, :], in0=w["a2"][:, :], in1=w["b2"][:, :], op=ALU.add),
                          reads=[("w", "a2"), ("w", "b2")], writes=[("w", "zi")])
                    CK = "carry"
                    tiny(DVE, lambda e: e.tensor_tensor(out=cr[:, 4:5], in0=cr[:, 1:2], in1=col(15), op=ALU.mult), [CK, K], [("carry", 4)])
                    tiny(DVE, lambda e: e.tensor_tensor(out=cr[:, 5:6], in0=cr[:, 1:2], in1=col(7), op=ALU.mult), [CK, K], [("carry", 5)])
                    tiny(DVE, lambda e: e.scalar_tensor_tensor(out=cr[:, 2:3], in0=cr[:, 0:1], scalar=col(7), in1=cr[:, 4:5],
                                                               op0=ALU.mult, op1=ALU.add), [CK, K, ("carry", 4)], [("carry", 2)])
                    tiny(DVE, lambda e: e.scalar_tensor_tensor(out=cr[:, 3:4], in0=cr[:, 0:1], scalar=col(8), in1=cr[:, 5:6],
                                                               op0=ALU.mult, op1=ALU.add), [CK, K, ("carry", 5)], [("carry", 3)])
                    S.add(DVE, lambda e: e.tensor_tensor_scan(out=w["cr"][:, :], data0=iota_ones(), data1=w["zr"][:, :],
                                                              initial=cr[:, 2:3], op0=ALU.mult, op1=ALU.add),
                          reads=[("w", "zr"), ("carry", 2), "onesL"], writes=[("w", "cr")])
                    S.add(DVE, lambda e: e.tensor_tensor_scan(out=w["ci"][:, :], data0=iota_ones(), data1=w["zi"][:, :],
                                                              initial=cr[:, 3:4], op0=ALU.mult, op1=ALU.add),
                          reads=[("w", "zi"), ("carry", 3), "onesL"], writes=[("w", "ci")])
                    sl = nx % 2
                    nx += 1
                    S.add(DVE, lambda e: e.tensor_tensor(out=w["a"][:, :], in0=w["cr"][:, :], in1=tb["pr"][:, :], op=ALU.mult),
                          reads=[("w", "cr"), TK], writes=[("w", "a")])
                    S.add(DVE, lambda e: e.tensor_tensor(out=w["b2"][:, :], in0=w["cr"][:, :], in1=tb["pi"][:, :], op=ALU.mult),
                          reads=[("w", "cr"), TK], writes=[("w", "b2")])
                    S.add(DVE, lambda e: e.tensor_tensor(out=w["b"][:, :], in0=w["ci"][:, :], in1=tb["pi"][:, :], op=ALU.mult),
                          reads=[("w", "ci"), TK], writes=[("w", "b")])
                    S.add(DVE, lambda e: e.tensor_tensor(out=w["a2"][:, :], in0=w["ci"][:, :], in1=tb["pr"][:, :], op=ALU.mult),
                          reads=[("w", "ci"), TK], writes=[("w", "a2")])
                    S.add(DVE, lambda e, sl=sl: e.tensor_tensor(out=xb[0][sl][:, :], in0=w["a"][:, :], in1=w["b"][:, :],
                                                               op=ALU.subtract),
                          reads=[("w", "a"), ("w", "b")], writes=[("xb", 0, sl)])
                    S.add(DVE, lambda e, sl=sl: e.scalar_tensor_tensor(out=xb[1][sl][:, :], in0=w["a2"][:, :], scalar=-1.0,
                                                                      in1=w["b2"][:, :], op0=ALU.mult, op1=ALU.subtract),
                          reads=[("w", "a2"), ("w", "b2")], writes=[("xb", 1, sl)])
                    tiny(DVE, lambda e: e.tensor_tensor(out=cr[:, 0:1], in0=w["a"][:, L - 1:L], in1=w["b"][:, L - 1:L],
                                                        op=ALU.subtract),
                         [("w", "a"), ("w", "b"), ("carry", 2), ("carry", 3), ("carry", 4), ("carry", 5)], [CK])
                    tiny(DVE, lambda e: e.tensor_tensor(out=cr[:, 1:2], in0=w["a2"][:, L - 1:L], in1=w["b2"][:, L - 1:L],
                                                        op=ALU.add), [("w", "a2"), ("w", "b2")], [CK])
                    pby = 2 + (nx % 2)
                    S.add(PE, lambda e, pr=pr, sl=sl, pby=pby: e.matmul(ps[pby][:, :], lhsT=mats[:, pr * 4 + 2, :],
                                                                        rhs=xb[0][sl][:, :], start=True, stop=False),
                          reads=["mats", ("xb", 0, sl)], writes=[("ps", pby)])
                    S.add(PE, lambda e, pr=pr, sl=sl, pby=pby: e.matmul(ps[pby][:, :], lhsT=mats[:, pr * 4 + 3, :],
                                                                        rhs=xb[1][sl][:, :], start=False, stop=True),
                          reads=["mats", ("xb", 1, sl)], writes=[("ps", pby)])
                    if pr == 0:
                        S.add(ACT, lambda e, tsl=tsl, pby=pby: e.copy(out=yacc[:, tsl], in_=ps[pby][:, :]),
                              reads=[("ps", pby)], writes=[("yacc", t)])
                    else:
                        S.add(DVE, lambda e, tsl=tsl, pby=pby: e.tensor_tensor(out=yacc[:, tsl], in0=yacc[:, tsl],
                                                                              in1=ps[pby][:, :], op=ALU.add),
                              reads=[("ps", pby), ("yacc", t)], writes=[("yacc", t)])
                    bg_step(bg_per_step)
            YK = [("yacc", t) for t in range(NT)]
            dcol = cols[:, abc + AB_D + c:abc + AB_D + c + 1]
            S.add(DVE, lambda e, dcol=dcol: e.scalar_tensor_tensor(out=yacc[:, :], in0=u32[:, :], scalar=dcol, in1=yacc[:, :],
                                                                   op0=ALU.mult, op1=ALU.add),
                  reads=YK + ["u32", "cols"], writes=YK)
            S.add(DVE, lambda e: e.tensor_tensor(out=g1[:, :], in0=yacc[:, :], in1=yacc[:, :], op=ALU.mult),
                  reads=YK, writes=["g1"])
            S.add(DVE, lambda e: e.tensor_scalar(out=g1[:, :], in0=g1[:, :], scalar1=0.044715, scalar2=1.0,
                                                 op0=ALU.mult, op1=ALU.add), reads=["g1"], writes=["g1"])
            S.add(DVE, lambda e: e.tensor_tensor(out=g1[:, :], in0=g1[:, :], in1=yacc[:, :], op=ALU.mult),
                  reads=["g1"] + YK, writes=["g1"])
            S.add(ACT, lambda e: e.activation(out=g1[:, :], in_=g1[:, :], func=AF.Sigmoid, scale=2.0 * 0.7978845608028654),
                  reads=["g1"], writes=["g1"])
            S.add(DVE, lambda e: e.tensor_tensor(out=g1[:, :], in0=g1[:, :], in1=yacc[:, :], op=ALU.mult),
                  reads=["g1"] + YK, writes=["g1"])
            S.add(POOL, lambda e, c=c: e.dma_start(out=G32[c * 128:(c + 1) * 128, :], in_=g1[:, :]),
                  reads=["g1"], dsem=g_sem)
        bg_step(len(bg_tasks))
        S.barrier()

    onesL_t = A.alloc([128, TT], F32, "onesL")
    persist_end = A.off
    S.add(POOL, lambda e: e.memset(onesL_t[:, :], 1.0), writes=["onesL"])

    def iota_ones():
        return onesL_t[:, :]

    def outproj_phase(l, src, dst):
        i = l // 2
        A.reset(persist_end)
        abc = AB_COL0 + i * AB_NCOL
        xt = A.alloc([128, NCH, TT], F32, "xt")
        g32 = A.alloc([128, 8, TT], F32, "g32")
        mixT = A.alloc([128, NCH, TT], BF16, "mixT")
        gbf = A.alloc([128, 8, TT], BF16, "gbf")
        sgm = [A.alloc([128, TT], F32, "sgm") for _ in range(2)]
        ws = WStream(f"op{l}", NCH, 3)
        x_sem = S.new_dma_sem(f"ox{l}")
        xs_sem = S.new_dma_sem(f"oxs{l}", sw=True)
        g_sem = S.new_dma_sem(f"og{l}")
        a_sem = S.new_dma_sem(f"oa{l}")
        for t in range(NT):
            tsl = slice(t * TT, (t + 1) * TT)
            load_x(xt, src, t, x_sem)
            S.add(SP, lambda e, tsl=tsl: e.dma_start(out=g32[:, :, :], in_=cview(G32)[:, :, tsl]),
                  writes=["g32"], dsem=g_sem)
            S.add(SP, lambda e, tsl=tsl: e.dma_start(out=mixT[:, 0:8, :], in_=cview(ATT)[:, :, tsl]),
                  writes=["mixA"], dsem=a_sem)
            S.add(ACT, lambda e: e.copy(out=gbf[:, :, :], in_=g32[:, :, :]), reads=["g32"], writes=["gbf"])
            for oc in range(8):
                pb = oc % 2
                linear_block(ws, WGLU[i, oc], 8, lambda k: gbf[:, k, :], lambda k: ["gbf"], pb)
                S.add(ACT, lambda e, oc=oc, pb=pb: e.activation(
                    out=sgm[pb][:, :], in_=ps[pb][:, :], func=AF.Sigmoid,
                    bias=cols[:, abc + AB_BGLU + oc:abc + AB_BGLU + oc + 1]),
                    reads=[("ps", pb), "cols"], writes=[("sgm", pb)])
                S.add(DVE, lambda e, oc=oc, pb=pb: e.tensor_tensor(out=mixT[:, 8 + oc, :], in0=sgm[pb][:, :],
                                                                  in1=g32[:, oc, :], op=ALU.mult),
                      reads=[("sgm", pb), "g32"], writes=[("mixS", oc)])
            for oc in range(NCH):
                pb = 2 + oc % 2
                linear_block(ws, WOUT[i, oc], NCH, lambda k: mixT[:, k, :],
                             lambda k: ["mixA"] if k < 8 else [("mixS", k - 8)], pb)
                S.add(DVE, lambda e, oc=oc, pb=pb: e.tensor_tensor(
                    out=xt[:, oc, :], in0=xt[:, oc, :], in1=ps[pb][:, :], op=ALU.add),
                    reads=[("ps", pb), "xt"], writes=["xt"])
            store_x(xt, dst, t, xs_sem)
        S.barrier()

    prepass()
    cur = xT_in
    for l in range(n_layers):
        if l % 2 == 0:
            inproj_phase(l, cur)
            attention_phase(l)
            s5_phase(l)
            outproj_phase(l, cur, XT)
        else:
            conformer_phase(l, cur, XT)
        cur = XT
        last = (l == n_layers - 1)
        ffn_phase(l, XT, outT if last else XT, final_norm=last)
    S.emit(final_waits=[st_out])
    return nc


_NC_CACHE = {}


def make_in_map(inputs, b, T=SEQ):
    x = np.asarray(inputs["x"], np.float32)
    m = {
        "xT": np.ascontiguousarray(x[b].T),
        "cols": pack_cols(inputs),
        "biasT": pack_bias(inputs),
        "consts": pack_consts(),
        "s5mats": pack_s5mats(inputs),
    }
    for nm in ["ffn_w_up", "ffn_w_down", "ab_w_in", "ab_w_out", "s5_w_glu", "conv_w_pw1", "conv_w_pw2"]:
        m[nm] = np.ascontiguousarray(np.asarray(inputs[nm], np.float32))
    return m


def kernel(**inputs):
    x = np.asarray(inputs["x"], np.float32)
    B = x.shape[0]
    if "full" not in _NC_CACHE:
        _NC_CACHE["full"] = build_program()
    nc = _NC_CACHE["full"]
    base = make_in_map(inputs, 0)
    in_maps = []
    for c in range(8):
        m = dict(base)
        m["xT"] = np.ascontiguousarray(x[c % B].T)
        in_maps.append(m)
    res = run_bass_kernel_spmd(nc, in_maps, core_ids=list(range(8)))
    out = np.stack([np.ascontiguousarray(res.results[b]["outT"].T) for b in range(B)], axis=0)
    return out.astype(np.float32)
```
